# Optimizing a Trainium2 kernel written in Bass

```python
import jax, jax.numpy as jnp
from jax import lax
import numpy as np

D_MODEL = 1024
BATCH = 4
SEQ = 4096
DEPTH = 4
DEC_BATCH = 8
DEC_SEQ = 64
PAST_LEN = 4096

CHUNK = 64
N_EVEN = (DEPTH + 1) // 2
N_ODD = DEPTH // 2
EPS = 1e-6
NEG_INF = -1e30
CONV_DIM = D_MODEL // 2
CONV_WIDTH = 3
GLA_HEADS = 4
GLA_DV = D_MODEL // 2
GLA_DK = GLA_DV // 2
GLA_HEAD_V = GLA_DV // GLA_HEADS
GLA_HEAD_K = GLA_DK // GLA_HEADS
GATE_RANK = 16
GATE_NORM = 16.0
MIX_AB = CONV_DIM + GLA_DV
IN_SIZES = (CONV_DIM, CONV_DIM, CONV_DIM, GLA_DK, GLA_DK, GLA_DV, GLA_DV, GATE_RANK)
IN_AB = sum(IN_SIZES)
ATT_HEADS = 16
ATT_HEAD_DIM = 64
ATT_DIM = ATT_HEADS * ATT_HEAD_DIM
N_PREV_CHUNKS = 8
BAND_ROWS = N_PREV_CHUNKS * CHUNK
MAX_REL = 256
REL_SIZE = MAX_REL + CHUNK
D_FF = ((8 * D_MODEL // 3 + 255) // 256) * 256

kernel_name = "hybrid_streaming_conv_gla_chunkattn_step"


def rms_norm(x, g):
    xf = x.astype(jnp.float32)
    y = xf * lax.rsqrt(jnp.mean(xf * xf, axis=-1, keepdims=True) + EPS)
    return (y * g.astype(jnp.float32)).astype(x.dtype)


def short_conv_mixer(c_gate, b_gate, h, conv_w, prev):
    T = h.shape[1]
    u = c_gate * h
    up = jnp.concatenate([prev.astype(u.dtype), u], axis=1)
    y = sum(conv_w[i] * up[:, i:i + T] for i in range(CONV_WIDTH))
    return b_gate * y, up[:, T:]


def gla_scan(q, k, v, log_a, s0):
    B, T, H, DK = q.shape
    DV = v.shape[-1]
    L = min(T, CHUNK)
    n = T // L

    def blocks(a):
        return a.reshape(B, n, L, H, a.shape[-1]).swapaxes(0, 1)

    causal = jnp.tril(jnp.ones((L, L), dtype=bool))

    def step(S, blk):
        qc, kc, vc, gc = blk
        b = jnp.cumsum(gc, axis=1)
        b_last = b[:, -1]
        q_e = qc * jnp.exp(b)
        k_e = kc * jnp.exp(-b)
        att = jnp.where(causal, jnp.einsum('bihd,bjhd->bhij', q_e, k_e), 0.0)
        o = jnp.einsum('bhij,bjhv->bihv', att, vc) + jnp.einsum('bihd,bhdv->bihv', q_e, S)
        k_dec = kc * jnp.exp(b_last[:, None] - b)
        S = jnp.exp(b_last)[..., None] * S + jnp.einsum('bjhd,bjhv->bhdv', k_dec, vc)
        return S, o

    S, o = lax.scan(step, s0.astype(jnp.float32), (blocks(q), blocks(k), blocks(v), blocks(log_a)))
    return o.swapaxes(0, 1).reshape(B, T, H, DV), S


def gla_mixer(q, k, v, g, gk_low, gk_w2, gk_b, onorm, s0):
    B, T, _ = q.shape
    f32 = jnp.float32
    shp_k = (B, T, GLA_HEADS, GLA_HEAD_K)
    shp_v = (B, T, GLA_HEADS, GLA_HEAD_V)
    qh = q.astype(f32).reshape(shp_k) * GLA_HEAD_K ** -0.5
    kh = k.astype(f32).reshape(shp_k)
    vh = v.astype(f32).reshape(shp_v)
    log_a = jax.nn.log_sigmoid((gk_low @ gk_w2 + gk_b).astype(f32)).reshape(shp_k) / GATE_NORM
    o, s_new = gla_scan(qh, kh, vh, log_a, s0)
    o = rms_norm(o, onorm) * jax.nn.silu(g.astype(f32)).reshape(shp_v)
    return o.reshape(B, T, GLA_DV).astype(q.dtype), s_new


def band_attention(q, k, v, k_hist, v_hist, pos0, rel_bias):
    B, T, H, Dh = q.shape
    W = k_hist.shape[1]
    L = min(T, CHUNK)
    n = T // L
    k_all = jnp.concatenate([k_hist.astype(k.dtype), k], axis=1)
    v_all = jnp.concatenate([v_hist.astype(v.dtype), v], axis=1)
    scale = Dh ** -0.5

    def one_block(c):
        start = c * L
        qc = lax.dynamic_slice_in_dim(q, start, L, axis=1)
        kc = lax.dynamic_slice_in_dim(k_all, start, W + L, axis=1)
        vc = lax.dynamic_slice_in_dim(v_all, start, W + L, axis=1)
        q_pos = pos0 + start + jnp.arange(L)
        k_pos = pos0 - W + start + jnp.arange(W + L)
        q_chunk = q_pos[:, None] // CHUNK
        k_chunk = k_pos[None, :] // CHUNK
        allowed = (k_pos[None, :] >= 0) & (k_chunk <= q_chunk) & (k_chunk >= q_chunk - N_PREV_CHUNKS)
        rel = jnp.clip(q_pos[:, None] - k_pos[None, :], -(CHUNK - 1), MAX_REL) + (CHUNK - 1)
        bias = rel_bias[:, rel].astype(jnp.float32)
        s = jnp.einsum('bqhd,bkhd->bhqk', qc, kc).astype(jnp.float32) * scale + bias
        p = jax.nn.softmax(jnp.where(allowed, s, NEG_INF), axis=-1).astype(vc.dtype)
        return jnp.einsum('bhqk,bkhd->bqhd', p, vc)

    out = lax.map(one_block, jnp.arange(n))
    return out.swapaxes(0, 1).reshape(B, T, H, Dh)


def swiglu_ffn(h, w_in, w_out):
    gate, up = jnp.split(h @ w_in, 2, axis=-1)
    return (jax.nn.silu(gate) * up) @ w_out


def trunk(x, pos0, keep_rows, conv_prev, gla_prev, k_hist, v_hist,
          norm_mix, norm_ffn, w_in_ab, conv_w, gla_gk_w2, gla_gk_b, gla_onorm, w_out_ab,
          w_qkv, q_norm, k_norm, rel_bias, w_o_att, w_ffn_in, w_ffn_out):
    B, T, _ = x.shape
    split_at = [int(s) for s in np.cumsum(IN_SIZES)[:-1]]
    conv_new, gla_new, k_new, v_new = [], [], [], []
    for layer in range(DEPTH):
        h = rms_norm(x, norm_mix[layer])
        if layer % 2 == 0:
            e = layer // 2
            c_g, b_g, hc, q, k, v, g, gk_low = jnp.split(h @ w_in_ab[e], split_at, axis=-1)
            ya, cs = short_conv_mixer(c_g, b_g, hc, conv_w[e], conv_prev[e])
            yb, ss = gla_mixer(q, k, v, g, gk_low, gla_gk_w2[e], gla_gk_b[e], gla_onorm[e], gla_prev[e])
            x = x + jnp.concatenate([ya, yb], axis=-1) @ w_out_ab[e]
            conv_new.append(cs)
            gla_new.append(ss)
        else:
            o = layer // 2
            qkv = (h @ w_qkv[o]).reshape(B, T, 3, ATT_HEADS, ATT_HEAD_DIM)
            q = rms_norm(qkv[:, :, 0], q_norm[o])
            k = rms_norm(qkv[:, :, 1], k_norm[o])
            v = qkv[:, :, 2]
            att = band_attention(q, k, v, k_hist[o], v_hist[o], pos0, rel_bias[o])
            x = x + att.reshape(B, T, ATT_DIM) @ w_o_att[o]
            k_new.append(jnp.concatenate([k_hist[o].astype(k.dtype), k], axis=1)[:, -keep_rows:])
            v_new.append(jnp.concatenate([v_hist[o].astype(v.dtype), v], axis=1)[:, -keep_rows:])
        x = x + swiglu_ffn(rms_norm(x, norm_ffn[layer]), w_ffn_in[layer], w_ffn_out[layer])
    return x, jnp.stack(conv_new), jnp.stack(gla_new), jnp.stack(k_new), jnp.stack(v_new)


def setup_inputs(seed: int = 0) -> dict:
    key = jax.random.key(seed)
    ks = jax.random.split(key, 24)

    def nrm(k, shape, scale):
        return jax.random.normal(k, shape, jnp.float32) * scale

    win_rows = min(BAND_ROWS, PAST_LEN)
    return {
        "x_prompt": nrm(ks[0], (BATCH, SEQ, D_MODEL), 1.0),
        "x_sample": nrm(ks[1], (DEC_BATCH, DEC_SEQ, D_MODEL), 1.0),
        "state_conv": nrm(ks[2], (N_EVEN, DEC_BATCH, CONV_WIDTH - 1, CONV_DIM), 1.0),
        "state_gla": nrm(ks[3], (N_EVEN, DEC_BATCH, GLA_HEADS, GLA_HEAD_K, GLA_HEAD_V), 0.5),
        "cache_k": nrm(ks[4], (N_ODD, DEC_BATCH, win_rows, ATT_HEADS, ATT_HEAD_DIM), 1.0),
        "cache_v": nrm(ks[5], (N_ODD, DEC_BATCH, win_rows, ATT_HEADS, ATT_HEAD_DIM), 1.0),
        "norm_mix": 1.0 + nrm(ks[6], (DEPTH, D_MODEL), 0.02),
        "norm_ffn": 1.0 + nrm(ks[7], (DEPTH, D_MODEL), 0.02),
        "w_in_ab": nrm(ks[8], (N_EVEN, D_MODEL, IN_AB), D_MODEL ** -0.5),
        "conv_w": nrm(ks[9], (N_EVEN, CONV_WIDTH, CONV_DIM), CONV_WIDTH ** -0.5),
        "gla_gk_w2": nrm(ks[10], (N_EVEN, GATE_RANK, GLA_DK), GATE_RANK ** -0.5),
        "gla_gk_b": nrm(ks[11], (N_EVEN, GLA_DK), 0.1),
        "gla_onorm": 1.0 + nrm(ks[12], (N_EVEN, GLA_HEAD_V), 0.02),
        "w_out_ab": nrm(ks[13], (N_EVEN, MIX_AB, D_MODEL), MIX_AB ** -0.5),
        "w_qkv": nrm(ks[14], (N_ODD, D_MODEL, 3 * ATT_DIM), D_MODEL ** -0.5),
        "q_norm": 1.0 + nrm(ks[15], (N_ODD, ATT_HEAD_DIM), 0.02),
        "k_norm": 1.0 + nrm(ks[16], (N_ODD, ATT_HEAD_DIM), 0.02),
        "rel_bias": nrm(ks[17], (N_ODD, ATT_HEADS, REL_SIZE), 0.1),
        "w_o_att": nrm(ks[18], (N_ODD, ATT_DIM, D_MODEL), ATT_DIM ** -0.5),
        "w_ffn_in": nrm(ks[19], (DEPTH, D_MODEL, 2 * D_FF), D_MODEL ** -0.5),
        "w_ffn_out": nrm(ks[20], (DEPTH, D_FF, D_MODEL), D_FF ** -0.5),
    }


def reference(x_prompt, x_sample, state_conv, state_gla, cache_k, cache_v,
              norm_mix, norm_ffn, w_in_ab, conv_w, gla_gk_w2, gla_gk_b, gla_onorm, w_out_ab,
              w_qkv, q_norm, k_norm, rel_bias, w_o_att, w_ffn_in, w_ffn_out):
    B, T, _ = x_prompt.shape
    dt = x_prompt.dtype
    conv0 = jnp.zeros((N_EVEN, B, CONV_WIDTH - 1, CONV_DIM), dt)
    gla0 = jnp.zeros((N_EVEN, B, GLA_HEADS, GLA_HEAD_K, GLA_HEAD_V), jnp.float32)
    kv0 = jnp.zeros((N_ODD, B, BAND_ROWS, ATT_HEADS, ATT_HEAD_DIM), dt)
    y_prompt, conv_p, gla_p, k_p, v_p = trunk(
        x_prompt, 0, min(BAND_ROWS, T), conv0, gla0, kv0, kv0,
        norm_mix, norm_ffn, w_in_ab, conv_w, gla_gk_w2, gla_gk_b, gla_onorm, w_out_ab,
        w_qkv, q_norm, k_norm, rel_bias, w_o_att, w_ffn_in, w_ffn_out)
    y_sample, conv_s, gla_s, k_s, v_s = trunk(
        x_sample, PAST_LEN, cache_k.shape[2], state_conv, state_gla, cache_k, cache_v,
        norm_mix, norm_ffn, w_in_ab, conv_w, gla_gk_w2, gla_gk_b, gla_onorm, w_out_ab,
        w_qkv, q_norm, k_norm, rel_bias, w_o_att, w_ffn_in, w_ffn_out)
    return (y_prompt, y_sample, conv_p, gla_p, k_p, v_p, conv_s, gla_s, k_s, v_s)
```

```python
import numpy as np
import concourse.bass as bass
import concourse.mybir as mybir
from concourse.bass_utils import run_bass_kernel_spmd

F32 = mybir.dt.float32
BF16 = mybir.dt.bfloat16
I32 = mybir.dt.int32
AF = mybir.ActivationFunctionType
ALU = mybir.AluOpType

import os
DBG_DMA = bool(os.environ.get("DBG_DMA"))
SAME_ENGINE_SYNC = True
RAW_ONLY_SAME_ENGINE = bool(int(os.environ.get("RAW_ONLY", "0")))
EPOCH = 30000
N_DMA_SEMS = 8


class Unit:
    __slots__ = ("name", "w", "rs")

    def __init__(self, name):
        self.name = name
        self.w = None
        self.rs = []


class Rec:
    __slots__ = ("eng", "fn", "deps", "is_dma", "marked", "num", "sem_i", "cnt", "prev_cnt", "desc", "raw")


class V:
    __slots__ = ("ap", "units", "ro")

    def __init__(self, ap, units):
        self.ap = ap
        self.units = list(units)
        self.ro = ()


class Prog:
    ENGS = ("pe", "act", "dve", "pool", "sp")

    def __init__(self, nc):
        self.nc = nc
        self.q = {e: [] for e in self.ENGS}
        self.units = {}
        self.n_dma = 0
        self.n_dma_q = {}
        self.dma_recs = []

    def unit(self, key):
        u = self.units.get(key)
        if u is None:
            u = Unit(key)
            self.units[key] = u
        return u

    def op(self, eng, fn, r=(), w=(), dma=False):
        rec = Rec()
        rec.eng = eng
        rec.fn = fn
        rec.is_dma = dma
        rec.marked = False
        rec.num = 0
        deps = {}
        raw = set()
        for u in r:
            if u.w is not None:
                deps[id(u.w)] = u.w
                raw.add(id(u.w))
        rec.raw = raw
        for u in w:
            if u.w is not None:
                deps[id(u.w)] = u.w
            for x in u.rs:
                deps[id(x)] = x
        deps.pop(id(rec), None)
        rec.deps = list(deps.values())
        for u in r:
            u.rs.append(rec)
        for u in w:
            u.w = rec
            u.rs = []
        if dma:
            kq = self.n_dma_q.get(eng, 0)
            self.n_dma_q[eng] = kq + 1
            self.n_dma += 1
            base = {"sp": 0, "pool": 1, "act": 2}[eng] * N_DMA_SEMS
            rec.sem_i = base + kq % N_DMA_SEMS
            rec.cnt = 16 * (kq // N_DMA_SEMS + 1)
            self.dma_recs.append(rec)
        self.q[eng].append(rec)
        return rec

    def _units(self, views):
        out = []
        for v in views:
            if v is not None:
                out.extend(v.units)
        return out

    def mm(self, out, lhsT, rhs, start=True, stop=True, **kw):
        return self.op("pe", lambda e: e.matmul(out.ap, lhsT.ap, rhs.ap, start=start, stop=stop, **kw),
                       r=self._units([lhsT, rhs]), w=out.units)

    def transpose(self, out, in_, ident):
        return self.op("pe", lambda e: e.transpose(out.ap, in_.ap, ident.ap),
                       r=self._units([in_, ident]), w=out.units)

    def act(self, out, in_, func, bias=None, scale=None, eng="act"):
        kw = {}
        rs = [in_]
        if bias is not None:
            if isinstance(bias, V):
                kw["bias"] = bias.ap
                rs.append(bias)
            else:
                kw["bias"] = bias
        if scale is not None:
            if isinstance(scale, V):
                kw["scale"] = scale.ap
                rs.append(scale)
            else:
                kw["scale"] = scale
        return self.op(eng, lambda e: e.activation(out.ap, in_.ap, func, **kw),
                       r=self._units(rs), w=out.units)

    def tt(self, out, in0, in1, op, eng="dve"):
        return self.op(eng, lambda e: e.tensor_tensor(out.ap, in0.ap, in1.ap, op),
                       r=self._units([in0, in1]), w=out.units)

    def stt(self, out, in0, scalar, in1, op0, op1, eng="dve"):
        rs = [in0, in1]
        sc = scalar
        if isinstance(scalar, V):
            rs.append(scalar)
            sc = scalar.ap
        return self.op(eng, lambda e: e.scalar_tensor_tensor(out.ap, in0.ap, sc, in1.ap, op0, op1),
                       r=self._units(rs), w=out.units)

    def ts(self, out, in0, s1, s2, op0, op1=None, eng="dve"):
        rs = [in0]
        a1, a2 = s1, s2
        if isinstance(s1, V):
            rs.append(s1)
            a1 = s1.ap
        if isinstance(s2, V):
            rs.append(s2)
            a2 = s2.ap
        if op1 is None:
            return self.op(eng, lambda e: e.tensor_scalar(out.ap, in0.ap, a1, None, op0),
                           r=self._units(rs), w=out.units)
        return self.op(eng, lambda e: e.tensor_scalar(out.ap, in0.ap, a1, a2, op0, op1),
                       r=self._units(rs), w=out.units)

    def copy(self, out, in_, eng="dve"):
        if eng == "act":
            return self.op("act", lambda e: e.copy(out.ap, in_.ap), r=in_.units, w=out.units)
        return self.op(eng, lambda e: e.tensor_copy(out.ap, in_.ap), r=in_.units, w=out.units)

    def memset(self, out, val, eng="dve"):
        return self.op(eng, lambda e: e.memset(out.ap, val), r=(), w=out.units)

    def recip(self, out, in_):
        return self.op("dve", lambda e: e.reciprocal(out.ap, in_.ap), r=in_.units, w=out.units)

    def dma(self, out, in_, q="sp", **kw):
        return self.op(q, lambda e: e.dma_start(out=out.ap, in_=in_.ap, **kw),
                       r=in_.units, w=out.units, dma=True)

    def emit(self):
        nc = self.nc
        for e in self.ENGS:
            for rec in self.q[e]:
                keep = []
                for d in rec.deps:
                    if d.is_dma:
                        keep.append(d)
                    elif d.eng != rec.eng:
                        d.marked = True
                        keep.append(d)
                    elif d.eng != "pe" and (rec.is_dma or (SAME_ENGINE_SYNC and (not RAW_ONLY_SAME_ENGINE or id(d) in rec.raw))):
                        d.marked = True
                        keep.append(d)
                rec.deps = keep
        nmark = {}
        for e in self.ENGS:
            n = 0
            for rec in self.q[e]:
                if (not rec.is_dma) and rec.marked:
                    n += 1
                    rec.num = n
            nmark[e] = n
        esems = {e: [nc.alloc_semaphore(f"s_{e}_{k}") for k in range(nmark[e] // EPOCH + 1)]
                 for e in self.ENGS}
        dsems = [nc.alloc_semaphore(f"s_dma_{k}") for k in range(2 * N_DMA_SEMS)]
        stats = {}
        with nc.Block() as block:
            decos = {"pe": block.tensor, "act": block.scalar, "dve": block.vector,
                     "pool": block.gpsimd, "sp": block.sync}
            for e in self.ENGS:
                def body(eo, e=e):
                    seen = {}
                    seen_d = {}
                    nw = 0
                    for rec in self.q[e]:
                        if rec.is_dma and rec.cnt > 16:
                            if seen_d.get(rec.sem_i, 0) < rec.cnt - 16:
                                eo.wait_ge(dsems[rec.sem_i], rec.cnt - 16)
                                seen_d[rec.sem_i] = rec.cnt - 16
                                nw += 1
                        need_e = {}
                        need_d = {}
                        for d in rec.deps:
                            if d.is_dma:
                                if d.cnt > need_d.get(d.sem_i, 0):
                                    need_d[d.sem_i] = d.cnt
                            elif d.num > need_e.get(d.eng, 0):
                                need_e[d.eng] = d.num
                        for si, cnt in need_d.items():
                            if seen_d.get(si, 0) >= cnt:
                                continue
                            eo.wait_ge(dsems[si], cnt)
                            seen_d[si] = cnt
                            nw += 1
                        for de, num in need_e.items():
                            if seen.get(de, 0) >= num:
                                continue
                            ep = (num - 1) // EPOCH
                            eo.wait_ge(esems[de][ep], num - ep * EPOCH)
                            seen[de] = num
                            nw += 1
                        if rec.is_dma and DBG_DMA:
                            print("DMA", nc.get_next_instruction_name(), e, getattr(rec, "desc", None))
                        ins = rec.fn(eo)
                        if rec.is_dma:
                            ins.then_inc(dsems[rec.sem_i], 16)
                        elif rec.marked:
                            ep = (rec.num - 1) // EPOCH
                            ins.then_inc(esems[e][ep], 1)
                    if e == "sp":
                        last = {}
                        for rec in self.dma_recs:
                            last[rec.sem_i] = max(last.get(rec.sem_i, 0), rec.cnt)
                        for i, c in last.items():
                            if seen_d.get(i, 0) < c:
                                eo.wait_ge(dsems[i], c)
                    stats[e] = (len(self.q[e]), nw)
                decos[e](body)
        return stats


class Tile:
    def __init__(self, P, name, shape, dtype, psum=False):
        self.P = P
        self.name = name
        self.shape = shape
        nc = P.nc
        if psum:
            self.t = nc.alloc_psum_tensor(name, shape, dtype)
        else:
            self.t = nc.alloc_sbuf_tensor(name, shape, dtype)

    def v(self, idx=None, keys=("",)):
        ap = self.t[idx] if idx is not None else self.t[:]
        if not isinstance(keys, list):
            keys = (keys,)
        return V(ap, [self.P.unit((self.name, k)) for k in keys])


def dram_v(P, ap, key):
    return V(ap, [P.unit(("dram", key))])

D = 1024
DFF = 2816
NSLAB = DFF // 256
EPS = 1e-6
NEG = -30000.0
JT = 8
SBUF_LO = 16384 + 256
SBUF_HI = 229376 - 128

_CFG = {"DEPTH": 4}


def _consts():
    c = np.zeros((128, 7, 128), np.float32)
    idx = np.arange(128)
    c[:, 0] = np.eye(128)
    c[:, 1] = np.eye(128)[::-1]
    same = (idx[:, None] // 64) == (idx[None, :] // 64)
    c[:, 2] = (same & (idx[:, None] <= idx[None, :])).astype(np.float32)
    c[:, 3] = (same & (idx[:, None] > idx[None, :])).astype(np.float32) * (-1.0 / 16)
    c[:, 4] = 1.0 / 1024
    c[:, 5] = same.astype(np.float32) / 64.0
    c[:, 6] = 1.0 / 128
    return c.reshape(128, 7 * 128)


def build(SEQ, DEPTH):
    NJ = SEQ // (JT * 128)
    NTSEQ = SEQ // 128
    NE = (DEPTH + 1) // 2
    NO = DEPTH // 2
    TOKMAX = (JT + 1) * 128
    nc = bass.Bass("TRN2", target_bir_lowering=False)
    P = Prog(nc)

    def din(name, shape):
        return nc.dram_tensor(name, shape, F32, kind="ExternalInput").ap()

    def dout(name, shape):
        return nc.dram_tensor(name, shape, F32, kind="ExternalOutput").ap()

    xp = din("xp", [SEQ, D]); xs = din("xs", [128, D])
    sconv = din("sconv", [NE, 2, 2, 512]); sgla = din("sgla", [NE, 2, 4, 64, 128])
    ck = din("ck", [max(NO, 1), 2, 512, D]); cv = din("cv", [max(NO, 1), 2, 512, D])
    norm_mix = din("norm_mix", [4, D]); norm_ffn = din("norm_ffn", [4, D])
    w_in_ab = din("w_in_ab", [2, D, 3088]); conv_w = din("conv_w", [2, 3, 512])
    gk_w2 = din("gk_w2", [2, 16, 256]); gk_b = din("gk_b", [2, 256]); onorm = din("onorm", [2, 128])
    w_out_ab = din("w_out_ab", [2, D, D]); w_qkv = din("w_qkv", [2, D, 3 * D])
    q_norm = din("q_norm", [2, 64]); k_norm = din("k_norm", [2, 64]); rel_bias = din("rel_bias", [2, 16, 320])
    w_o_att = din("w_o_att", [2, D, D]); w_ffn_in = din("w_ffn_in", [4, D, 2 * DFF]); w_ffn_out = din("w_ffn_out", [4, DFF, D])
    consts_d = din("consts", [128, 7 * 128])
    yp = dout("yp", [SEQ, D]); ys = dout("ys", [128, D])
    convp = dout("convp", [NE, 2, 512]); glap = dout("glap", [NE, 4, 64, 128])
    kp = dout("kp", [max(NO, 1), 512, D]); vp = dout("vp", [max(NO, 1), 512, D])
    convs = dout("convs", [NE, 2, 2, 512]); glas = dout("glas", [NE, 2, 4, 64, 128])
    ks = dout("ks", [max(NO, 1), 2, 512, D]); vs = dout("vs", [max(NO, 1), 2, 512, D])
    ext_d = nc.dram_tensor("ext_scr", [max(NO, 1), 16, 768], F32)
    kscr = nc.dram_tensor("k_scr", [max(NO, 1), 128, 8, 512], BF16)
    vscr = nc.dram_tensor("v_scr", [max(NO, 1), 128, 4, 1040], BF16)
    hscr = nc.dram_tensor("h_scr", [max(NO, 1), 128, 16 * 5 * 128], BF16)

    def dv(ap, key):
        return V(ap, [P.unit(("dram", key))])

    cur = [SBUF_LO]

    class T2:
        def __init__(self, name, shape, dtype, base=None, ov=None):
            nbytes = int(np.prod(shape[1:])) * (4 if dtype in (F32, I32) else 2)
            nbytes = (nbytes + 63) // 64 * 64
            if base is None:
                off = cur[0]; cur[0] += nbytes
            else:
                off = base[0]; base[0] += nbytes
            assert off + nbytes <= SBUF_HI, (name, off, nbytes)
            self.t = nc.alloc_sbuf_tensor_at(name, shape, dtype, offset=off)
            self.name = name
            self.ov = ov

        def v(self, idx=None, keys=("",)):
            ap = self.t[idx] if idx is not None else self.t[:]
            if not isinstance(keys, list):
                keys = (keys,)
            vv = V(ap, [P.unit((self.name, k)) for k in keys])
            if self.ov is not None:
                vv.ro = [self.ov]
            return vv

        def w(self, ap, keys=("",)):
            if not isinstance(keys, list):
                keys = (keys,)
            vv = V(ap, [P.unit((self.name, k)) for k in keys])
            if self.ov is not None:
                vv.ro = [self.ov]
            return vv

    class Alias:
        def __init__(self, base, ap):
            self.base = base; self.ap0 = ap
        def v(self, idx=None, keys=("",)):
            return self.base.w(self.ap0 if idx is None else self.ap0[idx], keys)

    x = T2("x", [128, 8, TOKMAX], F32)
    cst = T2("cst", [128, 7, 128], F32)
    jbf = T2("jbf", [128, 128], BF16)
    ones1 = T2("ones1", [1, 128], F32)
    pv = T2("pv", [128, 128], F32); pvst = T2("pvst", [128, 128], F32)
    gw2b = T2("gw2b", [32, 2, 256], BF16)
    qnc = T2("qnc", [128, 2], F32)
    uhalo = T2("uhalo", [128, 2, 4, 2], F32)
    S_p = [T2(f"S_p{e}", [128, 2, 128], F32) for e in range(NE)]
    S_s = [[T2(f"S_s{e}_{s}", [128, 2, 128], F32) for s in range(2)] for e in range(NE)]
    dummy = T2("dummy", [128, 8], F32)
    wA = T2("wA", [128, 8, 3088], BF16)
    wB = T2("wB", [128, 8, D], BF16)
    R1 = cur[0]
    ov_unit = P.unit(("ov", "R1"))

    _b_io = [R1]
    xstages = [T2(f"xstage{i_}", [128, D], F32, base=_b_io, ov=ov_unit) for i_ in range(3)]
    xst_i = [0]
    ident = cst.v(np.s_[:, 0, :]); Jf = cst.v(np.s_[:, 1, :]); Mc = cst.v(np.s_[:, 2, :]); M2 = cst.v(np.s_[:, 3, :])
    onesD = cst.v(np.s_[:, 4, :]); ones64 = cst.v(np.s_[:, 5, :]); ones128 = cst.v(np.s_[:, 6, :])

    def phase_barrier():
        P.op("dve", lambda e: e.memset(dummy.t[:, 0:1], 0.0), r=(), w=[ov_unit, P.unit(("dummy", ""))])

    banks = [Tile(P, f"pb{i}", [128, 512], F32, psum=True) for i in range(8)]
    bank_i = [0]

    cur_pool = [None]
    pool_i = [0, 0]

    def ps():
        if cur_pool[0] is not None:
            p_ = cur_pool[0]
            b = banks[p_ * 4 + pool_i[p_] % 4]
            pool_i[p_] += 1
            return b
        b = banks[bank_i[0] % 8]
        bank_i[0] += 1
        return b

    def bv(b, idx=None):
        return V(b.t[idx] if idx is not None else b.t[:], [P.unit((b.name, ""))])

    def units_r(views):
        out = []
        for v_ in views:
            if v_ is None or not isinstance(v_, V):
                continue
            out.extend(v_.units)
            out.extend(getattr(v_, "ro", ()))
        return out

    def units_w(v_):
        return list(v_.units)

    def units_wr(v_):
        return list(getattr(v_, "ro", ()))

    def OP(eng, fn, ins, out, dma=False):
        return P.op(eng, fn, r=units_r(ins) + units_wr(out), w=units_w(out), dma=dma)

    def mm(out, lhsT, rhs, start=True, stop=True):
        return OP("pe", lambda e: e.matmul(out.ap, lhsT.ap, rhs.ap, start=start, stop=stop), [lhsT, rhs], out)

    def tr(out, in_, idv):
        return OP("pe", lambda e: e.transpose(out.ap, in_.ap, idv.ap), [in_, idv], out)

    def act(out, in_, func, bias=None, scale=None):
        kw = {}
        if bias is not None:
            kw["bias"] = bias.ap if isinstance(bias, V) else bias
        if scale is not None:
            kw["scale"] = scale.ap if isinstance(scale, V) else scale
        return OP("act", lambda e: e.activation(out.ap, in_.ap, func, **kw), [in_, bias, scale], out)

    def tt(out, a, b, op, eng="dve"):
        return OP(eng, lambda e: e.tensor_tensor(out.ap, a.ap, b.ap, op), [a, b], out)

    def stt(out, a, sc, b, op0, op1, eng="dve"):
        s_ = sc.ap if isinstance(sc, V) else sc
        return OP(eng, lambda e: e.scalar_tensor_tensor(out.ap, a.ap, s_, b.ap, op0, op1), [a, sc, b], out)

    def cp(out, in_, eng="dve"):
        if eng == "act":
            return OP("act", lambda e: e.copy(out.ap, in_.ap), [in_], out)
        return OP(eng, lambda e: e.tensor_copy(out.ap, in_.ap), [in_], out)

    def ms(out, val, eng="dve"):
        return OP(eng, lambda e: e.memset(out.ap, val), [], out)

    def rcp(out, in_):
        return OP("dve", lambda e: e.reciprocal(out.ap, in_.ap), [in_], out)

    def dma(out, in_, q="sp"):
        return OP(q, lambda e: e.dma_start(out=out.ap, in_=in_.ap), [in_], out, dma=True)

    dma(cst.v(), dv(consts_d.rearrange("p (a b) -> p a b", a=7), "consts"))
    cp(jbf.v(), Jf)
    ms(ones1.v(), 1.0)
    ms(pvst.v(), 0.0)
    dma(pvst.v(np.s_[0:32, :]), dv(norm_mix.rearrange("l (c p) -> (l c) p", p=128), "nm"))
    dma(pvst.v(np.s_[32:64, :]), dv(norm_ffn.rearrange("l (c p) -> (l c) p", p=128), "nf"))
    dma(pvst.v(np.s_[64:88, :]), dv(conv_w.rearrange("e i (c p) -> (e i c) p", p=128), "cw"))
    dma(pvst.v(np.s_[88:90, :]), dv(onorm, "onc"))
    for hh in range(2):
        dma(pvst.v(np.s_[90:92, hh * 64:(hh + 1) * 64]), dv(q_norm, "qn"))
        dma(pvst.v(np.s_[92:94, hh * 64:(hh + 1) * 64]), dv(k_norm, "kn"))
    _bt = ps()
    tr(bv(_bt, np.s_[:, 0:128]), pvst.v(), ident)
    cp(pv.v(), bv(_bt, np.s_[:, 0:128]))
    OP("dve", lambda e: e.tensor_scalar(qnc.t[:], pv.t[:, 90:92], 0.125, None, ALU.mult), [pv.v()], qnc.v())
    ms(gw2b.v(), 0.0)
    dma(gw2b.v(np.s_[0:16, :, :]), dv(gk_w2.rearrange("e r n -> r e n"), "gw2"), q="pool")
    dma(gw2b.v(np.s_[16:17, :, :]), dv(gk_b.rearrange("(o e) n -> o e n", o=1), "gb"), q="pool")
    ms(uhalo.v(), 0.0)
    for e_ in range(NE):
        ms(S_p[e_].v(), 0.0)
        for s in range(2):
            for hh in range(2):
                dma(S_s[e_][s].w(S_s[e_][s].t[hh * 64:(hh + 1) * 64, :, :]),
                    dv(sgla[e_, s].rearrange("(pr hh) k v -> hh k pr v", hh=2)[hh], "sgla"))
    for o in range(NO):
        dma(dv(ext_d.ap()[o, :, 0:64], ("ext", o)), dv(rel_bias[o, :, 0:64], "rb"))
        dma(dv(ext_d.ap()[o, :, 64:384], ("ext", o)), dv(rel_bias[o, :, 0:320], "rb"))
        dma(pvst.v(np.s_[0:16, 0:1]), dv(rel_bias[o, :, 319:320], "rb"))
        cp(pvst.v(np.s_[0:16, 1:128]), V(pvst.t[0:16, 0:1].to_broadcast([16, 127]), pvst.v().units))
        for q_ in range(3):
            dma(dv(ext_d.ap()[o, :, 384 + q_ * 128:512 + q_ * 128], ("ext", o)), pvst.v(np.s_[0:16, :]))

    def load_x_tile(tcol, src_rows):
        xstage = xstages[xst_i[0] % 3]; xst_i[0] += 1
        dma(xstage.v(), dv(src_rows, "xin"))
        for half in range(2):
            b = ps()
            for c in range(4):
                tr(bv(b, np.s_[:, c * 128:(c + 1) * 128]), xstage.v(np.s_[:, (half * 4 + c) * 128:(half * 4 + c + 1) * 128]), ident)
            cp(x.v(np.s_[:, half * 4:half * 4 + 4, tcol:tcol + 128], ("t", tcol)),
               V(b.t[:, :].rearrange("p (c n) -> p c n", c=4), bv(b).units), eng="act")

    def store_y_tile(tcol, dst_rows):
        xstage = xstages[xst_i[0] % 3]; xst_i[0] += 1
        for half in range(2):
            b = ps()
            for c in range(4):
                tr(bv(b, np.s_[:, c * 128:(c + 1) * 128]), x.v(np.s_[:, half * 4 + c, tcol:tcol + 128], ("t", tcol)), ident)
            cp(xstage.v(np.s_[:, half * 512:(half + 1) * 512]), bv(b), eng="act")
        dma(dv(dst_rows, "yout"), xstage.v())

    def rmsnorm(tcol, gcol_t, l, hT, sqt, rst):
        xt = x.v(np.s_[:, :, tcol:tcol + 128], ("t", tcol))
        act(sqt.v(), xt, AF.Square)
        b = ps()
        for c in range(8):
            mm(bv(b, np.s_[:, 0:128]), onesD, sqt.v(np.s_[:, c, :]), start=(c == 0), stop=(c == 7))
        act(rst.v(), bv(b, np.s_[:, 0:128]), AF.Ln, bias=EPS)
        act(rst.v(), rst.v(), AF.Exp, scale=-0.5)
        for c in range(8):
            stt(hT(c), x.v(np.s_[:, c, tcol:tcol + 128], ("t", tcol)), pv.w(pv.t[:, gcol_t + l * 8 + c:gcol_t + l * 8 + c + 1]), rst.v(), ALU.mult, ALU.mult)

    def add_to_x(tcol, c0, nch, b, n=128):
        xv = x.v(np.s_[:, c0:c0 + nch, tcol:tcol + n], ("t", tcol))
        tt(xv, xv, V(b.t[:, 0:nch * n].rearrange("p (c n) -> p c n", c=nch), bv(b).units), ALU.add)

    def load_w(dst, src_ap, key):
        nk = src_ap.shape[1]
        for kc in range(nk):
            dma(V(dst.ap[:, kc, :], dst.units + list(dst.ro)) if False else _sub(dst, kc), dv(src_ap[:, kc, :], key), q="pool")

    def _sub(dst, kc):
        vv = V(dst.ap[:, kc, :], dst.units)
        vv.ro = dst.ro
        return vv

    _ebase = [R1]

    def even_bufs(par):
        base = _ebase
        B = {}
        def mk(name, shape, dt):
            B[name] = T2(f"e{par}_" + name, shape, dt, base=base, ov=ov_unit)
        mk("hT", [128, 8, 128], BF16); mk("sqt", [128, 8, 128], F32); mk("rst", [128, 128], F32)
        mk("cg", [128, 4, 128], F32); mk("ub", [128, 4, 130], F32); mk("ubA", [128, 4, 66], F32); mk("ubB", [128, 4, 66], F32)
        mk("yc", [128, 4, 128], F32); mk("tmp", [128, 4, 128], F32)
        mk("mixT", [128, 8, 128], BF16); mk("sg", [128, 4, 128], F32); mk("gkT", [32, 128], BF16)
        mk("ktok", [128, 256], F32); mk("vbf", [128, 512], BF16)
        mk("e1", [128, 256], F32); mk("sp", [128, 256], F32); mk("erb", [128, 256], F32); mk("kdec", [128, 2, 256], BF16)
        mk("ebT", [128, 2, 128], F32); mk("enbT", [128, 2, 128], F32)
        mk("qe", [128, 2, 2, 128], BF16); mk("ke", [128, 2, 128], BF16)
        mk("attm", [128, 4, 128], BF16); mk("Sb0", [128, 2, 128], BF16); mk("Sb1", [128, 2, 128], BF16)
        mk("sq", [128, 512], F32); mk("rs2", [128, 512], F32); mk("y1", [128, 4, 128], F32)
        mk("cstg", [2, 512], F32); mk("qkraw", [128, 4, 128], F32)
        return B

    def odd_bufs():
        base = [R1]
        B = {}
        def mk(name, shape, dt):
            B[name] = T2("o_" + name, shape, dt, base=base, ov=ov_unit)
        mk("hT", [128, 8, 128], BF16); mk("rst", [128, 128], F32)
        mk("qT", [128, 2, 8, 128], BF16); mk("kn", [128, 8, 128], F32); mk("kown", [128, 8, 128], BF16)
        mk("raw", [128, 512], F32); mk("sq", [128, 512], F32); mk("rs2", [128, 512], F32)
        mk("kring", [128, 8, 8 * 128], BF16); mk("vring", [128, 8, 16 * 65], BF16); mk("vown", [128, 16 * 65], BF16)
        mk("H", [128, 16, 5, 128], BF16); mk("pT", [128, 4, 5, 128], BF16)
        mk("atok", [128, D], F32); mk("rc", [128, 16], F32); mk("attT", [128, 8, 128], BF16)
        mk("stage", [128, D], F32)
        B["ckst"] = B["stage"]
        B["sqt"] = Alias(B["atok"], B["atok"].t[:, :].rearrange("p (c n) -> p c n", c=8))
        return B

    def ffn_bufs():
        base = [R1]
        B = {}
        def mk(name, shape, dt):
            B[name] = T2("f_" + name, shape, dt, base=base, ov=ov_unit)
        mk("hTall", [128, 8, TOKMAX], BF16); mk("sqt", [128, 8, 128], F32); mk("rst", [128, 128], F32)
        for i in range(2):
            mk(f"win{i}", [128, 8, 512], BF16); mk(f"wout{i}", [128, 2, D], BF16)
            mk(f"a{i}", [128, 2, 512], BF16); mk(f"sgf{i}", [128, 512], F32)
        return B

    EBS = [even_bufs(0), even_bufs(1)]; OB = odd_bufs(); FB = ffn_bufs()

    def even_gen(e, l, tcol, sample, want_state, par):
        B = EBS[par]
        hT = lambda c: B["hT"].v(np.s_[:, c, :])
        rmsnorm(tcol, 0, l, hT, B["sqt"], B["rst"])
        hTa = B["hT"]
        yield None

        def fm4(col0, nch=4):
            b = ps()
            for c in range(nch):
                for kc in range(8):
                    mm(bv(b, np.s_[:, c * 128:(c + 1) * 128]), wA.v(np.s_[:, kc, col0 + c * 128:col0 + (c + 1) * 128]), hTa.v(np.s_[:, kc, :]),
                       start=(kc == 0), stop=(kc == 7))
            return b

        def b3(b, nch=4):
            return V(b.t[:, 0:nch * 128].rearrange("p (c n) -> p c n", c=nch), bv(b).units)

        def conv_out(ub, col, dst, key):
            _b = ps()
            for c_ in range(4):
                tr(bv(_b, np.s_[0:2, c_ * 128:(c_ + 1) * 128]), ub.v(np.s_[:, c_, col:col + 2]), ident)
            cp(B["cstg"].v(), bv(_b, np.s_[0:2, :]), eng="act")
            dma(dv(dst, key), B["cstg"].v())

        bcg = fm4(0)
        cp(B["cg"].v(), b3(bcg), eng="act")
        yield None
        bhc = fm4(1024)
        if not sample:
            cp(B["ub"].v(np.s_[:, :, 0:2]), uhalo.v(np.s_[:, e, :, :]), eng="dve")
            tt(B["ub"].v(np.s_[:, :, 2:130]), b3(bhc), B["cg"].v(), ALU.mult)
            segs = [(B["ub"], 128, 0)]
        else:
            for s, ub in enumerate((B["ubA"], B["ubB"])):
                dma(B["cstg"].v(), dv(sconv[e, s], "sconv"))
                _b = ps()
                for c_ in range(4):
                    tr(bv(_b, np.s_[:, c_ * 2:c_ * 2 + 2]), B["cstg"].v(np.s_[:, c_ * 128:(c_ + 1) * 128]), cst.w(cst.t[0:2, 0, 0:2]))
                cp(ub.v(np.s_[:, :, 0:2]), V(_b.t[:, 0:8].rearrange("p (c r) -> p c r", c=4), bv(_b).units), eng="act")
                tt(ub.v(np.s_[:, :, 2:66]), V(bhc.t[:, :].rearrange("p (c n) -> p c n", c=4)[:, :, s * 64:(s + 1) * 64], bv(bhc).units),
                   B["cg"].v(np.s_[:, :, s * 64:(s + 1) * 64]), ALU.mult)
            segs = [(B["ubA"], 64, 0), (B["ubB"], 64, 64)]
        for ub, n, off in segs:
            yv = B["yc"].v(np.s_[:, :, off:off + n])
            tv = B["tmp"].v(np.s_[:, :, off:off + n])
            wv = lambda i: V(pv.t[:, 64 + (e * 3 + i) * 4:64 + (e * 3 + i) * 4 + 4].unsqueeze(2).to_broadcast([128, 4, n]), pv.v().units)
            tt(yv, ub.v(np.s_[:, :, 2:2 + n]), wv(2), ALU.mult, eng="dve")
            tt(tv, ub.v(np.s_[:, :, 1:1 + n]), wv(1), ALU.mult, eng="dve")
            tt(yv, yv, tv, ALU.add, eng="dve")
            tt(tv, ub.v(np.s_[:, :, 0:n]), wv(0), ALU.mult, eng="dve")
            tt(yv, yv, tv, ALU.add, eng="dve")
        if not sample:
            cp(uhalo.v(np.s_[:, e, :, :]), B["ub"].v(np.s_[:, :, 128:130]), eng="dve")
            if want_state:
                conv_out(B["ub"], 128, convp[e], "convp")
        else:
            for s, ub in enumerate((B["ubA"], B["ubB"])):
                conv_out(ub, 64, convs[e, s], "convs")
        yield None
        bbg = fm4(512)
        tt(B["mixT"].v(np.s_[:, 0:4, :]), b3(bbg), B["yc"].v(), ALU.mult)
        yield None
        bgk = ps()
        for kc in range(8):
            mm(bv(bgk, np.s_[0:16, 0:128]), wA.v(np.s_[:, kc, 3072:3088]), hTa.v(np.s_[:, kc, :]), start=(kc == 0), stop=(kc == 7))
        ms(B["gkT"].v(), 1.0)
        cp(B["gkT"].v(np.s_[0:16, :]), bv(bgk, np.s_[0:16, 0:128]), eng="act")
        bk = ps(); bvv = ps()
        for kc in range(8):
            mm(bv(bk, np.s_[:, 0:256]), hTa.v(np.s_[:, kc, :]), wA.v(np.s_[:, kc, 1792:2048]), start=(kc == 0), stop=(kc == 7))
        for kc in range(8):
            mm(bv(bvv), hTa.v(np.s_[:, kc, :]), wA.v(np.s_[:, kc, 2048:2560]), start=(kc == 0), stop=(kc == 7))
        cp(B["ktok"].v(), bv(bk, np.s_[:, 0:256]), eng="act")
        cp(B["vbf"].v(), bv(bvv), eng="act")
        yield None
        bqk = fm4(1536)
        cp(B["qkraw"].v(), b3(bqk), eng="act")
        yield None
        bg = fm4(2560)
        act(B["sg"].v(), b3(bg), AF.Silu)
        yield "half"

        bz = ps()
        mm(bv(bz, np.s_[:, 0:256]), B["gkT"].v(), gw2b.v(np.s_[:, e, :]))
        act(B["e1"].v(), bv(bz, np.s_[:, 0:256]), AF.Exp, scale=-1.0)
        act(B["sp"].v(), B["e1"].v(), AF.Ln, bias=1.0)
        yield None
        brb = ps()
        mm(bv(brb, np.s_[:, 0:256]), M2, B["sp"].v())
        bbT = ps()
        for p_ in range(2):
            mm(bv(bbT, np.s_[:, p_ * 128:(p_ + 1) * 128]), B["sp"].v(np.s_[:, p_ * 128:(p_ + 1) * 128]), Mc)
        act(B["erb"].v(), bv(brb, np.s_[:, 0:256]), AF.Exp)
        for cc_ in range(2):
            oc_ = 1 - cc_
            ms(B["kdec"].v(np.s_[oc_ * 64:(oc_ + 1) * 64, cc_, :]), 0.0)
            tt(B["kdec"].v(np.s_[cc_ * 64:(cc_ + 1) * 64, cc_, :]), B["ktok"].v(np.s_[cc_ * 64:(cc_ + 1) * 64, :]), B["erb"].v(np.s_[cc_ * 64:(cc_ + 1) * 64, :]), ALU.mult)
        bT3 = V(bbT.t[:, 0:256].rearrange("p (c n) -> p c n", c=2), bv(bbT).units)
        act(B["ebT"].v(), bT3, AF.Exp, scale=-1.0 / 16)
        act(B["enbT"].v(), bT3, AF.Exp, scale=1.0 / 16)
        qk3 = B["qkraw"].v()
        for hh_ in range(2):
            oh_ = 1 - hh_
            ms(B["qe"].v(np.s_[oh_ * 64:(oh_ + 1) * 64, hh_, :, :]), 0.0)
            stt(B["qe"].v(np.s_[hh_ * 64:(hh_ + 1) * 64, hh_, :, :]), B["qkraw"].v(np.s_[hh_ * 64:(hh_ + 1) * 64, 0:2, :]), 0.125,
                B["ebT"].v(np.s_[hh_ * 64:(hh_ + 1) * 64, :, :]), ALU.mult, ALU.mult)
        tt(B["ke"].v(), B["qkraw"].v(np.s_[:, 2:4, :]), B["enbT"].v(), ALU.mult)
        yield None
        batt = ps()
        for h in range(4):
            pr, hh = h // 2, h % 2
            mm(bv(batt, np.s_[:, h * 128:(h + 1) * 128]), B["ke"].v(np.s_[:, pr, :]), B["qe"].v(np.s_[:, hh, pr, :]))
        tt(B["attm"].v(), b3(batt), V(cst.t[:, 2, :].unsqueeze(1).to_broadcast([128, 4, 128]), cst.v().units), ALU.mult)
        yield "need_state"
        if not sample:
            Sc = [S_p[e], S_p[e]]
        else:
            Sc = [S_s[e][0], S_s[e][1]]
        Sb = [B["Sb0"], B["Sb1"]]

        def upd(cc):
            S = Sc[cc]
            bs = ps()
            for pr in range(2):
                mm(bv(bs, np.s_[:, pr * 256:(pr + 1) * 256]), B["kdec"].v(np.s_[:, cc, pr * 128:(pr + 1) * 128]),
                   B["vbf"].v(np.s_[:, pr * 256:(pr + 1) * 256]))
            for pr in range(2):
                for hh in range(2):
                    sv = S.v(np.s_[hh * 64:(hh + 1) * 64, pr, :])
                    stt(sv, sv, B["ebT"].v(np.s_[hh * 64:(hh + 1) * 64, pr, cc * 64 + 63:cc * 64 + 64]),
                        bv(bs, np.s_[hh * 64:(hh + 1) * 64, pr * 256 + hh * 128:pr * 256 + (hh + 1) * 128]), ALU.mult, ALU.add)

        if not sample:
            cp(Sb[0].v(), Sc[0].v(), eng="act")
            upd(0)
            cp(Sb[1].v(), Sc[1].v(), eng="act")
            upd(1)
        else:
            cp(Sb[0].v(), Sc[0].v(), eng="act")
            cp(Sb[1].v(), Sc[1].v(), eng="act")
            upd(0)
            upd(1)
        if want_state:
            if not sample:
                for hh in range(2):
                    dma(dv(glap[e].rearrange("(pr hh) k v -> hh k pr v", hh=2)[hh], "glap"), S_p[e].v(np.s_[hh * 64:(hh + 1) * 64, :, :]))
            else:
                for s in range(2):
                    for hh in range(2):
                        dma(dv(glas[e, s].rearrange("(pr hh) k v -> hh k pr v", hh=2)[hh], "glas"), S_s[e][s].v(np.s_[hh * 64:(hh + 1) * 64, :, :]))
        yield "state_done"
        bo = ps()
        for h in range(4):
            pr, hh = h // 2, h % 2
            for cc in range(2):
                ov_ = bv(bo, np.s_[:, h * 128 + cc * 64:h * 128 + (cc + 1) * 64])
                mm(ov_, B["vbf"].v(np.s_[:, h * 128:(h + 1) * 128]), B["attm"].v(np.s_[:, h, cc * 64:(cc + 1) * 64]), start=True, stop=False)
                mm(ov_, Sb[cc].v(np.s_[:, pr, :]), B["qe"].v(np.s_[:, hh, pr, cc * 64:(cc + 1) * 64]), start=False, stop=True)
        act(B["sq"].v(), bv(bo), AF.Square)
        yield None
        bm = ps()
        mm(bv(bm), ones128, B["sq"].v())
        act(B["rs2"].v(), bv(bm), AF.Ln, bias=EPS)
        act(B["rs2"].v(), B["rs2"].v(), AF.Exp, scale=-0.5)
        stt(B["y1"].v(), b3(bo), pv.w(pv.t[:, 88 + e:89 + e]), V(B["rs2"].t[:, :].rearrange("p (c n) -> p c n", c=4), B["rs2"].v().units), ALU.mult, ALU.mult)
        tt(B["mixT"].v(np.s_[:, 4:8, :]), B["y1"].v(), B["sg"].v(), ALU.mult, eng="dve")
        yield None
        for half in range(2):
            b = ps()
            for c in range(4):
                dc = half * 4 + c
                for mc in range(8):
                    mm(bv(b, np.s_[:, c * 128:(c + 1) * 128]), wB.v(np.s_[:, mc, dc * 128:(dc + 1) * 128]), B["mixT"].v(np.s_[:, mc, :]),
                       start=(mc == 0), stop=(mc == 7))
            add_to_x(tcol, half * 4, 4, b)
            yield None

    def run_pipelined(specs, genf):
        def mk(i, spec):
            return {"g": genf(par=i % 2, **spec), "par": i % 2, "half": False, "sdone": False, "done": False, "wait": False}

        def step(st):
            cur_pool[0] = st["par"]
            r = next(st["g"], "DONE")
            cur_pool[0] = None
            if r == "DONE":
                st["done"] = True; st["sdone"] = True; st["half"] = True
            elif r == "half":
                st["half"] = True
            elif r == "need_state":
                st["wait"] = True
            elif r == "state_done":
                st["sdone"] = True

        old = None
        for i, spec in enumerate(specs):
            new = mk(i, spec)
            while True:
                if new["wait"] and (old is None or old["sdone"]):
                    new["wait"] = False
                progressed = False
                if not new["done"] and not new["wait"] and not (new["half"] and old is None and False):
                    step(new); progressed = True
                if old is not None and not old["done"]:
                    step(old); progressed = True
                if old is not None and old["done"]:
                    old = None
                if new["done"] or (new["half"] and old is None):
                    break
                assert progressed
            old = None if new["done"] else new
        while old is not None and not old["done"]:
            old["wait"] = False
            step(old)

    def attend(o, qlo, nq, kblk, vblk):
        B = OB
        obanks = [banks[0], banks[1], banks[2]]
        for hg in range(4):
            sB = banks[3]
            for hi in range(4):
                h = hg * 4 + hi
                pr, hh = h // 2, h % 2
                sA = banks[4 + hi]
                qv = B["qT"].v(np.s_[:, hh, pr, qlo:qlo + nq])
                if nq == 128:
                    mm(bv(sA), jbf.v(), B["H"].w(B["H"].t[:, h, 0:4, :].rearrange("p a b -> p (a b)")), start=True, stop=False)
                    for kb in range(1, 5):
                        sl = 4 - kb
                        mm(bv(sA, np.s_[:, sl * 128:sl * 128 + nq]), kblk(kb, h), qv, start=False, stop=(kb == 4))
                else:
                    for kb in range(1, 5):
                        sl = 4 - kb
                        tgt = bv(sA, np.s_[:, sl * 128:sl * 128 + nq])
                        mm(tgt, jbf.v(), B["H"].v(np.s_[:, h, sl, qlo:qlo + nq]), start=True, stop=False)
                        mm(tgt, kblk(kb, h), qv, start=False, stop=True)
                tgt = bv(sB, np.s_[:, hi * 128:hi * 128 + nq])
                mm(tgt, jbf.v(), B["H"].v(np.s_[:, h, 4, qlo:qlo + nq]), start=True, stop=False)
                mm(tgt, kblk(0, h), qv, start=False, stop=True)
                act(B["pT"].w(B["pT"].t[:, hi, 0:4, 0:nq]),
                    V(sA.t[:, :].rearrange("p (c n) -> p c n", c=4)[:, :, 0:nq], bv(sA).units), AF.Exp)
            act(B["pT"].w(B["pT"].t[:, :, 4, 0:nq]), V(sB.t[:, :].rearrange("p (c n) -> p c n", c=4)[:, :, 0:nq], bv(sB).units), AF.Exp)
            for hi in range(4):
                h = hg * 4 + hi
                ob = obanks[h // 7]
                col = (h % 7) * 65
                for kb in range(5):
                    mm(bv(ob, np.s_[0:nq, col:col + 65]), B["pT"].w(B["pT"].t[:, hi, 4 - kb, 0:nq]), vblk(kb, h), start=(kb == 0), stop=(kb == 4))
        for bi, ob in enumerate(obanks):
            nh = 7 if bi < 2 else 2
            h0 = bi * 7
            o3 = V(ob.t[0:nq, 0:nh * 65].rearrange("p (h n) -> p h n", h=nh), bv(ob).units)
            rcp(B["rc"].w(B["rc"].t[0:nq, h0:h0 + nh]), V(o3.ap[:, :, 64], o3.units))
            tt(B["atok"].w(B["atok"].t[0:nq, h0 * 64:(h0 + nh) * 64].rearrange("p (h n) -> p h n", h=nh)),
               V(o3.ap[:, :, 0:64], o3.units),
               B["rc"].w(B["rc"].t[0:nq, h0:h0 + nh].unsqueeze(2).to_broadcast([nq, nh, 64])), ALU.mult)
        for half in range(2):
            b = banks[4 + half]
            for c in range(4):
                fc = half * 4 + c
                tr(bv(b, np.s_[:, c * 128:c * 128 + nq]), B["atok"].w(B["atok"].t[0:nq, fc * 128:(fc + 1) * 128]), cst.w(cst.t[0:nq, 0, 0:nq]))
            cp(B["attT"].w(B["attT"].t[:, half * 4:half * 4 + 4, qlo:qlo + nq]),
               V(b.t[:, :].rearrange("p (c n) -> p c n", c=4)[:, :, 0:nq], bv(b).units), eng="act")

    def build_hscr(o):
        B = OB
        for h0 in range(16):
            src = bass.AP(ext_d, o * 16 * 768 + h0 * 768, [[1, 128], [128, 5], [1, 128]])
            dma(B["H"].w(B["H"].t[:, h0, :, :]), dv(src, ("ext", o)), q="pool")
        ms(B["H"].w(B["H"].t[64:128, :, 4, 64:128]), NEG)
        ms(B["H"].w(B["H"].t[0:64, :, 0, 0:64]), NEG)
        dma(dv(hscr.ap()[o], ("hscr", o)), B["H"].w(B["H"].t[:, :, :, :].rearrange("p a b c -> p (a b c)")))

    def odd_phase_begin(o, j):
        B = OB
        dma(B["H"].w(B["H"].t[:, :, :, :].rearrange("p a b c -> p (a b c)")), dv(hscr.ap()[o], ("hscr", o)))
        if j == 0:
            ms(B["kring"].w(B["kring"].t[:, :, 512:1024], [("s", s) for s in range(4, 8)]), 0.0)
            ms(B["vring"].w(B["vring"].t[:, 4:8, :], [("s", s) for s in range(4, 8)]), 0.0)
        else:
            dma(B["kring"].w(B["kring"].t[:, :, 512:1024], [("s", s) for s in range(4, 8)]), dv(kscr.ap()[o], ("kscr", o)))
            dma(B["vring"].w(B["vring"].t[:, 4:8, :], [("s", s) for s in range(4, 8)]), dv(vscr.ap()[o], ("vscr", o)))

    def odd_phase_end(o, j, last):
        B = OB
        if not last:
            dma(dv(kscr.ap()[o], ("kscr", o)), B["kring"].w(B["kring"].t[:, :, 512:1024], [("s", s) for s in range(4, 8)]))
            dma(dv(vscr.ap()[o], ("vscr", o)), B["vring"].w(B["vring"].t[:, 4:8, :], [("s", s) for s in range(4, 8)]))

    def odd_tile(o, l, tcol, t, sample, out_rows):
        B = OB
        hT = lambda c: B["hT"].v(np.s_[:, c, :])
        rmsnorm(tcol, 0, l, hT, B["sqt"], B["rst"])
        hTa = B["hT"]
        slot = t if not sample else None
        for which in range(2):
            for half in range(2):
                b = ps()
                for c in range(4):
                    col0 = which * D + (half * 4 + c) * 128
                    for kc in range(8):
                        mm(bv(b, np.s_[:, c * 128:(c + 1) * 128]), wA.v(np.s_[:, kc, col0:col0 + 128]), hTa.v(np.s_[:, kc, :]),
                           start=(kc == 0), stop=(kc == 7))
                cp(B["raw"].v(), bv(b), eng="act")
                act(B["sq"].v(), bv(b), AF.Square)
                bm = ps()
                mm(bv(bm), ones64, B["sq"].v())
                act(B["rs2"].v(), bv(bm), AF.Ln, bias=EPS)
                act(B["rs2"].v(), B["rs2"].v(), AF.Exp, scale=-0.5)
                r3 = lambda T_: V(T_.t[:, :].rearrange("p (c n) -> p c n", c=4), T_.v().units + [ov_unit] * 0)
                if which == 0:
                    for hh_ in range(2):
                        oh_ = 1 - hh_
                        ms(B["qT"].v(np.s_[oh_ * 64:(oh_ + 1) * 64, hh_, half * 4:half * 4 + 4, :]), 0.0)
                        stt(B["qT"].v(np.s_[hh_ * 64:(hh_ + 1) * 64, hh_, half * 4:half * 4 + 4, :]),
                            B["raw"].w(r3(B["raw"]).ap[hh_ * 64:(hh_ + 1) * 64]), qnc.w(qnc.t[hh_ * 64:(hh_ + 1) * 64, o:o + 1]),
                            B["rs2"].w(r3(B["rs2"]).ap[hh_ * 64:(hh_ + 1) * 64]), ALU.mult, ALU.mult)
                else:
                    stt(B["kn"].v(np.s_[:, half * 4:half * 4 + 4, :]), B["raw"].w(r3(B["raw"]).ap), pv.w(pv.t[:, 92 + o:93 + o]),
                        B["rs2"].w(r3(B["rs2"]).ap), ALU.mult, ALU.mult)
        if not sample:
            cp(B["kring"].w(B["kring"].t[:, :, slot * 128:(slot + 1) * 128], ("s", slot)), B["kn"].v(), eng="act")
        else:
            cp(B["kown"].v(), B["kn"].v(), eng="act")
        vb = [ps(), ps()]
        for half in range(2):
            for kc in range(8):
                mm(bv(vb[half]), hTa.v(np.s_[:, kc, :]), wA.v(np.s_[:, kc, 2 * D + half * 512:2 * D + (half + 1) * 512]), start=(kc == 0), stop=(kc == 7))
        for half in range(2):
            src = V(vb[half].t[:, :].rearrange("p (h n) -> p h n", h=8), bv(vb[half]).units)
            if not sample:
                dst = B["vring"].w(B["vring"].t[:, slot, :].rearrange("p (h n) -> p h n", h=16)[:, half * 8:(half + 1) * 8, 0:64], ("s", slot))
            else:
                dst = B["vown"].w(B["vown"].t[:, :].rearrange("p (h n) -> p h n", h=16)[:, half * 8:(half + 1) * 8, 0:64])
            cp(dst, src, eng="act")
        if not sample:
            ms(B["vring"].w(B["vring"].t[:, slot, :].rearrange("p (h n) -> p h n", h=16)[:, :, 64:65], ("s", slot)), 1.0)
        else:
            ms(B["vown"].w(B["vown"].t[:, :].rearrange("p (h n) -> p h n", h=16)[:, :, 64:65]), 1.0)
        if out_rows is not None or sample:
            for half in range(2):
                cp(B["stage"].v(np.s_[:, half * 512:(half + 1) * 512]), bv(vb[half]), eng="act")
            if not sample:
                dma(dv(vp[o, out_rows:out_rows + 128, :], "vp"), B["stage"].v())
            else:
                for s in range(2):
                    dma(dv(vs[o, s, 448:512, :], "vs"), B["stage"].v(np.s_[s * 64:(s + 1) * 64, :]))
            for half in range(2):
                b = ps()
                for c in range(4):
                    tr(bv(b, np.s_[:, c * 128:(c + 1) * 128]), B["kn"].v(np.s_[:, half * 4 + c, :]), ident)
                cp(B["stage"].v(np.s_[:, half * 512:(half + 1) * 512]), bv(b), eng="act")
            if not sample:
                dma(dv(kp[o, out_rows:out_rows + 128, :], "kp"), B["stage"].v())
            else:
                for s in range(2):
                    dma(dv(ks[o, s, 448:512, :], "ks"), B["stage"].v(np.s_[s * 64:(s + 1) * 64, :]))
        if not sample:
            def kblk(kb, h):
                s_ = (t - 4 + kb) % 8
                return B["kring"].w(B["kring"].t[:, h // 2, s_ * 128:(s_ + 1) * 128], ("s", s_))
            def vblk(kb, h):
                s_ = (t - 4 + kb) % 8
                return B["vring"].w(B["vring"].t[:, s_, h * 65:(h + 1) * 65], ("s", s_))
            attend(o, 0, 128, kblk, vblk)
        else:
            for s in range(2):
                dma(dv(ks[o, s, 0:448, :], "ks"), dv(ck[o, s, 64:512, :], "ck"))
                dma(dv(vs[o, s, 0:448, :], "vs"), dv(cv[o, s, 64:512, :], "cv"))
                shift = 64 * s
                for m in range(5):
                    r_lo = max(0, 128 * m - shift); r_hi = min(512, 128 * m + 128 - shift)
                    kslot = B["kring"].w(B["kring"].t[:, :, m * 128:(m + 1) * 128], ("s", m))
                    vslot = B["vring"].w(B["vring"].t[:, m, :], ("s", m))
                    ms(vslot, 0.0)
                    if r_hi > r_lo:
                        p_lo = r_lo + shift - 128 * m
                        n = r_hi - r_lo
                        ms(B["ckst"].v(), 0.0)
                        dma(B["ckst"].w(B["ckst"].t[p_lo:p_lo + n, :]), dv(ck[o, s, r_lo:r_hi, :], "ck"))
                        for half in range(2):
                            b = ps()
                            for c in range(4):
                                tr(bv(b, np.s_[:, c * 128:(c + 1) * 128]), B["ckst"].v(np.s_[:, (half * 4 + c) * 128:(half * 4 + c + 1) * 128]), ident)
                            cp(B["kring"].w(B["kring"].t[:, half * 4:half * 4 + 4, m * 128:(m + 1) * 128], ("s", m)),
                               V(b.t[:, :].rearrange("p (c n) -> p c n", c=4), bv(b).units), eng="act")
                        dma(B["vring"].w(B["vring"].t[p_lo:p_lo + n, m, :].rearrange("p (h n) -> p h n", h=16)[:, :, 0:64], ("s", m)),
                            dv(cv[o, s, r_lo:r_hi, :].rearrange("r (h n) -> r h n", h=16), "cv"), q="pool")
                        ms(B["vring"].w(B["vring"].t[:, m, :].rearrange("p (h n) -> p h n", h=16)[:, :, 64:65], ("s", m)), 1.0)
                    else:
                        ms(kslot, 0.0)
                cp(B["kring"].w(B["kring"].t[:, :, 4 * 128 + shift:4 * 128 + shift + 64], ("s", 4)), B["kown"].v(np.s_[:, :, shift:shift + 64]), eng="act")
                cp(B["vring"].w(B["vring"].t[shift:shift + 64, 4, :], ("s", 4)), B["vown"].v(np.s_[shift:shift + 64, :]), eng="dve")
                def kblk(kb, h):
                    return B["kring"].w(B["kring"].t[:, h // 2, kb * 128:(kb + 1) * 128], ("s", kb))
                def vblk(kb, h):
                    return B["vring"].w(B["vring"].t[:, kb, h * 65:(h + 1) * 65], ("s", kb))
                attend(o, shift, 64, kblk, vblk)
        for half in range(2):
            b = ps()
            for c in range(4):
                dc = half * 4 + c
                for mc in range(8):
                    mm(bv(b, np.s_[:, c * 128:(c + 1) * 128]), wB.v(np.s_[:, mc, dc * 128:(dc + 1) * 128]), B["attT"].v(np.s_[:, mc, :]),
                       start=(mc == 0), stop=(mc == 7))
            add_to_x(tcol, half * 4, 4, b)

    def ffn_phase(l, ntiles):
        B = FB
        ntok = ntiles * 128

        def load_slab(s):
            i = s % 2
            dma(B[f"win{i}"].v(np.s_[:, :, 0:256], "g"), dv(w_ffn_in[l][:, s * 256:(s + 1) * 256].rearrange("(kc p) n -> p kc n", p=128), "wfi"), q="pool")
            dma(B[f"win{i}"].v(np.s_[:, :, 256:512], "u"), dv(w_ffn_in[l][:, DFF + s * 256:DFF + (s + 1) * 256].rearrange("(kc p) n -> p kc n", p=128), "wfi"), q="pool")
            dma(B[f"wout{i}"].v(), dv(w_ffn_out[l][s * 256:(s + 1) * 256, :].rearrange("(fc p) n -> p fc n", p=128), "wfo"), q="pool")

        load_slab(0)
        for t in range(ntiles):
            hT = lambda c, t=t: B["hTall"].v(np.s_[:, c, t * 128:(t + 1) * 128], ("t", t))
            rmsnorm(t * 128, 32, l, hT, B["sqt"], B["rst"])
        chunks = []
        c0 = 0
        while c0 < ntok:
            n = min(512, ntok - c0)
            chunks.append((c0, n))
            c0 += n
        it = 0
        for s in range(NSLAB):
            if s + 1 < NSLAB:
                load_slab(s + 1)
            i = s % 2
            win = B[f"win{i}"]; wout = B[f"wout{i}"]
            for (c0, n) in chunks:
                tkeys = [("t", tt_) for tt_ in range(c0 // 128, (c0 + n) // 128)]
                ab = B[f"a{it % 2}"]; sgf = B[f"sgf{it % 2}"]
                it += 1
                for fc in range(2):
                    bg_ = ps(); bu_ = ps()
                    for kc in range(8):
                        mm(bv(bg_, np.s_[:, 0:n]), win.v(np.s_[:, kc, fc * 128:(fc + 1) * 128], "g"), B["hTall"].v(np.s_[:, kc, c0:c0 + n], tkeys), start=(kc == 0), stop=(kc == 7))
                    for kc in range(8):
                        mm(bv(bu_, np.s_[:, 0:n]), win.v(np.s_[:, kc, 256 + fc * 128:256 + (fc + 1) * 128], "u"), B["hTall"].v(np.s_[:, kc, c0:c0 + n], tkeys), start=(kc == 0), stop=(kc == 7))
                    act(sgf.v(np.s_[:, 0:n]), bv(bg_, np.s_[:, 0:n]), AF.Silu)
                    tt(ab.v(np.s_[:, fc, 0:n]), sgf.v(np.s_[:, 0:n]), bv(bu_, np.s_[:, 0:n]), ALU.mult)
                for dc in range(8):
                    by = ps()
                    for fc in range(2):
                        mm(bv(by, np.s_[:, 0:n]), wout.v(np.s_[:, fc, dc * 128:(dc + 1) * 128]), ab.v(np.s_[:, fc, 0:n]), start=(fc == 0), stop=(fc == 1))
                    xv = x.v(np.s_[:, dc, c0:c0 + n], [("t", tt_ * 128) for tt_ in range(c0 // 128, (c0 + n) // 128)])
                    tt(xv, xv, bv(by, np.s_[:, 0:n]), ALU.add)

    def issue_mixer_weights(l):
        if l % 2 == 0:
            load_w(wA.v(), w_in_ab[l // 2].rearrange("(kc p) n -> p kc n", p=128), "w_in")
            load_w(wB.v(), w_out_ab[l // 2].rearrange("(kc p) n -> p kc n", p=128), "w_out")
        else:
            load_w(wA.v(np.s_[:, :, 0:3 * D]), w_qkv[l // 2].rearrange("(kc p) n -> p kc n", p=128), "w_qkv")
            load_w(wB.v(), w_o_att[l // 2].rearrange("(kc p) n -> p kc n", p=128), "w_o")

    issue_mixer_weights(0)
    phase_barrier()
    for o_ in range(NO):
        build_hscr(o_)
    phase_barrier()
    for t in range(JT):
        load_x_tile(t * 128, xp[t * 128:(t + 1) * 128, :])
    for j in range(NJ):
        last = (j == NJ - 1)
        ntiles = JT + (1 if last else 0)
        if last:
            load_x_tile(JT * 128, xs[:, :])
        for l in range(DEPTH):
            if _CFG.get("STAGE") == "io":
                break
            phase_barrier()
            if l % 2 == 0:
                e = l // 2
                specs = [dict(e=e, l=l, tcol=t * 128, sample=False, want_state=(last and t == JT - 1)) for t in range(JT)]
                if last:
                    specs.append(dict(e=e, l=l, tcol=JT * 128, sample=True, want_state=True))
                run_pipelined(specs, even_gen)
            else:
                o = l // 2
                odd_phase_begin(o, j)
                for t in range(JT):
                    g = j * JT + t
                    orow = (g - (NTSEQ - 4)) * 128 if g >= NTSEQ - 4 else None
                    odd_tile(o, l, t * 128, t, False, orow)
                odd_phase_end(o, j, last)
                if last:
                    odd_tile(o, l, JT * 128, None, True, None)
            if _CFG.get("STAGE") == "mix" or (_CFG.get("STAGE") == "mix0" and l == 0):
                continue
            phase_barrier()
            if l + 1 < DEPTH:
                issue_mixer_weights(l + 1)
            elif not last:
                issue_mixer_weights(0)
            ffn_phase(l, ntiles)
        phase_barrier()
        for t in range(JT):
            store_y_tile(t * 128, yp[(j * JT + t) * 128:(j * JT + t + 1) * 128, :])
            if not last:
                load_x_tile(t * 128, xp[((j + 1) * JT + t) * 128:((j + 1) * JT + t + 1) * 128, :])
        if last:
            store_y_tile(JT * 128, ys[:, :])

    with nc.allow_non_contiguous_dma(reason="small strided parameter/state transfers"):
        stats = P.emit()
    return nc, stats


_BUILD_CACHE = {}


def kernel(x_prompt, x_sample, state_conv, state_gla, cache_k, cache_v,
           norm_mix, norm_ffn, w_in_ab, conv_w, gla_gk_w2, gla_gk_b, gla_onorm, w_out_ab,
           w_qkv, q_norm, k_norm, rel_bias, w_o_att, w_ffn_in, w_ffn_out):
    f = lambda a: np.ascontiguousarray(np.asarray(a, dtype=np.float32))
    x_prompt = f(x_prompt); x_sample = f(x_sample)
    BATCH, SEQ, _ = x_prompt.shape
    DEPTH = _CFG["DEPTH"]
    NE = (DEPTH + 1) // 2; NO = DEPTH // 2
    key = (SEQ, DEPTH)
    if key not in _BUILD_CACHE:
        _BUILD_CACHE[key] = build(SEQ, DEPTH)
    nc, stats = _BUILD_CACHE[key]
    state_conv = f(state_conv); state_gla = f(state_gla); cache_k = f(cache_k); cache_v = f(cache_v)
    shared = {"norm_mix": f(norm_mix), "norm_ffn": f(norm_ffn), "w_in_ab": f(w_in_ab), "conv_w": f(conv_w),
              "gk_w2": f(gla_gk_w2), "gk_b": f(gla_gk_b), "onorm": f(gla_onorm), "w_out_ab": f(w_out_ab),
              "w_qkv": f(w_qkv), "q_norm": f(q_norm), "k_norm": f(k_norm), "rel_bias": f(rel_bias),
              "w_o_att": f(w_o_att), "w_ffn_in": f(w_ffn_in), "w_ffn_out": f(w_ffn_out), "consts": _consts()}
    in_maps = []
    for c in range(8):
        b = c % 4
        m = dict(shared)
        m["xp"] = x_prompt[b]
        m["xs"] = x_sample[2 * b:2 * b + 2].reshape(128, D)
        m["sconv"] = np.ascontiguousarray(state_conv[:NE, 2 * b:2 * b + 2])
        m["sgla"] = np.ascontiguousarray(state_gla[:NE, 2 * b:2 * b + 2])
        if NO > 0:
            m["ck"] = np.ascontiguousarray(cache_k[:NO, 2 * b:2 * b + 2].reshape(NO, 2, 512, D))
            m["cv"] = np.ascontiguousarray(cache_v[:NO, 2 * b:2 * b + 2].reshape(NO, 2, 512, D))
        else:
            m["ck"] = np.zeros((1, 2, 512, D), np.float32); m["cv"] = np.zeros((1, 2, 512, D), np.float32)
        in_maps.append(m)
    ncores = _CFG.get("NCORES", 8)
    res = run_bass_kernel_spmd(nc, in_maps[:ncores], core_ids=list(range(ncores)))
    R = list(res.results)
    while len(R) < 4:
        R.append(R[0])
    y_prompt = np.stack([R[b]["yp"] for b in range(4)])
    y_sample = np.concatenate([R[b]["ys"].reshape(2, 64, D) for b in range(4)])
    conv_p = np.stack([R[b]["convp"] for b in range(4)], axis=1)
    gla_p = np.stack([R[b]["glap"] for b in range(4)], axis=1)
    k_p = np.stack([R[b]["kp"][:NO].reshape(NO, 512, 16, 64) for b in range(4)], axis=1)
    v_p = np.stack([R[b]["vp"][:NO].reshape(NO, 512, 16, 64) for b in range(4)], axis=1)
    conv_s = np.concatenate([R[b]["convs"] for b in range(4)], axis=1)
    gla_s = np.concatenate([R[b]["glas"] for b in range(4)], axis=1)
    k_s = np.concatenate([R[b]["ks"][:NO].reshape(NO, 2, 512, 16, 64) for b in range(4)], axis=1)
    v_s = np.concatenate([R[b]["vs"][:NO].reshape(NO, 2, 512, 16, 64) for b in range(4)], axis=1)
    return (y_prompt, y_sample, conv_p, gla_p, k_p, v_p, conv_s, gla_s, k_s, v_s)
```

```python
import numpy as np
import concourse.bass as bass
import concourse.mybir as mybir
from concourse.bass_utils import run_bass_kernel_spmd

F32 = mybir.dt.float32
BF16 = mybir.dt.bfloat16
I32 = mybir.dt.int32
AF = mybir.ActivationFunctionType
ALU = mybir.AluOpType

import os
DBG_DMA = bool(os.environ.get("DBG_DMA"))
SAME_ENGINE_SYNC = True
RAW_ONLY_SAME_ENGINE = bool(int(os.environ.get("RAW_ONLY", "0")))
EPOCH = 30000
N_DMA_SEMS = 8


class Unit:
    __slots__ = ("name", "w", "rs")

    def __init__(self, name):
        self.name = name
        self.w = None
        self.rs = []


class Rec:
    __slots__ = ("eng", "fn", "deps", "is_dma", "marked", "num", "sem_i", "cnt", "prev_cnt", "desc", "raw")


class V:
    __slots__ = ("ap", "units", "ro")

    def __init__(self, ap, units):
        self.ap = ap
        self.units = list(units)
        self.ro = ()


class Prog:
    ENGS = ("pe", "act", "dve", "pool", "sp")

    def __init__(self, nc):
        self.nc = nc
        self.q = {e: [] for e in self.ENGS}
        self.units = {}
        self.n_dma = 0
        self.n_dma_q = {}
        self.dma_recs = []

    def unit(self, key):
        u = self.units.get(key)
        if u is None:
            u = Unit(key)
            self.units[key] = u
        return u

    def op(self, eng, fn, r=(), w=(), dma=False):
        rec = Rec()
        rec.eng = eng
        rec.fn = fn
        rec.is_dma = dma
        rec.marked = False
        rec.num = 0
        deps = {}
        raw = set()
        for u in r:
            if u.w is not None:
                deps[id(u.w)] = u.w
                raw.add(id(u.w))
        rec.raw = raw
        for u in w:
            if u.w is not None:
                deps[id(u.w)] = u.w
            for x in u.rs:
                deps[id(x)] = x
        deps.pop(id(rec), None)
        rec.deps = list(deps.values())
        for u in r:
            u.rs.append(rec)
        for u in w:
            u.w = rec
            u.rs = []
        if dma:
            kq = self.n_dma_q.get(eng, 0)
            self.n_dma_q[eng] = kq + 1
            self.n_dma += 1
            base = {"sp": 0, "pool": 1, "act": 2}[eng] * N_DMA_SEMS
            rec.sem_i = base + kq % N_DMA_SEMS
            rec.cnt = 16 * (kq // N_DMA_SEMS + 1)
            self.dma_recs.append(rec)
        self.q[eng].append(rec)
        return rec

    def _units(self, views):
        out = []
        for v in views:
            if v is not None:
                out.extend(v.units)
        return out

    def mm(self, out, lhsT, rhs, start=True, stop=True, **kw):
        return self.op("pe", lambda e: e.matmul(out.ap, lhsT.ap, rhs.ap, start=start, stop=stop, **kw),
                       r=self._units([lhsT, rhs]), w=out.units)

    def transpose(self, out, in_, ident):
        return self.op("pe", lambda e: e.transpose(out.ap, in_.ap, ident.ap),
                       r=self._units([in_, ident]), w=out.units)

    def act(self, out, in_, func, bias=None, scale=None, eng="act"):
        kw = {}
        rs = [in_]
        if bias is not None:
            if isinstance(bias, V):
                kw["bias"] = bias.ap
                rs.append(bias)
            else:
                kw["bias"] = bias
        if scale is not None:
            if isinstance(scale, V):
                kw["scale"] = scale.ap
                rs.append(scale)
            else:
                kw["scale"] = scale
        return self.op(eng, lambda e: e.activation(out.ap, in_.ap, func, **kw),
                       r=self._units(rs), w=out.units)

    def tt(self, out, in0, in1, op, eng="dve"):
        return self.op(eng, lambda e: e.tensor_tensor(out.ap, in0.ap, in1.ap, op),
                       r=self._units([in0, in1]), w=out.units)

    def stt(self, out, in0, scalar, in1, op0, op1, eng="dve"):
        rs = [in0, in1]
        sc = scalar
        if isinstance(scalar, V):
            rs.append(scalar)
            sc = scalar.ap
        return self.op(eng, lambda e: e.scalar_tensor_tensor(out.ap, in0.ap, sc, in1.ap, op0, op1),
                       r=self._units(rs), w=out.units)

    def ts(self, out, in0, s1, s2, op0, op1=None, eng="dve"):
        rs = [in0]
        a1, a2 = s1, s2
        if isinstance(s1, V):
            rs.append(s1)
            a1 = s1.ap
        if isinstance(s2, V):
            rs.append(s2)
            a2 = s2.ap
        if op1 is None:
            return self.op(eng, lambda e: e.tensor_scalar(out.ap, in0.ap, a1, None, op0),
                           r=self._units(rs), w=out.units)
        return self.op(eng, lambda e: e.tensor_scalar(out.ap, in0.ap, a1, a2, op0, op1),
                       r=self._units(rs), w=out.units)

    def copy(self, out, in_, eng="dve"):
        if eng == "act":
            return self.op("act", lambda e: e.copy(out.ap, in_.ap), r=in_.units, w=out.units)
        return self.op(eng, lambda e: e.tensor_copy(out.ap, in_.ap), r=in_.units, w=out.units)

    def memset(self, out, val, eng="dve"):
        return self.op(eng, lambda e: e.memset(out.ap, val), r=(), w=out.units)

    def recip(self, out, in_):
        return self.op("dve", lambda e: e.reciprocal(out.ap, in_.ap), r=in_.units, w=out.units)

    def dma(self, out, in_, q="sp", **kw):
        return self.op(q, lambda e: e.dma_start(out=out.ap, in_=in_.ap, **kw),
                       r=in_.units, w=out.units, dma=True)

    def emit(self):
        nc = self.nc
        for e in self.ENGS:
            for rec in self.q[e]:
                keep = []
                for d in rec.deps:
                    if d.is_dma:
                        keep.append(d)
                    elif d.eng != rec.eng:
                        d.marked = True
                        keep.append(d)
                    elif d.eng != "pe" and (rec.is_dma or (SAME_ENGINE_SYNC and (not RAW_ONLY_SAME_ENGINE or id(d) in rec.raw))):
                        d.marked = True
                        keep.append(d)
                rec.deps = keep
        nmark = {}
        for e in self.ENGS:
            n = 0
            for rec in self.q[e]:
                if (not rec.is_dma) and rec.marked:
                    n += 1
                    rec.num = n
            nmark[e] = n
        esems = {e: [nc.alloc_semaphore(f"s_{e}_{k}") for k in range(nmark[e] // EPOCH + 1)]
                 for e in self.ENGS}
        dsems = [nc.alloc_semaphore(f"s_dma_{k}") for k in range(2 * N_DMA_SEMS)]
        stats = {}
        with nc.Block() as block:
            decos = {"pe": block.tensor, "act": block.scalar, "dve": block.vector,
                     "pool": block.gpsimd, "sp": block.sync}
            for e in self.ENGS:
                def body(eo, e=e):
                    seen = {}
                    seen_d = {}
                    nw = 0
                    for rec in self.q[e]:
                        if rec.is_dma and rec.cnt > 16:
                            if seen_d.get(rec.sem_i, 0) < rec.cnt - 16:
                                eo.wait_ge(dsems[rec.sem_i], rec.cnt - 16)
                                seen_d[rec.sem_i] = rec.cnt - 16
                                nw += 1
                        need_e = {}
                        need_d = {}
                        for d in rec.deps:
                            if d.is_dma:
                                if d.cnt > need_d.get(d.sem_i, 0):
                                    need_d[d.sem_i] = d.cnt
                            elif d.num > need_e.get(d.eng, 0):
                                need_e[d.eng] = d.num
                        for si, cnt in need_d.items():
                            if seen_d.get(si, 0) >= cnt:
                                continue
                            eo.wait_ge(dsems[si], cnt)
                            seen_d[si] = cnt
                            nw += 1
                        for de, num in need_e.items():
                            if seen.get(de, 0) >= num:
                                continue
                            ep = (num - 1) // EPOCH
                            eo.wait_ge(esems[de][ep], num - ep * EPOCH)
                            seen[de] = num
                            nw += 1
                        if rec.is_dma and DBG_DMA:
                            print("DMA", nc.get_next_instruction_name(), e, getattr(rec, "desc", None))
                        ins = rec.fn(eo)
                        if rec.is_dma:
                            ins.then_inc(dsems[rec.sem_i], 16)
                        elif rec.marked:
                            ep = (rec.num - 1) // EPOCH
                            ins.then_inc(esems[e][ep], 1)
                    if e == "sp":
                        last = {}
                        for rec in self.dma_recs:
                            last[rec.sem_i] = max(last.get(rec.sem_i, 0), rec.cnt)
                        for i, c in last.items():
                            if seen_d.get(i, 0) < c:
                                eo.wait_ge(dsems[i], c)
                    stats[e] = (len(self.q[e]), nw)
                decos[e](body)
        return stats


class Tile:
    def __init__(self, P, name, shape, dtype, psum=False):
        self.P = P
        self.name = name
        self.shape = shape
        nc = P.nc
        if psum:
            self.t = nc.alloc_psum_tensor(name, shape, dtype)
        else:
            self.t = nc.alloc_sbuf_tensor(name, shape, dtype)

    def v(self, idx=None, keys=("",)):
        ap = self.t[idx] if idx is not None else self.t[:]
        if not isinstance(keys, list):
            keys = (keys,)
        return V(ap, [self.P.unit((self.name, k)) for k in keys])


def dram_v(P, ap, key):
    return V(ap, [P.unit(("dram", key))])

D = 1024
DFF = 2816
NSLAB = DFF // 256
EPS = 1e-6
NEG = -30000.0
JT = 8
SBUF_LO = 16384 + 256
SBUF_HI = 229376 - 128

_CFG = {"DEPTH": 4}


def _consts():
    c = np.zeros((128, 7, 128), np.float32)
    idx = np.arange(128)
    c[:, 0] = np.eye(128)
    c[:, 1] = np.eye(128)[::-1]
    same = (idx[:, None] // 64) == (idx[None, :] // 64)
    c[:, 2] = (same & (idx[:, None] <= idx[None, :])).astype(np.float32)
    c[:, 3] = (same & (idx[:, None] > idx[None, :])).astype(np.float32) * (-1.0 / 16)
    c[:, 4] = 1.0 / 1024
    c[:, 5] = same.astype(np.float32) / 64.0
    c[:, 6] = 1.0 / 128
    return c.reshape(128, 7 * 128)


def build(SEQ, DEPTH):
    NJ = SEQ // (JT * 128)
    NTSEQ = SEQ // 128
    NE = (DEPTH + 1) // 2
    NO = DEPTH // 2
    TOKMAX = (JT + 1) * 128
    nc = bass.Bass("TRN2", target_bir_lowering=False)
    P = Prog(nc)

    def din(name, shape):
        return nc.dram_tensor(name, shape, F32, kind="ExternalInput").ap()

    def dout(name, shape):
        return nc.dram_tensor(name, shape, F32, kind="ExternalOutput").ap()

    xp = din("xp", [SEQ, D]); xs = din("xs", [128, D])
    sconv = din("sconv", [NE, 2, 2, 512]); sgla = din("sgla", [NE, 2, 4, 64, 128])
    ck = din("ck", [max(NO, 1), 2, 512, D]); cv = din("cv", [max(NO, 1), 2, 512, D])
    norm_mix = din("norm_mix", [4, D]); norm_ffn = din("norm_ffn", [4, D])
    w_in_ab = din("w_in_ab", [2, D, 3088]); conv_w = din("conv_w", [2, 3, 512])
    gk_w2 = din("gk_w2", [2, 16, 256]); gk_b = din("gk_b", [2, 256]); onorm = din("onorm", [2, 128])
    w_out_ab = din("w_out_ab", [2, D, D]); w_qkv = din("w_qkv", [2, D, 3 * D])
    q_norm = din("q_norm", [2, 64]); k_norm = din("k_norm", [2, 64]); rel_bias = din("rel_bias", [2, 16, 320])
    w_o_att = din("w_o_att", [2, D, D]); w_ffn_in = din("w_ffn_in", [4, D, 2 * DFF]); w_ffn_out = din("w_ffn_out", [4, DFF, D])
    consts_d = din("consts", [128, 7 * 128])
    yp = dout("yp", [SEQ, D]); ys = dout("ys", [128, D])
    convp = dout("convp", [NE, 2, 512]); glap = dout("glap", [NE, 4, 64, 128])
    kp = dout("kp", [max(NO, 1), 512, D]); vp = dout("vp", [max(NO, 1), 512, D])
    convs = dout("convs", [NE, 2, 2, 512]); glas = dout("glas", [NE, 2, 4, 64, 128])
    ks = dout("ks", [max(NO, 1), 2, 512, D]); vs = dout("vs", [max(NO, 1), 2, 512, D])
    ext_d = nc.dram_tensor("ext_scr", [max(NO, 1), 16, 768], F32)
    kscr = nc.dram_tensor("k_scr", [max(NO, 1), 128, 8, 512], BF16)
    vscr = nc.dram_tensor("v_scr", [max(NO, 1), 128, 4, 1040], BF16)
    hscr = nc.dram_tensor("h_scr", [max(NO, 1), 128, 16 * 5 * 128], BF16)

    def dv(ap, key):
        return V(ap, [P.unit(("dram", key))])

    cur = [SBUF_LO]

    class T2:
        def __init__(self, name, shape, dtype, base=None, ov=None):
            nbytes = int(np.prod(shape[1:])) * (4 if dtype in (F32, I32) else 2)
            nbytes = (nbytes + 63) // 64 * 64
            if base is None:
                off = cur[0]; cur[0] += nbytes
            else:
                off = base[0]; base[0] += nbytes
            assert off + nbytes <= SBUF_HI, (name, off, nbytes)
            self.t = nc.alloc_sbuf_tensor_at(name, shape, dtype, offset=off)
            self.name = name
            self.ov = ov

        def v(self, idx=None, keys=("",)):
            ap = self.t[idx] if idx is not None else self.t[:]
            if not isinstance(keys, list):
                keys = (keys,)
            vv = V(ap, [P.unit((self.name, k)) for k in keys])
            if self.ov is not None:
                vv.ro = [self.ov]
            return vv

        def w(self, ap, keys=("",)):
            if not isinstance(keys, list):
                keys = (keys,)
            vv = V(ap, [P.unit((self.name, k)) for k in keys])
            if self.ov is not None:
                vv.ro = [self.ov]
            return vv

    class Alias:
        def __init__(self, base, ap):
            self.base = base; self.ap0 = ap
        def v(self, idx=None, keys=("",)):
            return self.base.w(self.ap0 if idx is None else self.ap0[idx], keys)

    x = T2("x", [128, 8, TOKMAX], F32)
    cst = T2("cst", [128, 7, 128], F32)
    jbf = T2("jbf", [128, 128], BF16)
    ones1 = T2("ones1", [1, 128], F32)
    pv = T2("pv", [128, 128], F32); pvst = T2("pvst", [128, 128], F32)
    gw2b = T2("gw2b", [32, 2, 256], BF16)
    qnc = T2("qnc", [128, 2], F32)
    uhalo = T2("uhalo", [128, 2, 4, 2], F32)
    S_p = [T2(f"S_p{e}", [128, 2, 128], F32) for e in range(NE)]
    S_s = [[T2(f"S_s{e}_{s}", [128, 2, 128], F32) for s in range(2)] for e in range(NE)]
    dummy = T2("dummy", [128, 8], F32)
    wA = T2("wA", [128, 8, 3088], BF16)
    wB = T2("wB", [128, 8, D], BF16)
    R1 = cur[0]
    ov_unit = P.unit(("ov", "R1"))

    _b_io = [R1]
    xstages = [T2(f"xstage{i_}", [128, D], F32, base=_b_io, ov=ov_unit) for i_ in range(3)]
    xst_i = [0]
    ident = cst.v(np.s_[:, 0, :]); Jf = cst.v(np.s_[:, 1, :]); Mc = cst.v(np.s_[:, 2, :]); M2 = cst.v(np.s_[:, 3, :])
    onesD = cst.v(np.s_[:, 4, :]); ones64 = cst.v(np.s_[:, 5, :]); ones128 = cst.v(np.s_[:, 6, :])

    def phase_barrier():
        P.op("dve", lambda e: e.memset(dummy.t[:, 0:1], 0.0), r=(), w=[ov_unit, P.unit(("dummy", ""))])

    banks = [Tile(P, f"pb{i}", [128, 512], F32, psum=True) for i in range(8)]
    bank_i = [0]

    cur_pool = [None]
    pool_i = [0, 0]

    def ps():
        if cur_pool[0] is not None:
            p_ = cur_pool[0]
            b = banks[p_ * 4 + pool_i[p_] % 4]
            pool_i[p_] += 1
            return b
        b = banks[bank_i[0] % 8]
        bank_i[0] += 1
        return b

    def bv(b, idx=None):
        return V(b.t[idx] if idx is not None else b.t[:], [P.unit((b.name, ""))])

    def units_r(views):
        out = []
        for v_ in views:
            if v_ is None or not isinstance(v_, V):
                continue
            out.extend(v_.units)
            out.extend(getattr(v_, "ro", ()))
        return out

    def units_w(v_):
        return list(v_.units)

    def units_wr(v_):
        return list(getattr(v_, "ro", ()))

    def OP(eng, fn, ins, out, dma=False):
        return P.op(eng, fn, r=units_r(ins) + units_wr(out), w=units_w(out), dma=dma)

    def mm(out, lhsT, rhs, start=True, stop=True):
        return OP("pe", lambda e: e.matmul(out.ap, lhsT.ap, rhs.ap, start=start, stop=stop), [lhsT, rhs], out)

    def tr(out, in_, idv):
        return OP("pe", lambda e: e.transpose(out.ap, in_.ap, idv.ap), [in_, idv], out)

    def act(out, in_, func, bias=None, scale=None):
        kw = {}
        if bias is not None:
            kw["bias"] = bias.ap if isinstance(bias, V) else bias
        if scale is not None:
            kw["scale"] = scale.ap if isinstance(scale, V) else scale
        return OP("act", lambda e: e.activation(out.ap, in_.ap, func, **kw), [in_, bias, scale], out)

    def tt(out, a, b, op, eng="dve"):
        return OP(eng, lambda e: e.tensor_tensor(out.ap, a.ap, b.ap, op), [a, b], out)

    def stt(out, a, sc, b, op0, op1, eng="dve"):
        s_ = sc.ap if isinstance(sc, V) else sc
        return OP(eng, lambda e: e.scalar_tensor_tensor(out.ap, a.ap, s_, b.ap, op0, op1), [a, sc, b], out)

    def cp(out, in_, eng="dve"):
        if eng == "act":
            return OP("act", lambda e: e.copy(out.ap, in_.ap), [in_], out)
        return OP(eng, lambda e: e.tensor_copy(out.ap, in_.ap), [in_], out)

    def ms(out, val, eng="dve"):
        return OP(eng, lambda e: e.memset(out.ap, val), [], out)

    def rcp(out, in_):
        return OP("dve", lambda e: e.reciprocal(out.ap, in_.ap), [in_], out)

    def dma(out, in_, q="sp"):
        return OP(q, lambda e: e.dma_start(out=out.ap, in_=in_.ap), [in_], out, dma=True)

    dma(cst.v(), dv(consts_d.rearrange("p (a b) -> p a b", a=7), "consts"))
    cp(jbf.v(), Jf)
    ms(ones1.v(), 1.0)
    ms(pvst.v(), 0.0)
    dma(pvst.v(np.s_[0:32, :]), dv(norm_mix.rearrange("l (c p) -> (l c) p", p=128), "nm"))
    dma(pvst.v(np.s_[32:64, :]), dv(norm_ffn.rearrange("l (c p) -> (l c) p", p=128), "nf"))
    dma(pvst.v(np.s_[64:88, :]), dv(conv_w.rearrange("e i (c p) -> (e i c) p", p=128), "cw"))
    dma(pvst.v(np.s_[88:90, :]), dv(onorm, "onc"))
    for hh in range(2):
        dma(pvst.v(np.s_[90:92, hh * 64:(hh + 1) * 64]), dv(q_norm, "qn"))
        dma(pvst.v(np.s_[92:94, hh * 64:(hh + 1) * 64]), dv(k_norm, "kn"))
    _bt = ps()
    tr(bv(_bt, np.s_[:, 0:128]), pvst.v(), ident)
    cp(pv.v(), bv(_bt, np.s_[:, 0:128]))
    OP("dve", lambda e: e.tensor_scalar(qnc.t[:], pv.t[:, 90:92], 0.125, None, ALU.mult), [pv.v()], qnc.v())
    ms(gw2b.v(), 0.0)
    dma(gw2b.v(np.s_[0:16, :, :]), dv(gk_w2.rearrange("e r n -> r e n"), "gw2"), q="pool")
    dma(gw2b.v(np.s_[16:17, :, :]), dv(gk_b.rearrange("(o e) n -> o e n", o=1), "gb"), q="pool")
    ms(uhalo.v(), 0.0)
    for e_ in range(NE):
        ms(S_p[e_].v(), 0.0)
        for s in range(2):
            for hh in range(2):
                dma(S_s[e_][s].w(S_s[e_][s].t[hh * 64:(hh + 1) * 64, :, :]),
                    dv(sgla[e_, s].rearrange("(pr hh) k v -> hh k pr v", hh=2)[hh], "sgla"))
    for o in range(NO):
        dma(dv(ext_d.ap()[o, :, 0:64], ("ext", o)), dv(rel_bias[o, :, 0:64], "rb"))
        dma(dv(ext_d.ap()[o, :, 64:384], ("ext", o)), dv(rel_bias[o, :, 0:320], "rb"))
        dma(pvst.v(np.s_[0:16, 0:1]), dv(rel_bias[o, :, 319:320], "rb"))
        cp(pvst.v(np.s_[0:16, 1:128]), V(pvst.t[0:16, 0:1].to_broadcast([16, 127]), pvst.v().units))
        for q_ in range(3):
            dma(dv(ext_d.ap()[o, :, 384 + q_ * 128:512 + q_ * 128], ("ext", o)), pvst.v(np.s_[0:16, :]))

    def load_x_tile(tcol, src_rows):
        xstage = xstages[xst_i[0] % 3]; xst_i[0] += 1
        dma(xstage.v(), dv(src_rows, "xin"))
        for half in range(2):
            b = ps()
            for c in range(4):
                tr(bv(b, np.s_[:, c * 128:(c + 1) * 128]), xstage.v(np.s_[:, (half * 4 + c) * 128:(half * 4 + c + 1) * 128]), ident)
            cp(x.v(np.s_[:, half * 4:half * 4 + 4, tcol:tcol + 128], ("t", tcol)),
               V(b.t[:, :].rearrange("p (c n) -> p c n", c=4), bv(b).units), eng="act")

    def store_y_tile(tcol, dst_rows):
        xstage = xstages[xst_i[0] % 3]; xst_i[0] += 1
        for half in range(2):
            b = ps()
            for c in range(4):
                tr(bv(b, np.s_[:, c * 128:(c + 1) * 128]), x.v(np.s_[:, half * 4 + c, tcol:tcol + 128], ("t", tcol)), ident)
            cp(xstage.v(np.s_[:, half * 512:(half + 1) * 512]), bv(b), eng="act")
        dma(dv(dst_rows, "yout"), xstage.v())

    def rmsnorm(tcol, gcol_t, l, hT, sqt, rst):
        xt = x.v(np.s_[:, :, tcol:tcol + 128], ("t", tcol))
        act(sqt.v(), xt, AF.Square)
        b = ps()
        for c in range(8):
            mm(bv(b, np.s_[:, 0:128]), onesD, sqt.v(np.s_[:, c, :]), start=(c == 0), stop=(c == 7))
        act(rst.v(), bv(b, np.s_[:, 0:128]), AF.Ln, bias=EPS)
        act(rst.v(), rst.v(), AF.Exp, scale=-0.5)
        for c in range(8):
            stt(hT(c), x.v(np.s_[:, c, tcol:tcol + 128], ("t", tcol)), pv.w(pv.t[:, gcol_t + l * 8 + c:gcol_t + l * 8 + c + 1]), rst.v(), ALU.mult, ALU.mult)

    def add_to_x(tcol, c0, nch, b, n=128):
        xv = x.v(np.s_[:, c0:c0 + nch, tcol:tcol + n], ("t", tcol))
        tt(xv, xv, V(b.t[:, 0:nch * n].rearrange("p (c n) -> p c n", c=nch), bv(b).units), ALU.add)

    def load_w(dst, src_ap, key):
        nk = src_ap.shape[1]
        for kc in range(nk):
            dma(V(dst.ap[:, kc, :], dst.units + list(dst.ro)) if False else _sub(dst, kc), dv(src_ap[:, kc, :], key), q="pool")

    def _sub(dst, kc):
        vv = V(dst.ap[:, kc, :], dst.units)
        vv.ro = dst.ro
        return vv

    _ebase = [R1]

    def even_bufs(par):
        base = _ebase
        B = {}
        def mk(name, shape, dt):
            B[name] = T2(f"e{par}_" + name, shape, dt, base=base, ov=ov_unit)
        mk("hT", [128, 8, 128], BF16); mk("sqt", [128, 8, 128], F32); mk("rst", [128, 128], F32)
        mk("cg", [128, 4, 128], F32); mk("ub", [128, 4, 130], F32); mk("ubA", [128, 4, 66], F32); mk("ubB", [128, 4, 66], F32)
        mk("yc", [128, 4, 128], F32); mk("tmp", [128, 4, 128], F32)
        mk("mixT", [128, 8, 128], BF16); mk("sg", [128, 4, 128], F32); mk("gkT", [32, 128], BF16)
        mk("ktok", [128, 256], F32); mk("vbf", [128, 512], BF16)
        mk("e1", [128, 256], F32); mk("sp", [128, 256], F32); mk("erb", [128, 256], F32); mk("kdec", [128, 2, 256], BF16)
        mk("ebT", [128, 2, 128], F32); mk("enbT", [128, 2, 128], F32)
        mk("qe", [128, 2, 2, 128], BF16); mk("ke", [128, 2, 128], BF16)
        mk("attm", [128, 4, 128], BF16); mk("Sb0", [128, 2, 128], BF16); mk("Sb1", [128, 2, 128], BF16)
        mk("sq", [128, 512], F32); mk("rs2", [128, 512], F32); mk("y1", [128, 4, 128], F32)
        mk("cstg", [2, 512], F32); mk("qkraw", [128, 4, 128], F32)
        return B

    def odd_bufs():
        base = [R1]
        B = {}
        def mk(name, shape, dt):
            B[name] = T2("o_" + name, shape, dt, base=base, ov=ov_unit)
        mk("hT", [128, 8, 128], BF16); mk("rst", [128, 128], F32)
        mk("qT", [128, 2, 8, 128], BF16); mk("kn", [128, 8, 128], F32); mk("kown", [128, 8, 128], BF16)
        mk("raw", [128, 512], F32); mk("sq", [128, 512], F32); mk("rs2", [128, 512], F32)
        mk("kring", [128, 8, 8 * 128], BF16); mk("vring", [128, 8, 16 * 65], BF16); mk("vown", [128, 16 * 65], BF16)
        mk("H", [128, 16, 5, 128], BF16); mk("pT0", [128, 4, 5, 128], BF16); mk("pT1", [128, 4, 5, 128], BF16)
        mk("atok", [128, D], F32); mk("rc", [128, 16], F32); mk("attT", [128, 8, 128], BF16)
        mk("stage", [128, D], F32)
        B["ckst"] = B["stage"]
        B["sqt"] = Alias(B["atok"], B["atok"].t[:, :].rearrange("p (c n) -> p c n", c=8))
        return B

    def ffn_bufs():
        base = [R1]
        B = {}
        def mk(name, shape, dt):
            B[name] = T2("f_" + name, shape, dt, base=base, ov=ov_unit)
        mk("hTall", [128, 8, TOKMAX], BF16)
        for i in range(3):
            mk(f"sqt{i}", [128, 8, 128], F32); mk(f"rst{i}", [128, 128], F32)
        for i in range(2):
            mk(f"win{i}", [128, 8, 512], BF16); mk(f"wout{i}", [128, 2, D], BF16)
            mk(f"a{i}", [128, 2, 512], BF16); mk(f"sgf{i}", [128, 512], F32)
        return B

    EBS = [even_bufs(0), even_bufs(1)]; OB = odd_bufs(); FB = ffn_bufs()

    def even_gen(e, l, tcol, sample, want_state, par):
        B = EBS[par]
        hT = lambda c: B["hT"].v(np.s_[:, c, :])
        rmsnorm(tcol, 0, l, hT, B["sqt"], B["rst"])
        hTa = B["hT"]
        yield None

        def fm4(col0, nch=4):
            b = ps()
            for c in range(nch):
                for kc in range(8):
                    mm(bv(b, np.s_[:, c * 128:(c + 1) * 128]), wA.v(np.s_[:, kc, col0 + c * 128:col0 + (c + 1) * 128]), hTa.v(np.s_[:, kc, :]),
                       start=(kc == 0), stop=(kc == 7))
            return b

        def b3(b, nch=4):
            return V(b.t[:, 0:nch * 128].rearrange("p (c n) -> p c n", c=nch), bv(b).units)

        def conv_out(ub, col, dst, key):
            _b = ps()
            for c_ in range(4):
                tr(bv(_b, np.s_[0:2, c_ * 128:(c_ + 1) * 128]), ub.v(np.s_[:, c_, col:col + 2]), ident)
            cp(B["cstg"].v(), bv(_b, np.s_[0:2, :]), eng="act")
            dma(dv(dst, key), B["cstg"].v())

        bcg = fm4(0)
        cp(B["cg"].v(), b3(bcg), eng="act")
        yield None
        bhc = fm4(1024)
        if not sample:
            cp(B["ub"].v(np.s_[:, :, 0:2]), uhalo.v(np.s_[:, e, :, :]), eng="dve")
            tt(B["ub"].v(np.s_[:, :, 2:130]), b3(bhc), B["cg"].v(), ALU.mult)
            segs = [(B["ub"], 128, 0)]
        else:
            for s, ub in enumerate((B["ubA"], B["ubB"])):
                dma(B["cstg"].v(), dv(sconv[e, s], "sconv"))
                _b = ps()
                for c_ in range(4):
                    tr(bv(_b, np.s_[:, c_ * 2:c_ * 2 + 2]), B["cstg"].v(np.s_[:, c_ * 128:(c_ + 1) * 128]), cst.w(cst.t[0:2, 0, 0:2]))
                cp(ub.v(np.s_[:, :, 0:2]), V(_b.t[:, 0:8].rearrange("p (c r) -> p c r", c=4), bv(_b).units), eng="act")
                tt(ub.v(np.s_[:, :, 2:66]), V(bhc.t[:, :].rearrange("p (c n) -> p c n", c=4)[:, :, s * 64:(s + 1) * 64], bv(bhc).units),
                   B["cg"].v(np.s_[:, :, s * 64:(s + 1) * 64]), ALU.mult)
            segs = [(B["ubA"], 64, 0), (B["ubB"], 64, 64)]
        for ub, n, off in segs:
            yv = B["yc"].v(np.s_[:, :, off:off + n])
            tv = B["tmp"].v(np.s_[:, :, off:off + n])
            wv = lambda i: V(pv.t[:, 64 + (e * 3 + i) * 4:64 + (e * 3 + i) * 4 + 4].unsqueeze(2).to_broadcast([128, 4, n]), pv.v().units)
            tt(yv, ub.v(np.s_[:, :, 2:2 + n]), wv(2), ALU.mult, eng="dve")
            tt(tv, ub.v(np.s_[:, :, 1:1 + n]), wv(1), ALU.mult, eng="dve")
            tt(yv, yv, tv, ALU.add, eng="dve")
            tt(tv, ub.v(np.s_[:, :, 0:n]), wv(0), ALU.mult, eng="dve")
            tt(yv, yv, tv, ALU.add, eng="dve")
        if not sample:
            cp(uhalo.v(np.s_[:, e, :, :]), B["ub"].v(np.s_[:, :, 128:130]), eng="dve")
            if want_state:
                conv_out(B["ub"], 128, convp[e], "convp")
        else:
            for s, ub in enumerate((B["ubA"], B["ubB"])):
                conv_out(ub, 64, convs[e, s], "convs")
        yield None
        bbg = fm4(512)
        tt(B["mixT"].v(np.s_[:, 0:4, :]), b3(bbg), B["yc"].v(), ALU.mult)
        yield None
        bgk = ps()
        for kc in range(8):
            mm(bv(bgk, np.s_[0:16, 0:128]), wA.v(np.s_[:, kc, 3072:3088]), hTa.v(np.s_[:, kc, :]), start=(kc == 0), stop=(kc == 7))
        ms(B["gkT"].v(), 1.0)
        cp(B["gkT"].v(np.s_[0:16, :]), bv(bgk, np.s_[0:16, 0:128]), eng="act")
        bk = ps(); bvv = ps()
        for kc in range(8):
            mm(bv(bk, np.s_[:, 0:256]), hTa.v(np.s_[:, kc, :]), wA.v(np.s_[:, kc, 1792:2048]), start=(kc == 0), stop=(kc == 7))
        for kc in range(8):
            mm(bv(bvv), hTa.v(np.s_[:, kc, :]), wA.v(np.s_[:, kc, 2048:2560]), start=(kc == 0), stop=(kc == 7))
        cp(B["ktok"].v(), bv(bk, np.s_[:, 0:256]), eng="act")
        cp(B["vbf"].v(), bv(bvv), eng="act")
        yield None
        bqk = fm4(1536)
        cp(B["qkraw"].v(), b3(bqk), eng="act")
        yield None
        bg = fm4(2560)
        act(B["sg"].v(), b3(bg), AF.Silu)
        yield "half"

        bz = ps()
        mm(bv(bz, np.s_[:, 0:256]), B["gkT"].v(), gw2b.v(np.s_[:, e, :]))
        act(B["e1"].v(), bv(bz, np.s_[:, 0:256]), AF.Exp, scale=-1.0)
        act(B["sp"].v(), B["e1"].v(), AF.Ln, bias=1.0)
        yield None
        brb = ps()
        mm(bv(brb, np.s_[:, 0:256]), M2, B["sp"].v())
        bbT = ps()
        for p_ in range(2):
            mm(bv(bbT, np.s_[:, p_ * 128:(p_ + 1) * 128]), B["sp"].v(np.s_[:, p_ * 128:(p_ + 1) * 128]), Mc)
        act(B["erb"].v(), bv(brb, np.s_[:, 0:256]), AF.Exp)
        for cc_ in range(2):
            oc_ = 1 - cc_
            ms(B["kdec"].v(np.s_[oc_ * 64:(oc_ + 1) * 64, cc_, :]), 0.0)
            tt(B["kdec"].v(np.s_[cc_ * 64:(cc_ + 1) * 64, cc_, :]), B["ktok"].v(np.s_[cc_ * 64:(cc_ + 1) * 64, :]), B["erb"].v(np.s_[cc_ * 64:(cc_ + 1) * 64, :]), ALU.mult)
        bT3 = V(bbT.t[:, 0:256].rearrange("p (c n) -> p c n", c=2), bv(bbT).units)
        act(B["ebT"].v(), bT3, AF.Exp, scale=-1.0 / 16)
        act(B["enbT"].v(), bT3, AF.Exp, scale=1.0 / 16)
        qk3 = B["qkraw"].v()
        for hh_ in range(2):
            oh_ = 1 - hh_
            ms(B["qe"].v(np.s_[oh_ * 64:(oh_ + 1) * 64, hh_, :, :]), 0.0)
            stt(B["qe"].v(np.s_[hh_ * 64:(hh_ + 1) * 64, hh_, :, :]), B["qkraw"].v(np.s_[hh_ * 64:(hh_ + 1) * 64, 0:2, :]), 0.125,
                B["ebT"].v(np.s_[hh_ * 64:(hh_ + 1) * 64, :, :]), ALU.mult, ALU.mult)
        tt(B["ke"].v(), B["qkraw"].v(np.s_[:, 2:4, :]), B["enbT"].v(), ALU.mult)
        yield None
        batt = ps()
        for h in range(4):
            pr, hh = h // 2, h % 2
            mm(bv(batt, np.s_[:, h * 128:(h + 1) * 128]), B["ke"].v(np.s_[:, pr, :]), B["qe"].v(np.s_[:, hh, pr, :]))
        tt(B["attm"].v(), b3(batt), V(cst.t[:, 2, :].unsqueeze(1).to_broadcast([128, 4, 128]), cst.v().units), ALU.mult)
        yield "need_state"
        if not sample:
            Sc = [S_p[e], S_p[e]]
        else:
            Sc = [S_s[e][0], S_s[e][1]]
        Sb = [B["Sb0"], B["Sb1"]]

        def upd(cc):
            S = Sc[cc]
            bs = ps()
            for pr in range(2):
                mm(bv(bs, np.s_[:, pr * 256:(pr + 1) * 256]), B["kdec"].v(np.s_[:, cc, pr * 128:(pr + 1) * 128]),
                   B["vbf"].v(np.s_[:, pr * 256:(pr + 1) * 256]))
            for pr in range(2):
                for hh in range(2):
                    sv = S.v(np.s_[hh * 64:(hh + 1) * 64, pr, :])
                    stt(sv, sv, B["ebT"].v(np.s_[hh * 64:(hh + 1) * 64, pr, cc * 64 + 63:cc * 64 + 64]),
                        bv(bs, np.s_[hh * 64:(hh + 1) * 64, pr * 256 + hh * 128:pr * 256 + (hh + 1) * 128]), ALU.mult, ALU.add)

        if not sample:
            cp(Sb[0].v(), Sc[0].v(), eng="act")
            upd(0)
            cp(Sb[1].v(), Sc[1].v(), eng="act")
            upd(1)
        else:
            cp(Sb[0].v(), Sc[0].v(), eng="act")
            cp(Sb[1].v(), Sc[1].v(), eng="act")
            upd(0)
            upd(1)
        if want_state:
            if not sample:
                for hh in range(2):
                    dma(dv(glap[e].rearrange("(pr hh) k v -> hh k pr v", hh=2)[hh], "glap"), S_p[e].v(np.s_[hh * 64:(hh + 1) * 64, :, :]))
            else:
                for s in range(2):
                    for hh in range(2):
                        dma(dv(glas[e, s].rearrange("(pr hh) k v -> hh k pr v", hh=2)[hh], "glas"), S_s[e][s].v(np.s_[hh * 64:(hh + 1) * 64, :, :]))
        yield "state_done"
        bo = ps()
        for h in range(4):
            pr, hh = h // 2, h % 2
            for cc in range(2):
                ov_ = bv(bo, np.s_[:, h * 128 + cc * 64:h * 128 + (cc + 1) * 64])
                mm(ov_, B["vbf"].v(np.s_[:, h * 128:(h + 1) * 128]), B["attm"].v(np.s_[:, h, cc * 64:(cc + 1) * 64]), start=True, stop=False)
                mm(ov_, Sb[cc].v(np.s_[:, pr, :]), B["qe"].v(np.s_[:, hh, pr, cc * 64:(cc + 1) * 64]), start=False, stop=True)
        act(B["sq"].v(), bv(bo), AF.Square)
        yield None
        bm = ps()
        mm(bv(bm), ones128, B["sq"].v())
        act(B["rs2"].v(), bv(bm), AF.Ln, bias=EPS)
        act(B["rs2"].v(), B["rs2"].v(), AF.Exp, scale=-0.5)
        stt(B["y1"].v(), b3(bo), pv.w(pv.t[:, 88 + e:89 + e]), V(B["rs2"].t[:, :].rearrange("p (c n) -> p c n", c=4), B["rs2"].v().units), ALU.mult, ALU.mult)
        tt(B["mixT"].v(np.s_[:, 4:8, :]), B["y1"].v(), B["sg"].v(), ALU.mult, eng="dve")
        yield None
        for half in range(2):
            b = ps()
            for c in range(4):
                dc = half * 4 + c
                for mc in range(8):
                    mm(bv(b, np.s_[:, c * 128:(c + 1) * 128]), wB.v(np.s_[:, mc, dc * 128:(dc + 1) * 128]), B["mixT"].v(np.s_[:, mc, :]),
                       start=(mc == 0), stop=(mc == 7))
            add_to_x(tcol, half * 4, 4, b)
            yield None

    def run_pipelined(specs, genf):
        def mk(i, spec):
            return {"g": genf(par=i % 2, **spec), "par": i % 2, "half": False, "sdone": False, "done": False, "wait": False}

        def step(st):
            cur_pool[0] = st["par"]
            r = next(st["g"], "DONE")
            cur_pool[0] = None
            if r == "DONE":
                st["done"] = True; st["sdone"] = True; st["half"] = True
            elif r == "half":
                st["half"] = True
            elif r == "need_state":
                st["wait"] = True
            elif r == "state_done":
                st["sdone"] = True

        old = None
        for i, spec in enumerate(specs):
            new = mk(i, spec)
            while True:
                if new["wait"] and (old is None or old["sdone"]):
                    new["wait"] = False
                progressed = False
                if not new["done"] and not new["wait"] and not (new["half"] and old is None and False):
                    step(new); progressed = True
                if old is not None and not old["done"]:
                    step(old); progressed = True
                if old is not None and old["done"]:
                    old = None
                if new["done"] or (new["half"] and old is None):
                    break
                assert progressed
            old = None if new["done"] else new
        while old is not None and not old["done"]:
            old["wait"] = False
            step(old)

    def attend(o, qlo, nq, kblk, vblk):
        B = OB
        obanks = [banks[0], banks[1], banks[2]]
        sB = banks[3]

        def scores_a(hg):
            pT = B[f"pT{hg % 2}"]
            for hi in range(4):
                h = hg * 4 + hi
                pr, hh = h // 2, h % 2
                sA = banks[4 + hi]
                qv = B["qT"].v(np.s_[:, hh, pr, qlo:qlo + nq])
                if nq == 128:
                    mm(bv(sA), jbf.v(), B["H"].w(B["H"].t[:, h, 0:4, :].rearrange("p a b -> p (a b)")), start=True, stop=False)
                    for kb in range(1, 5):
                        sl = 4 - kb
                        mm(bv(sA, np.s_[:, sl * 128:sl * 128 + nq]), kblk(kb, h), qv, start=False, stop=(kb == 4))
                else:
                    for kb in range(1, 5):
                        sl = 4 - kb
                        tgt = bv(sA, np.s_[:, sl * 128:sl * 128 + nq])
                        mm(tgt, jbf.v(), B["H"].v(np.s_[:, h, sl, qlo:qlo + nq]), start=True, stop=False)
                        mm(tgt, kblk(kb, h), qv, start=False, stop=True)
                act(pT.w(pT.t[:, hi, 0:4, 0:nq]),
                    V(sA.t[:, :].rearrange("p (c n) -> p c n", c=4)[:, :, 0:nq], bv(sA).units), AF.Exp)

        def scores_b(hg):
            pT = B[f"pT{hg % 2}"]
            for hi in range(4):
                h = hg * 4 + hi
                pr, hh = h // 2, h % 2
                qv = B["qT"].v(np.s_[:, hh, pr, qlo:qlo + nq])
                tgt = bv(sB, np.s_[:, hi * 128:hi * 128 + nq])
                mm(tgt, jbf.v(), B["H"].v(np.s_[:, h, 4, qlo:qlo + nq]), start=True, stop=False)
                mm(tgt, kblk(0, h), qv, start=False, stop=True)
            act(pT.w(pT.t[:, :, 4, 0:nq]), V(sB.t[:, :].rearrange("p (c n) -> p c n", c=4)[:, :, 0:nq], bv(sB).units), AF.Exp)

        def pvs(hg):
            pT = B[f"pT{hg % 2}"]
            for hi in range(4):
                h = hg * 4 + hi
                ob = obanks[h // 7]
                col = (h % 7) * 65
                for kb in range(5):
                    mm(bv(ob, np.s_[0:nq, col:col + 65]), pT.w(pT.t[:, hi, 4 - kb, 0:nq]), vblk(kb, h), start=(kb == 0), stop=(kb == 4))

        scores_a(0); scores_b(0)
        for hg in range(1, 4):
            scores_a(hg)
            pvs(hg - 1)
            scores_b(hg)
        pvs(3)
        for bi, ob in enumerate(obanks):
            nh = 7 if bi < 2 else 2
            h0 = bi * 7
            o3 = V(ob.t[0:nq, 0:nh * 65].rearrange("p (h n) -> p h n", h=nh), bv(ob).units)
            rcp(B["rc"].w(B["rc"].t[0:nq, h0:h0 + nh]), V(o3.ap[:, :, 64], o3.units))
            tt(B["atok"].w(B["atok"].t[0:nq, h0 * 64:(h0 + nh) * 64].rearrange("p (h n) -> p h n", h=nh)),
               V(o3.ap[:, :, 0:64], o3.units),
               B["rc"].w(B["rc"].t[0:nq, h0:h0 + nh].unsqueeze(2).to_broadcast([nq, nh, 64])), ALU.mult)
        for half in range(2):
            b = banks[4 + half]
            for c in range(4):
                fc = half * 4 + c
                tr(bv(b, np.s_[:, c * 128:c * 128 + nq]), B["atok"].w(B["atok"].t[0:nq, fc * 128:(fc + 1) * 128]), cst.w(cst.t[0:nq, 0, 0:nq]))
            cp(B["attT"].w(B["attT"].t[:, half * 4:half * 4 + 4, qlo:qlo + nq]),
               V(b.t[:, :].rearrange("p (c n) -> p c n", c=4)[:, :, 0:nq], bv(b).units), eng="act")

    def build_hscr(o):
        B = OB
        for h0 in range(16):
            src = bass.AP(ext_d, o * 16 * 768 + h0 * 768, [[1, 128], [128, 5], [1, 128]])
            dma(B["H"].w(B["H"].t[:, h0, :, :]), dv(src, ("ext", o)), q="pool")
        ms(B["H"].w(B["H"].t[64:128, :, 4, 64:128]), NEG)
        ms(B["H"].w(B["H"].t[0:64, :, 0, 0:64]), NEG)
        dma(dv(hscr.ap()[o], ("hscr", o)), B["H"].w(B["H"].t[:, :, :, :].rearrange("p a b c -> p (a b c)")))

    def odd_phase_begin(o, j):
        B = OB
        dma(B["H"].w(B["H"].t[:, :, :, :].rearrange("p a b c -> p (a b c)")), dv(hscr.ap()[o], ("hscr", o)))
        if j == 0:
            ms(B["kring"].w(B["kring"].t[:, :, 512:1024], [("s", s) for s in range(4, 8)]), 0.0)
            ms(B["vring"].w(B["vring"].t[:, 4:8, :], [("s", s) for s in range(4, 8)]), 0.0)
        else:
            dma(B["kring"].w(B["kring"].t[:, :, 512:1024], [("s", s) for s in range(4, 8)]), dv(kscr.ap()[o], ("kscr", o)))
            dma(B["vring"].w(B["vring"].t[:, 4:8, :], [("s", s) for s in range(4, 8)]), dv(vscr.ap()[o], ("vscr", o)))

    def odd_phase_end(o, j, last):
        B = OB
        if not last:
            dma(dv(kscr.ap()[o], ("kscr", o)), B["kring"].w(B["kring"].t[:, :, 512:1024], [("s", s) for s in range(4, 8)]))
            dma(dv(vscr.ap()[o], ("vscr", o)), B["vring"].w(B["vring"].t[:, 4:8, :], [("s", s) for s in range(4, 8)]))

    def odd_tile(o, l, tcol, t, sample, out_rows):
        B = OB
        hT = lambda c: B["hT"].v(np.s_[:, c, :])
        rmsnorm(tcol, 0, l, hT, B["sqt"], B["rst"])
        hTa = B["hT"]
        slot = t if not sample else None
        for which in range(2):
            for half in range(2):
                b = ps()
                for c in range(4):
                    col0 = which * D + (half * 4 + c) * 128
                    for kc in range(8):
                        mm(bv(b, np.s_[:, c * 128:(c + 1) * 128]), wA.v(np.s_[:, kc, col0:col0 + 128]), hTa.v(np.s_[:, kc, :]),
                           start=(kc == 0), stop=(kc == 7))
                cp(B["raw"].v(), bv(b), eng="act")
                act(B["sq"].v(), bv(b), AF.Square)
                bm = ps()
                mm(bv(bm), ones64, B["sq"].v())
                act(B["rs2"].v(), bv(bm), AF.Ln, bias=EPS)
                act(B["rs2"].v(), B["rs2"].v(), AF.Exp, scale=-0.5)
                r3 = lambda T_: V(T_.t[:, :].rearrange("p (c n) -> p c n", c=4), T_.v().units + [ov_unit] * 0)
                if which == 0:
                    for hh_ in range(2):
                        oh_ = 1 - hh_
                        ms(B["qT"].v(np.s_[oh_ * 64:(oh_ + 1) * 64, hh_, half * 4:half * 4 + 4, :]), 0.0)
                        stt(B["qT"].v(np.s_[hh_ * 64:(hh_ + 1) * 64, hh_, half * 4:half * 4 + 4, :]),
                            B["raw"].w(r3(B["raw"]).ap[hh_ * 64:(hh_ + 1) * 64]), qnc.w(qnc.t[hh_ * 64:(hh_ + 1) * 64, o:o + 1]),
                            B["rs2"].w(r3(B["rs2"]).ap[hh_ * 64:(hh_ + 1) * 64]), ALU.mult, ALU.mult)
                else:
                    stt(B["kn"].v(np.s_[:, half * 4:half * 4 + 4, :]), B["raw"].w(r3(B["raw"]).ap), pv.w(pv.t[:, 92 + o:93 + o]),
                        B["rs2"].w(r3(B["rs2"]).ap), ALU.mult, ALU.mult)
        if not sample:
            cp(B["kring"].w(B["kring"].t[:, :, slot * 128:(slot + 1) * 128], ("s", slot)), B["kn"].v(), eng="act")
        else:
            cp(B["kown"].v(), B["kn"].v(), eng="act")
        vb = [ps(), ps()]
        for half in range(2):
            for kc in range(8):
                mm(bv(vb[half]), hTa.v(np.s_[:, kc, :]), wA.v(np.s_[:, kc, 2 * D + half * 512:2 * D + (half + 1) * 512]), start=(kc == 0), stop=(kc == 7))
        for half in range(2):
            src = V(vb[half].t[:, :].rearrange("p (h n) -> p h n", h=8), bv(vb[half]).units)
            if not sample:
                dst = B["vring"].w(B["vring"].t[:, slot, :].rearrange("p (h n) -> p h n", h=16)[:, half * 8:(half + 1) * 8, 0:64], ("s", slot))
            else:
                dst = B["vown"].w(B["vown"].t[:, :].rearrange("p (h n) -> p h n", h=16)[:, half * 8:(half + 1) * 8, 0:64])
            cp(dst, src, eng="act")
        if not sample:
            ms(B["vring"].w(B["vring"].t[:, slot, :].rearrange("p (h n) -> p h n", h=16)[:, :, 64:65], ("s", slot)), 1.0)
        else:
            ms(B["vown"].w(B["vown"].t[:, :].rearrange("p (h n) -> p h n", h=16)[:, :, 64:65]), 1.0)
        if out_rows is not None or sample:
            for half in range(2):
                cp(B["stage"].v(np.s_[:, half * 512:(half + 1) * 512]), bv(vb[half]), eng="act")
            if not sample:
                dma(dv(vp[o, out_rows:out_rows + 128, :], "vp"), B["stage"].v())
            else:
                for s in range(2):
                    dma(dv(vs[o, s, 448:512, :], "vs"), B["stage"].v(np.s_[s * 64:(s + 1) * 64, :]))
            for half in range(2):
                b = ps()
                for c in range(4):
                    tr(bv(b, np.s_[:, c * 128:(c + 1) * 128]), B["kn"].v(np.s_[:, half * 4 + c, :]), ident)
                cp(B["stage"].v(np.s_[:, half * 512:(half + 1) * 512]), bv(b), eng="act")
            if not sample:
                dma(dv(kp[o, out_rows:out_rows + 128, :], "kp"), B["stage"].v())
            else:
                for s in range(2):
                    dma(dv(ks[o, s, 448:512, :], "ks"), B["stage"].v(np.s_[s * 64:(s + 1) * 64, :]))
        if not sample:
            def kblk(kb, h):
                s_ = (t - 4 + kb) % 8
                return B["kring"].w(B["kring"].t[:, h // 2, s_ * 128:(s_ + 1) * 128], ("s", s_))
            def vblk(kb, h):
                s_ = (t - 4 + kb) % 8
                return B["vring"].w(B["vring"].t[:, s_, h * 65:(h + 1) * 65], ("s", s_))
            attend(o, 0, 128, kblk, vblk)
        else:
            for s in range(2):
                dma(dv(ks[o, s, 0:448, :], "ks"), dv(ck[o, s, 64:512, :], "ck"))
                dma(dv(vs[o, s, 0:448, :], "vs"), dv(cv[o, s, 64:512, :], "cv"))
                shift = 64 * s
                for m in range(5):
                    r_lo = max(0, 128 * m - shift); r_hi = min(512, 128 * m + 128 - shift)
                    kslot = B["kring"].w(B["kring"].t[:, :, m * 128:(m + 1) * 128], ("s", m))
                    vslot = B["vring"].w(B["vring"].t[:, m, :], ("s", m))
                    ms(vslot, 0.0)
                    if r_hi > r_lo:
                        p_lo = r_lo + shift - 128 * m
                        n = r_hi - r_lo
                        ms(B["ckst"].v(), 0.0)
                        dma(B["ckst"].w(B["ckst"].t[p_lo:p_lo + n, :]), dv(ck[o, s, r_lo:r_hi, :], "ck"))
                        for half in range(2):
                            b = ps()
                            for c in range(4):
                                tr(bv(b, np.s_[:, c * 128:(c + 1) * 128]), B["ckst"].v(np.s_[:, (half * 4 + c) * 128:(half * 4 + c + 1) * 128]), ident)
                            cp(B["kring"].w(B["kring"].t[:, half * 4:half * 4 + 4, m * 128:(m + 1) * 128], ("s", m)),
                               V(b.t[:, :].rearrange("p (c n) -> p c n", c=4), bv(b).units), eng="act")
                        dma(B["vring"].w(B["vring"].t[p_lo:p_lo + n, m, :].rearrange("p (h n) -> p h n", h=16)[:, :, 0:64], ("s", m)),
                            dv(cv[o, s, r_lo:r_hi, :].rearrange("r (h n) -> r h n", h=16), "cv"), q="pool")
                        ms(B["vring"].w(B["vring"].t[:, m, :].rearrange("p (h n) -> p h n", h=16)[:, :, 64:65], ("s", m)), 1.0)
                    else:
                        ms(kslot, 0.0)
                cp(B["kring"].w(B["kring"].t[:, :, 4 * 128 + shift:4 * 128 + shift + 64], ("s", 4)), B["kown"].v(np.s_[:, :, shift:shift + 64]), eng="act")
                cp(B["vring"].w(B["vring"].t[shift:shift + 64, 4, :], ("s", 4)), B["vown"].v(np.s_[shift:shift + 64, :]), eng="dve")
                def kblk(kb, h):
                    return B["kring"].w(B["kring"].t[:, h // 2, kb * 128:(kb + 1) * 128], ("s", kb))
                def vblk(kb, h):
                    return B["vring"].w(B["vring"].t[:, kb, h * 65:(h + 1) * 65], ("s", kb))
                attend(o, shift, 64, kblk, vblk)
        for half in range(2):
            b = ps()
            for c in range(4):
                dc = half * 4 + c
                for mc in range(8):
                    mm(bv(b, np.s_[:, c * 128:(c + 1) * 128]), wB.v(np.s_[:, mc, dc * 128:(dc + 1) * 128]), B["attT"].v(np.s_[:, mc, :]),
                       start=(mc == 0), stop=(mc == 7))
            add_to_x(tcol, half * 4, 4, b)

    def ffn_phase(l, ntiles):
        B = FB
        ntok = ntiles * 128

        def load_slab(s):
            i = s % 2
            dma(B[f"win{i}"].v(np.s_[:, :, 0:256], "g"), dv(w_ffn_in[l][:, s * 256:(s + 1) * 256].rearrange("(kc p) n -> p kc n", p=128), "wfi"), q="pool")
            dma(B[f"win{i}"].v(np.s_[:, :, 256:512], "u"), dv(w_ffn_in[l][:, DFF + s * 256:DFF + (s + 1) * 256].rearrange("(kc p) n -> p kc n", p=128), "wfi"), q="pool")
            dma(B[f"wout{i}"].v(), dv(w_ffn_out[l][s * 256:(s + 1) * 256, :].rearrange("(fc p) n -> p fc n", p=128), "wfo"), q="pool")

        load_slab(0)
        for t in range(ntiles):
            hT = lambda c, t=t: B["hTall"].v(np.s_[:, c, t * 128:(t + 1) * 128], ("t", t))
            rmsnorm(t * 128, 32, l, hT, B[f"sqt{t % 3}"], B[f"rst{t % 3}"])
        chunks = []
        c0 = 0
        while c0 < ntok:
            n = min(512, ntok - c0)
            chunks.append((c0, n))
            c0 += n
        it = 0
        for s in range(NSLAB):
            if s + 1 < NSLAB:
                load_slab(s + 1)
            i = s % 2
            win = B[f"win{i}"]; wout = B[f"wout{i}"]
            for (c0, n) in chunks:
                tkeys = [("t", tt_) for tt_ in range(c0 // 128, (c0 + n) // 128)]
                ab = B[f"a{it % 2}"]; sgf = B[f"sgf{it % 2}"]
                it += 1
                for fc in range(2):
                    bg_ = ps(); bu_ = ps()
                    for kc in range(8):
                        mm(bv(bg_, np.s_[:, 0:n]), win.v(np.s_[:, kc, fc * 128:(fc + 1) * 128], "g"), B["hTall"].v(np.s_[:, kc, c0:c0 + n], tkeys), start=(kc == 0), stop=(kc == 7))
                    for kc in range(8):
                        mm(bv(bu_, np.s_[:, 0:n]), win.v(np.s_[:, kc, 256 + fc * 128:256 + (fc + 1) * 128], "u"), B["hTall"].v(np.s_[:, kc, c0:c0 + n], tkeys), start=(kc == 0), stop=(kc == 7))
                    act(sgf.v(np.s_[:, 0:n]), bv(bg_, np.s_[:, 0:n]), AF.Silu)
                    tt(ab.v(np.s_[:, fc, 0:n]), sgf.v(np.s_[:, 0:n]), bv(bu_, np.s_[:, 0:n]), ALU.mult)
                for dc in range(8):
                    by = ps()
                    for fc in range(2):
                        mm(bv(by, np.s_[:, 0:n]), wout.v(np.s_[:, fc, dc * 128:(dc + 1) * 128]), ab.v(np.s_[:, fc, 0:n]), start=(fc == 0), stop=(fc == 1))
                    xv = x.v(np.s_[:, dc, c0:c0 + n], [("t", tt_ * 128) for tt_ in range(c0 // 128, (c0 + n) // 128)])
                    tt(xv, xv, bv(by, np.s_[:, 0:n]), ALU.add)

    def issue_mixer_weights(l):
        if l % 2 == 0:
            load_w(wA.v(), w_in_ab[l // 2].rearrange("(kc p) n -> p kc n", p=128), "w_in")
            load_w(wB.v(), w_out_ab[l // 2].rearrange("(kc p) n -> p kc n", p=128), "w_out")
        else:
            load_w(wA.v(np.s_[:, :, 0:3 * D]), w_qkv[l // 2].rearrange("(kc p) n -> p kc n", p=128), "w_qkv")
            load_w(wB.v(), w_o_att[l // 2].rearrange("(kc p) n -> p kc n", p=128), "w_o")

    issue_mixer_weights(0)
    phase_barrier()
    for o_ in range(NO):
        build_hscr(o_)
    phase_barrier()
    for t in range(JT):
        load_x_tile(t * 128, xp[t * 128:(t + 1) * 128, :])
    for j in range(NJ):
        last = (j == NJ - 1)
        ntiles = JT + (1 if last else 0)
        if last:
            load_x_tile(JT * 128, xs[:, :])
        for l in range(DEPTH):
            if _CFG.get("STAGE") == "io":
                break
            phase_barrier()
            if l % 2 == 0:
                e = l // 2
                specs = [dict(e=e, l=l, tcol=t * 128, sample=False, want_state=(last and t == JT - 1)) for t in range(JT)]
                if last:
                    specs.append(dict(e=e, l=l, tcol=JT * 128, sample=True, want_state=True))
                run_pipelined(specs, even_gen)
            else:
                o = l // 2
                odd_phase_begin(o, j)
                for t in range(JT):
                    g = j * JT + t
                    orow = (g - (NTSEQ - 4)) * 128 if g >= NTSEQ - 4 else None
                    odd_tile(o, l, t * 128, t, False, orow)
                odd_phase_end(o, j, last)
                if last:
                    odd_tile(o, l, JT * 128, None, True, None)
            if _CFG.get("STAGE") == "mix" or (_CFG.get("STAGE") == "mix0" and l == 0):
                continue
            phase_barrier()
            if l + 1 < DEPTH:
                issue_mixer_weights(l + 1)
            elif not last:
                issue_mixer_weights(0)
            ffn_phase(l, ntiles)
        phase_barrier()
        for t in range(JT):
            store_y_tile(t * 128, yp[(j * JT + t) * 128:(j * JT + t + 1) * 128, :])
            if not last:
                load_x_tile(t * 128, xp[((j + 1) * JT + t) * 128:((j + 1) * JT + t + 1) * 128, :])
        if last:
            store_y_tile(JT * 128, ys[:, :])

    with nc.allow_non_contiguous_dma(reason="small strided parameter/state transfers"):
        stats = P.emit()
    return nc, stats


_BUILD_CACHE = {}


def kernel(x_prompt, x_sample, state_conv, state_gla, cache_k, cache_v,
           norm_mix, norm_ffn, w_in_ab, conv_w, gla_gk_w2, gla_gk_b, gla_onorm, w_out_ab,
           w_qkv, q_norm, k_norm, rel_bias, w_o_att, w_ffn_in, w_ffn_out):
    f = lambda a: np.ascontiguousarray(np.asarray(a, dtype=np.float32))
    x_prompt = f(x_prompt); x_sample = f(x_sample)
    BATCH, SEQ, _ = x_prompt.shape
    DEPTH = _CFG["DEPTH"]
    NE = (DEPTH + 1) // 2; NO = DEPTH // 2
    key = (SEQ, DEPTH)
    if key not in _BUILD_CACHE:
        _BUILD_CACHE[key] = build(SEQ, DEPTH)
    nc, stats = _BUILD_CACHE[key]
    state_conv = f(state_conv); state_gla = f(state_gla); cache_k = f(cache_k); cache_v = f(cache_v)
    shared = {"norm_mix": f(norm_mix), "norm_ffn": f(norm_ffn), "w_in_ab": f(w_in_ab), "conv_w": f(conv_w),
              "gk_w2": f(gla_gk_w2), "gk_b": f(gla_gk_b), "onorm": f(gla_onorm), "w_out_ab": f(w_out_ab),
              "w_qkv": f(w_qkv), "q_norm": f(q_norm), "k_norm": f(k_norm), "rel_bias": f(rel_bias),
              "w_o_att": f(w_o_att), "w_ffn_in": f(w_ffn_in), "w_ffn_out": f(w_ffn_out), "consts": _consts()}
    in_maps = []
    for c in range(8):
        b = c % 4
        m = dict(shared)
        m["xp"] = x_prompt[b]
        m["xs"] = x_sample[2 * b:2 * b + 2].reshape(128, D)
        m["sconv"] = np.ascontiguousarray(state_conv[:NE, 2 * b:2 * b + 2])
        m["sgla"] = np.ascontiguousarray(state_gla[:NE, 2 * b:2 * b + 2])
        if NO > 0:
            m["ck"] = np.ascontiguousarray(cache_k[:NO, 2 * b:2 * b + 2].reshape(NO, 2, 512, D))
            m["cv"] = np.ascontiguousarray(cache_v[:NO, 2 * b:2 * b + 2].reshape(NO, 2, 512, D))
        else:
            m["ck"] = np.zeros((1, 2, 512, D), np.float32); m["cv"] = np.zeros((1, 2, 512, D), np.float32)
        in_maps.append(m)
    ncores = _CFG.get("NCORES", 8)
    res = run_bass_kernel_spmd(nc, in_maps[:ncores], core_ids=list(range(ncores)))
    R = list(res.results)
    while len(R) < 4:
        R.append(R[0])
    y_prompt = np.stack([R[b]["yp"] for b in range(4)])
    y_sample = np.concatenate([R[b]["ys"].reshape(2, 64, D) for b in range(4)])
    conv_p = np.stack([R[b]["convp"] for b in range(4)], axis=1)
    gla_p = np.stack([R[b]["glap"] for b in range(4)], axis=1)
    k_p = np.stack([R[b]["kp"][:NO].reshape(NO, 512, 16, 64) for b in range(4)], axis=1)
    v_p = np.stack([R[b]["vp"][:NO].reshape(NO, 512, 16, 64) for b in range(4)], axis=1)
    conv_s = np.concatenate([R[b]["convs"] for b in range(4)], axis=1)
    gla_s = np.concatenate([R[b]["glas"] for b in range(4)], axis=1)
    k_s = np.concatenate([R[b]["ks"][:NO].reshape(NO, 2, 512, 16, 64) for b in range(4)], axis=1)
    v_s = np.concatenate([R[b]["vs"][:NO].reshape(NO, 2, 512, 16, 64) for b in range(4)], axis=1)
    return (y_prompt, y_sample, conv_p, gla_p, k_p, v_p, conv_s, gla_s, k_s, v_s)
```

```python
import numpy as np
import concourse.bass as bass
import concourse.mybir as mybir
from concourse.bass_utils import run_bass_kernel_spmd

F32 = mybir.dt.float32
BF16 = mybir.dt.bfloat16
I32 = mybir.dt.int32
AF = mybir.ActivationFunctionType
ALU = mybir.AluOpType

import os
DBG_DMA = bool(os.environ.get("DBG_DMA"))
SAME_ENGINE_SYNC = True
RAW_ONLY_SAME_ENGINE = bool(int(os.environ.get("RAW_ONLY", "0")))
EPOCH = 30000
N_DMA_SEMS = 8


class Unit:
    __slots__ = ("name", "w", "rs")

    def __init__(self, name):
        self.name = name
        self.w = None
        self.rs = []


class Rec:
    __slots__ = ("eng", "fn", "deps", "is_dma", "marked", "num", "sem_i", "cnt", "prev_cnt", "desc", "raw")


class V:
    __slots__ = ("ap", "units", "ro")

    def __init__(self, ap, units):
        self.ap = ap
        self.units = list(units)
        self.ro = ()


class Prog:
    ENGS = ("pe", "act", "dve", "pool", "sp")

    def __init__(self, nc):
        self.nc = nc
        self.q = {e: [] for e in self.ENGS}
        self.units = {}
        self.n_dma = 0
        self.n_dma_q = {}
        self.dma_recs = []

    def unit(self, key):
        u = self.units.get(key)
        if u is None:
            u = Unit(key)
            self.units[key] = u
        return u

    def op(self, eng, fn, r=(), w=(), dma=False):
        rec = Rec()
        rec.eng = eng
        rec.fn = fn
        rec.is_dma = dma
        rec.marked = False
        rec.num = 0
        deps = {}
        raw = set()
        for u in r:
            if u.w is not None:
                deps[id(u.w)] = u.w
                raw.add(id(u.w))
        rec.raw = raw
        for u in w:
            if u.w is not None:
                deps[id(u.w)] = u.w
            for x in u.rs:
                deps[id(x)] = x
        deps.pop(id(rec), None)
        rec.deps = list(deps.values())
        for u in r:
            u.rs.append(rec)
        for u in w:
            u.w = rec
            u.rs = []
        if dma:
            kq = self.n_dma_q.get(eng, 0)
            self.n_dma_q[eng] = kq + 1
            self.n_dma += 1
            base = {"sp": 0, "pool": 1, "act": 2}[eng] * N_DMA_SEMS
            rec.sem_i = base + kq % N_DMA_SEMS
            rec.cnt = 16 * (kq // N_DMA_SEMS + 1)
            self.dma_recs.append(rec)
        self.q[eng].append(rec)
        return rec

    def _units(self, views):
        out = []
        for v in views:
            if v is not None:
                out.extend(v.units)
        return out

    def mm(self, out, lhsT, rhs, start=True, stop=True, **kw):
        return self.op("pe", lambda e: e.matmul(out.ap, lhsT.ap, rhs.ap, start=start, stop=stop, **kw),
                       r=self._units([lhsT, rhs]), w=out.units)

    def transpose(self, out, in_, ident):
        return self.op("pe", lambda e: e.transpose(out.ap, in_.ap, ident.ap),
                       r=self._units([in_, ident]), w=out.units)

    def act(self, out, in_, func, bias=None, scale=None, eng="act"):
        kw = {}
        rs = [in_]
        if bias is not None:
            if isinstance(bias, V):
                kw["bias"] = bias.ap
                rs.append(bias)
            else:
                kw["bias"] = bias
        if scale is not None:
            if isinstance(scale, V):
                kw["scale"] = scale.ap
                rs.append(scale)
            else:
                kw["scale"] = scale
        return self.op(eng, lambda e: e.activation(out.ap, in_.ap, func, **kw),
                       r=self._units(rs), w=out.units)

    def tt(self, out, in0, in1, op, eng="dve"):
        return self.op(eng, lambda e: e.tensor_tensor(out.ap, in0.ap, in1.ap, op),
                       r=self._units([in0, in1]), w=out.units)

    def stt(self, out, in0, scalar, in1, op0, op1, eng="dve"):
        rs = [in0, in1]
        sc = scalar
        if isinstance(scalar, V):
            rs.append(scalar)
            sc = scalar.ap
        return self.op(eng, lambda e: e.scalar_tensor_tensor(out.ap, in0.ap, sc, in1.ap, op0, op1),
                       r=self._units(rs), w=out.units)

    def ts(self, out, in0, s1, s2, op0, op1=None, eng="dve"):
        rs = [in0]
        a1, a2 = s1, s2
        if isinstance(s1, V):
            rs.append(s1)
            a1 = s1.ap
        if isinstance(s2, V):
            rs.append(s2)
            a2 = s2.ap
        if op1 is None:
            return self.op(eng, lambda e: e.tensor_scalar(out.ap, in0.ap, a1, None, op0),
                           r=self._units(rs), w=out.units)
        return self.op(eng, lambda e: e.tensor_scalar(out.ap, in0.ap, a1, a2, op0, op1),
                       r=self._units(rs), w=out.units)

    def copy(self, out, in_, eng="dve"):
        if eng == "act":
            return self.op("act", lambda e: e.copy(out.ap, in_.ap), r=in_.units, w=out.units)
        return self.op(eng, lambda e: e.tensor_copy(out.ap, in_.ap), r=in_.units, w=out.units)

    def memset(self, out, val, eng="dve"):
        return self.op(eng, lambda e: e.memset(out.ap, val), r=(), w=out.units)

    def recip(self, out, in_):
        return self.op("dve", lambda e: e.reciprocal(out.ap, in_.ap), r=in_.units, w=out.units)

    def dma(self, out, in_, q="sp", **kw):
        return self.op(q, lambda e: e.dma_start(out=out.ap, in_=in_.ap, **kw),
                       r=in_.units, w=out.units, dma=True)

    def emit(self):
        nc = self.nc
        for e in self.ENGS:
            for rec in self.q[e]:
                keep = []
                for d in rec.deps:
                    if d.is_dma:
                        keep.append(d)
                    elif d.eng != rec.eng:
                        d.marked = True
                        keep.append(d)
                    elif d.eng != "pe" and (rec.is_dma or (SAME_ENGINE_SYNC and (not RAW_ONLY_SAME_ENGINE or id(d) in rec.raw))):
                        d.marked = True
                        keep.append(d)
                rec.deps = keep
        nmark = {}
        for e in self.ENGS:
            n = 0
            for rec in self.q[e]:
                if (not rec.is_dma) and rec.marked:
                    n += 1
                    rec.num = n
            nmark[e] = n
        esems = {e: [nc.alloc_semaphore(f"s_{e}_{k}") for k in range(nmark[e] // EPOCH + 1)]
                 for e in self.ENGS}
        dsems = [nc.alloc_semaphore(f"s_dma_{k}") for k in range(2 * N_DMA_SEMS)]
        stats = {}
        with nc.Block() as block:
            decos = {"pe": block.tensor, "act": block.scalar, "dve": block.vector,
                     "pool": block.gpsimd, "sp": block.sync}
            for e in self.ENGS:
                def body(eo, e=e):
                    seen = {}
                    seen_d = {}
                    nw = 0
                    for rec in self.q[e]:
                        if rec.is_dma and rec.cnt > 16:
                            if seen_d.get(rec.sem_i, 0) < rec.cnt - 16:
                                eo.wait_ge(dsems[rec.sem_i], rec.cnt - 16)
                                seen_d[rec.sem_i] = rec.cnt - 16
                                nw += 1
                        need_e = {}
                        need_d = {}
                        for d in rec.deps:
                            if d.is_dma:
                                if d.cnt > need_d.get(d.sem_i, 0):
                                    need_d[d.sem_i] = d.cnt
                            elif d.num > need_e.get(d.eng, 0):
                                need_e[d.eng] = d.num
                        for si, cnt in need_d.items():
                            if seen_d.get(si, 0) >= cnt:
                                continue
                            eo.wait_ge(dsems[si], cnt)
                            seen_d[si] = cnt
                            nw += 1
                        for de, num in need_e.items():
                            if seen.get(de, 0) >= num:
                                continue
                            ep = (num - 1) // EPOCH
                            eo.wait_ge(esems[de][ep], num - ep * EPOCH)
                            seen[de] = num
                            nw += 1
                        if rec.is_dma and DBG_DMA:
                            print("DMA", nc.get_next_instruction_name(), e, getattr(rec, "desc", None))
                        ins = rec.fn(eo)
                        if rec.is_dma:
                            ins.then_inc(dsems[rec.sem_i], 16)
                        elif rec.marked:
                            ep = (rec.num - 1) // EPOCH
                            ins.then_inc(esems[e][ep], 1)
                    if e == "sp":
                        last = {}
                        for rec in self.dma_recs:
                            last[rec.sem_i] = max(last.get(rec.sem_i, 0), rec.cnt)
                        for i, c in last.items():
                            if seen_d.get(i, 0) < c:
                                eo.wait_ge(dsems[i], c)
                    stats[e] = (len(self.q[e]), nw)
                decos[e](body)
        return stats


class Tile:
    def __init__(self, P, name, shape, dtype, psum=False):
        self.P = P
        self.name = name
        self.shape = shape
        nc = P.nc
        if psum:
            self.t = nc.alloc_psum_tensor(name, shape, dtype)
        else:
            self.t = nc.alloc_sbuf_tensor(name, shape, dtype)

    def v(self, idx=None, keys=("",)):
        ap = self.t[idx] if idx is not None else self.t[:]
        if not isinstance(keys, list):
            keys = (keys,)
        return V(ap, [self.P.unit((self.name, k)) for k in keys])


def dram_v(P, ap, key):
    return V(ap, [P.unit(("dram", key))])

D = 1024
DFF = 2816
NSLAB = DFF // 256
EPS = 1e-6
NEG = -30000.0
JT = 8
SBUF_LO = 16384 + 256
SBUF_HI = 229376 - 128

_CFG = {"DEPTH": 4}


def _consts():
    c = np.zeros((128, 7, 128), np.float32)
    idx = np.arange(128)
    c[:, 0] = np.eye(128)
    c[:, 1] = np.eye(128)[::-1]
    same = (idx[:, None] // 64) == (idx[None, :] // 64)
    c[:, 2] = (same & (idx[:, None] <= idx[None, :])).astype(np.float32)
    c[:, 3] = (same & (idx[:, None] > idx[None, :])).astype(np.float32) * (-1.0 / 16)
    c[:, 4] = 1.0 / 1024
    c[:, 5] = same.astype(np.float32) / 64.0
    c[:, 6] = 1.0 / 128
    return c.reshape(128, 7 * 128)


def build(SEQ, DEPTH):
    NJ = SEQ // (JT * 128)
    NTSEQ = SEQ // 128
    NE = (DEPTH + 1) // 2
    NO = DEPTH // 2
    TOKMAX = (JT + 1) * 128
    nc = bass.Bass("TRN2", target_bir_lowering=False)
    P = Prog(nc)

    def din(name, shape):
        return nc.dram_tensor(name, shape, F32, kind="ExternalInput").ap()

    def dout(name, shape):
        return nc.dram_tensor(name, shape, F32, kind="ExternalOutput").ap()

    xp = din("xp", [SEQ, D]); xs = din("xs", [128, D])
    sconv = din("sconv", [NE, 2, 2, 512]); sgla = din("sgla", [NE, 2, 4, 64, 128])
    ck = din("ck", [max(NO, 1), 2, 512, D]); cv = din("cv", [max(NO, 1), 2, 512, D])
    norm_mix = din("norm_mix", [4, D]); norm_ffn = din("norm_ffn", [4, D])
    w_in_ab = din("w_in_ab", [2, D, 3088]); conv_w = din("conv_w", [2, 3, 512])
    gk_w2 = din("gk_w2", [2, 16, 256]); gk_b = din("gk_b", [2, 256]); onorm = din("onorm", [2, 128])
    w_out_ab = din("w_out_ab", [2, D, D]); w_qkv = din("w_qkv", [2, D, 3 * D])
    q_norm = din("q_norm", [2, 64]); k_norm = din("k_norm", [2, 64]); rel_bias = din("rel_bias", [2, 16, 320])
    w_o_att = din("w_o_att", [2, D, D]); w_ffn_in = din("w_ffn_in", [4, D, 2 * DFF]); w_ffn_out = din("w_ffn_out", [4, DFF, D])
    consts_d = din("consts", [128, 7 * 128])
    yp = dout("yp", [SEQ, D]); ys = dout("ys", [128, D])
    convp = dout("convp", [NE, 2, 512]); glap = dout("glap", [NE, 4, 64, 128])
    kp = dout("kp", [max(NO, 1), 512, D]); vp = dout("vp", [max(NO, 1), 512, D])
    convs = dout("convs", [NE, 2, 2, 512]); glas = dout("glas", [NE, 2, 4, 64, 128])
    ks = dout("ks", [max(NO, 1), 2, 512, D]); vs = dout("vs", [max(NO, 1), 2, 512, D])
    ext_d = nc.dram_tensor("ext_scr", [max(NO, 1), 16, 768], F32)
    kscr = nc.dram_tensor("k_scr", [max(NO, 1), 128, 8, 512], BF16)
    vscr = nc.dram_tensor("v_scr", [max(NO, 1), 128, 4, 1040], BF16)
    hscr = nc.dram_tensor("h_scr", [max(NO, 1), 128, 16 * 5 * 128], BF16)

    def dv(ap, key):
        return V(ap, [P.unit(("dram", key))])

    cur = [SBUF_LO]

    class T2:
        def __init__(self, name, shape, dtype, base=None, ov=None):
            nbytes = int(np.prod(shape[1:])) * (4 if dtype in (F32, I32) else 2)
            nbytes = (nbytes + 63) // 64 * 64
            if base is None:
                off = cur[0]; cur[0] += nbytes
            else:
                off = base[0]; base[0] += nbytes
            assert off + nbytes <= SBUF_HI, (name, off, nbytes)
            self.t = nc.alloc_sbuf_tensor_at(name, shape, dtype, offset=off)
            self.name = name
            self.ov = ov

        def v(self, idx=None, keys=("",)):
            ap = self.t[idx] if idx is not None else self.t[:]
            if not isinstance(keys, list):
                keys = (keys,)
            vv = V(ap, [P.unit((self.name, k)) for k in keys])
            if self.ov is not None:
                vv.ro = [self.ov]
            return vv

        def w(self, ap, keys=("",)):
            if not isinstance(keys, list):
                keys = (keys,)
            vv = V(ap, [P.unit((self.name, k)) for k in keys])
            if self.ov is not None:
                vv.ro = [self.ov]
            return vv

    class Alias:
        def __init__(self, base, ap):
            self.base = base; self.ap0 = ap
        def v(self, idx=None, keys=("",)):
            return self.base.w(self.ap0 if idx is None else self.ap0[idx], keys)

    x = T2("x", [128, 8, TOKMAX], F32)
    cst = T2("cst", [128, 7, 128], F32)
    jbf = T2("jbf", [128, 128], BF16)
    ones1 = T2("ones1", [1, 128], F32)
    pv = T2("pv", [128, 128], F32); pvst = T2("pvst", [128, 128], F32)
    gw2b = T2("gw2b", [32, 2, 256], BF16)
    qnc = T2("qnc", [128, 2], F32)
    uhalo = T2("uhalo", [128, 2, 4, 2], F32)
    S_p = [T2(f"S_p{e}", [128, 2, 128], F32) for e in range(NE)]
    S_s = [[T2(f"S_s{e}_{s}", [128, 2, 128], F32) for s in range(2)] for e in range(NE)]
    dummy = T2("dummy", [128, 8], F32)
    wA = T2("wA", [128, 8, 3088], BF16)
    wB = T2("wB", [128, 8, D], BF16)
    R1 = cur[0]
    ov_unit = P.unit(("ov", "R1"))

    _b_io = [R1]
    xstages = [T2(f"xstage{i_}", [128, D], F32, base=_b_io, ov=ov_unit) for i_ in range(3)]
    xst_i = [0]
    ident = cst.v(np.s_[:, 0, :]); Jf = cst.v(np.s_[:, 1, :]); Mc = cst.v(np.s_[:, 2, :]); M2 = cst.v(np.s_[:, 3, :])
    onesD = cst.v(np.s_[:, 4, :]); ones64 = cst.v(np.s_[:, 5, :]); ones128 = cst.v(np.s_[:, 6, :])

    def phase_barrier():
        P.op("dve", lambda e: e.memset(dummy.t[:, 0:1], 0.0), r=(), w=[ov_unit, P.unit(("dummy", ""))])

    banks = [Tile(P, f"pb{i}", [128, 512], F32, psum=True) for i in range(8)]
    bank_i = [0]

    cur_pool = [None]
    pool_i = [0, 0]

    def ps():
        if cur_pool[0] is not None:
            p_ = cur_pool[0]
            b = banks[p_ * 4 + pool_i[p_] % 4]
            pool_i[p_] += 1
            return b
        b = banks[bank_i[0] % 8]
        bank_i[0] += 1
        return b

    def bv(b, idx=None):
        return V(b.t[idx] if idx is not None else b.t[:], [P.unit((b.name, ""))])

    def units_r(views):
        out = []
        for v_ in views:
            if v_ is None or not isinstance(v_, V):
                continue
            out.extend(v_.units)
            out.extend(getattr(v_, "ro", ()))
        return out

    def units_w(v_):
        return list(v_.units)

    def units_wr(v_):
        return list(getattr(v_, "ro", ()))

    def OP(eng, fn, ins, out, dma=False):
        return P.op(eng, fn, r=units_r(ins) + units_wr(out), w=units_w(out), dma=dma)

    def mm(out, lhsT, rhs, start=True, stop=True):
        return OP("pe", lambda e: e.matmul(out.ap, lhsT.ap, rhs.ap, start=start, stop=stop), [lhsT, rhs], out)

    def tr(out, in_, idv):
        return OP("pe", lambda e: e.transpose(out.ap, in_.ap, idv.ap), [in_, idv], out)

    def act(out, in_, func, bias=None, scale=None):
        kw = {}
        if bias is not None:
            kw["bias"] = bias.ap if isinstance(bias, V) else bias
        if scale is not None:
            kw["scale"] = scale.ap if isinstance(scale, V) else scale
        return OP("act", lambda e: e.activation(out.ap, in_.ap, func, **kw), [in_, bias, scale], out)

    def tt(out, a, b, op, eng="dve"):
        return OP(eng, lambda e: e.tensor_tensor(out.ap, a.ap, b.ap, op), [a, b], out)

    def stt(out, a, sc, b, op0, op1, eng="dve"):
        s_ = sc.ap if isinstance(sc, V) else sc
        return OP(eng, lambda e: e.scalar_tensor_tensor(out.ap, a.ap, s_, b.ap, op0, op1), [a, sc, b], out)

    def cp(out, in_, eng="dve"):
        if eng == "act":
            return OP("act", lambda e: e.copy(out.ap, in_.ap), [in_], out)
        return OP(eng, lambda e: e.tensor_copy(out.ap, in_.ap), [in_], out)

    def ms(out, val, eng="dve"):
        return OP(eng, lambda e: e.memset(out.ap, val), [], out)

    def rcp(out, in_):
        return OP("dve", lambda e: e.reciprocal(out.ap, in_.ap), [in_], out)

    def dma(out, in_, q="sp"):
        return OP(q, lambda e: e.dma_start(out=out.ap, in_=in_.ap), [in_], out, dma=True)

    dma(cst.v(), dv(consts_d.rearrange("p (a b) -> p a b", a=7), "consts"))
    cp(jbf.v(), Jf)
    ms(ones1.v(), 1.0)
    ms(pvst.v(), 0.0)
    dma(pvst.v(np.s_[0:32, :]), dv(norm_mix.rearrange("l (c p) -> (l c) p", p=128), "nm"))
    dma(pvst.v(np.s_[32:64, :]), dv(norm_ffn.rearrange("l (c p) -> (l c) p", p=128), "nf"))
    dma(pvst.v(np.s_[64:88, :]), dv(conv_w.rearrange("e i (c p) -> (e i c) p", p=128), "cw"))
    dma(pvst.v(np.s_[88:90, :]), dv(onorm, "onc"))
    for hh in range(2):
        dma(pvst.v(np.s_[90:92, hh * 64:(hh + 1) * 64]), dv(q_norm, "qn"))
        dma(pvst.v(np.s_[92:94, hh * 64:(hh + 1) * 64]), dv(k_norm, "kn"))
    _bt = ps()
    tr(bv(_bt, np.s_[:, 0:128]), pvst.v(), ident)
    cp(pv.v(), bv(_bt, np.s_[:, 0:128]))
    OP("dve", lambda e: e.tensor_scalar(qnc.t[:], pv.t[:, 90:92], 0.125, None, ALU.mult), [pv.v()], qnc.v())
    ms(gw2b.v(), 0.0)
    dma(gw2b.v(np.s_[0:16, :, :]), dv(gk_w2.rearrange("e r n -> r e n"), "gw2"), q="pool")
    dma(gw2b.v(np.s_[16:17, :, :]), dv(gk_b.rearrange("(o e) n -> o e n", o=1), "gb"), q="pool")
    ms(uhalo.v(), 0.0)
    for e_ in range(NE):
        ms(S_p[e_].v(), 0.0)
        for s in range(2):
            for hh in range(2):
                dma(S_s[e_][s].w(S_s[e_][s].t[hh * 64:(hh + 1) * 64, :, :]),
                    dv(sgla[e_, s].rearrange("(pr hh) k v -> hh k pr v", hh=2)[hh], "sgla"))
    for o in range(NO):
        dma(dv(ext_d.ap()[o, :, 0:64], ("ext", o)), dv(rel_bias[o, :, 0:64], "rb"))
        dma(dv(ext_d.ap()[o, :, 64:384], ("ext", o)), dv(rel_bias[o, :, 0:320], "rb"))
        dma(pvst.v(np.s_[0:16, 0:1]), dv(rel_bias[o, :, 319:320], "rb"))
        cp(pvst.v(np.s_[0:16, 1:128]), V(pvst.t[0:16, 0:1].to_broadcast([16, 127]), pvst.v().units))
        for q_ in range(3):
            dma(dv(ext_d.ap()[o, :, 384 + q_ * 128:512 + q_ * 128], ("ext", o)), pvst.v(np.s_[0:16, :]))

    def load_x_tile(tcol, src_rows):
        xstage = xstages[xst_i[0] % 3]; xst_i[0] += 1
        dma(xstage.v(), dv(src_rows, "xin"))
        for half in range(2):
            b = ps()
            for c in range(4):
                tr(bv(b, np.s_[:, c * 128:(c + 1) * 128]), xstage.v(np.s_[:, (half * 4 + c) * 128:(half * 4 + c + 1) * 128]), ident)
            cp(x.v(np.s_[:, half * 4:half * 4 + 4, tcol:tcol + 128], ("t", tcol)),
               V(b.t[:, :].rearrange("p (c n) -> p c n", c=4), bv(b).units), eng="act")

    def store_y_tile(tcol, dst_rows):
        xstage = xstages[xst_i[0] % 3]; xst_i[0] += 1
        for half in range(2):
            b = ps()
            for c in range(4):
                tr(bv(b, np.s_[:, c * 128:(c + 1) * 128]), x.v(np.s_[:, half * 4 + c, tcol:tcol + 128], ("t", tcol)), ident)
            cp(xstage.v(np.s_[:, half * 512:(half + 1) * 512]), bv(b), eng="act")
        dma(dv(dst_rows, "yout"), xstage.v())

    def rmsnorm(tcol, gcol_t, l, hT, sqt, rst):
        xt = x.v(np.s_[:, :, tcol:tcol + 128], ("t", tcol))
        act(sqt.v(), xt, AF.Square)
        b = ps()
        for c in range(8):
            mm(bv(b, np.s_[:, 0:128]), onesD, sqt.v(np.s_[:, c, :]), start=(c == 0), stop=(c == 7))
        act(rst.v(), bv(b, np.s_[:, 0:128]), AF.Ln, bias=EPS)
        act(rst.v(), rst.v(), AF.Exp, scale=-0.5)
        for c in range(8):
            stt(hT(c), x.v(np.s_[:, c, tcol:tcol + 128], ("t", tcol)), pv.w(pv.t[:, gcol_t + l * 8 + c:gcol_t + l * 8 + c + 1]), rst.v(), ALU.mult, ALU.mult)

    def add_to_x(tcol, c0, nch, b, n=128):
        xv = x.v(np.s_[:, c0:c0 + nch, tcol:tcol + n], ("t", tcol))
        tt(xv, xv, V(b.t[:, 0:nch * n].rearrange("p (c n) -> p c n", c=nch), bv(b).units), ALU.add)

    def load_w_thunks(dst, src_ap, key):
        nk = src_ap.shape[1]
        return [(lambda kc=kc: dma(_sub(dst, kc), dv(src_ap[:, kc, :], key), q="pool")) for kc in range(nk)]

    def load_w(dst, src_ap, key):
        for th in load_w_thunks(dst, src_ap, key):
            th()

    def _sub(dst, kc):
        vv = V(dst.ap[:, kc, :], dst.units)
        vv.ro = dst.ro
        return vv

    _ebase = [R1]

    def even_bufs(par):
        base = _ebase
        B = {}
        def mk(name, shape, dt):
            B[name] = T2(f"e{par}_" + name, shape, dt, base=base, ov=ov_unit)
        mk("hT", [128, 8, 128], BF16); mk("sqt", [128, 8, 128], F32); mk("rst", [128, 128], F32)
        mk("cg", [128, 4, 128], F32); mk("ub", [128, 4, 130], F32); mk("ubA", [128, 4, 66], F32); mk("ubB", [128, 4, 66], F32)
        mk("yc", [128, 4, 128], F32); mk("tmp", [128, 4, 128], F32)
        mk("mixT", [128, 8, 128], BF16); mk("sg", [128, 4, 128], F32); mk("gkT", [32, 128], BF16)
        mk("ktok", [128, 256], F32); mk("vbf", [128, 512], BF16)
        mk("e1", [128, 256], F32); mk("sp", [128, 256], F32); mk("erb", [128, 256], F32); mk("kdec", [128, 2, 256], BF16)
        mk("ebT", [128, 2, 128], F32); mk("enbT", [128, 2, 128], F32)
        mk("qe", [128, 2, 2, 128], BF16); mk("ke", [128, 2, 128], BF16)
        mk("attm", [128, 4, 128], BF16); mk("Sb0", [128, 2, 128], BF16); mk("Sb1", [128, 2, 128], BF16)
        mk("sq", [128, 512], F32); mk("rs2", [128, 512], F32); mk("y1", [128, 4, 128], F32)
        mk("cstg", [2, 512], F32); mk("qkraw", [128, 4, 128], F32)
        return B

    def odd_bufs():
        base = [R1]
        B = {}
        def mk(name, shape, dt):
            B[name] = T2("o_" + name, shape, dt, base=base, ov=ov_unit)
        mk("hT", [128, 8, 128], BF16); mk("rst", [128, 128], F32)
        mk("qT", [128, 2, 8, 128], BF16); mk("kn", [128, 8, 128], F32); mk("kown", [128, 8, 128], BF16)
        mk("raw", [128, 512], F32); mk("sq", [128, 512], F32); mk("rs2", [128, 512], F32)
        mk("kring", [128, 8, 8 * 128], BF16); mk("vring", [128, 8, 16 * 65], BF16); mk("vown", [128, 16 * 65], BF16)
        mk("H", [128, 16, 5, 128], BF16); mk("pT0", [128, 4, 5, 128], BF16); mk("pT1", [128, 4, 5, 128], BF16)
        mk("atok", [128, D], F32); mk("rc", [128, 16], F32); mk("attT", [128, 8, 128], BF16)
        mk("stage", [128, D], F32)
        B["ckst"] = B["stage"]
        B["sqt"] = Alias(B["atok"], B["atok"].t[:, :].rearrange("p (c n) -> p c n", c=8))
        return B

    def ffn_bufs():
        base = [R1]
        B = {}
        def mk(name, shape, dt):
            B[name] = T2("f_" + name, shape, dt, base=base, ov=ov_unit)
        mk("hTall", [128, 8, TOKMAX], BF16)
        for i in range(3):
            mk(f"sqt{i}", [128, 8, 128], F32); mk(f"rst{i}", [128, 128], F32)
        for i in range(2):
            mk(f"win{i}", [128, 8, 512], BF16); mk(f"wout{i}", [128, 2, D], BF16)
            mk(f"a{i}", [128, 2, 512], BF16); mk(f"sgf{i}", [128, 512], F32)
        return B

    EBS = [even_bufs(0), even_bufs(1)]; OB = odd_bufs(); FB = ffn_bufs()

    def even_gen(e, l, tcol, sample, want_state, par):
        B = EBS[par]
        hT = lambda c: B["hT"].v(np.s_[:, c, :])
        rmsnorm(tcol, 0, l, hT, B["sqt"], B["rst"])
        hTa = B["hT"]
        yield None

        def fm4(col0, nch=4):
            b = ps()
            for c in range(nch):
                for kc in range(8):
                    mm(bv(b, np.s_[:, c * 128:(c + 1) * 128]), wA.v(np.s_[:, kc, col0 + c * 128:col0 + (c + 1) * 128]), hTa.v(np.s_[:, kc, :]),
                       start=(kc == 0), stop=(kc == 7))
            return b

        def b3(b, nch=4):
            return V(b.t[:, 0:nch * 128].rearrange("p (c n) -> p c n", c=nch), bv(b).units)

        def conv_out(ub, col, dst, key):
            _b = ps()
            for c_ in range(4):
                tr(bv(_b, np.s_[0:2, c_ * 128:(c_ + 1) * 128]), ub.v(np.s_[:, c_, col:col + 2]), ident)
            cp(B["cstg"].v(), bv(_b, np.s_[0:2, :]), eng="act")
            dma(dv(dst, key), B["cstg"].v())

        bcg = fm4(0)
        cp(B["cg"].v(), b3(bcg), eng="act")
        yield None
        bhc = fm4(1024)
        if not sample:
            cp(B["ub"].v(np.s_[:, :, 0:2]), uhalo.v(np.s_[:, e, :, :]), eng="dve")
            tt(B["ub"].v(np.s_[:, :, 2:130]), b3(bhc), B["cg"].v(), ALU.mult)
            segs = [(B["ub"], 128, 0)]
        else:
            for s, ub in enumerate((B["ubA"], B["ubB"])):
                dma(B["cstg"].v(), dv(sconv[e, s], "sconv"))
                _b = ps()
                for c_ in range(4):
                    tr(bv(_b, np.s_[:, c_ * 2:c_ * 2 + 2]), B["cstg"].v(np.s_[:, c_ * 128:(c_ + 1) * 128]), cst.w(cst.t[0:2, 0, 0:2]))
                cp(ub.v(np.s_[:, :, 0:2]), V(_b.t[:, 0:8].rearrange("p (c r) -> p c r", c=4), bv(_b).units), eng="act")
                tt(ub.v(np.s_[:, :, 2:66]), V(bhc.t[:, :].rearrange("p (c n) -> p c n", c=4)[:, :, s * 64:(s + 1) * 64], bv(bhc).units),
                   B["cg"].v(np.s_[:, :, s * 64:(s + 1) * 64]), ALU.mult)
            segs = [(B["ubA"], 64, 0), (B["ubB"], 64, 64)]
        for ub, n, off in segs:
            yv = B["yc"].v(np.s_[:, :, off:off + n])
            tv = B["tmp"].v(np.s_[:, :, off:off + n])
            wv = lambda i: V(pv.t[:, 64 + (e * 3 + i) * 4:64 + (e * 3 + i) * 4 + 4].unsqueeze(2).to_broadcast([128, 4, n]), pv.v().units)
            tt(yv, ub.v(np.s_[:, :, 2:2 + n]), wv(2), ALU.mult, eng="dve")
            tt(tv, ub.v(np.s_[:, :, 1:1 + n]), wv(1), ALU.mult, eng="dve")
            tt(yv, yv, tv, ALU.add, eng="dve")
            tt(tv, ub.v(np.s_[:, :, 0:n]), wv(0), ALU.mult, eng="dve")
            tt(yv, yv, tv, ALU.add, eng="dve")
        if not sample:
            cp(uhalo.v(np.s_[:, e, :, :]), B["ub"].v(np.s_[:, :, 128:130]), eng="dve")
            if want_state:
                conv_out(B["ub"], 128, convp[e], "convp")
        else:
            for s, ub in enumerate((B["ubA"], B["ubB"])):
                conv_out(ub, 64, convs[e, s], "convs")
        yield None
        bbg = fm4(512)
        tt(B["mixT"].v(np.s_[:, 0:4, :]), b3(bbg), B["yc"].v(), ALU.mult)
        yield None
        bgk = ps()
        for kc in range(8):
            mm(bv(bgk, np.s_[0:16, 0:128]), wA.v(np.s_[:, kc, 3072:3088]), hTa.v(np.s_[:, kc, :]), start=(kc == 0), stop=(kc == 7))
        ms(B["gkT"].v(), 1.0)
        cp(B["gkT"].v(np.s_[0:16, :]), bv(bgk, np.s_[0:16, 0:128]), eng="act")
        bk = ps(); bvv = ps()
        for kc in range(8):
            mm(bv(bk, np.s_[:, 0:256]), hTa.v(np.s_[:, kc, :]), wA.v(np.s_[:, kc, 1792:2048]), start=(kc == 0), stop=(kc == 7))
        for kc in range(8):
            mm(bv(bvv), hTa.v(np.s_[:, kc, :]), wA.v(np.s_[:, kc, 2048:2560]), start=(kc == 0), stop=(kc == 7))
        cp(B["ktok"].v(), bv(bk, np.s_[:, 0:256]), eng="act")
        cp(B["vbf"].v(), bv(bvv), eng="act")
        yield None
        bqk = fm4(1536)
        cp(B["qkraw"].v(), b3(bqk), eng="act")
        yield None
        bg = fm4(2560)
        act(B["sg"].v(), b3(bg), AF.Silu)
        yield "half"

        bz = ps()
        mm(bv(bz, np.s_[:, 0:256]), B["gkT"].v(), gw2b.v(np.s_[:, e, :]))
        act(B["e1"].v(), bv(bz, np.s_[:, 0:256]), AF.Exp, scale=-1.0)
        act(B["sp"].v(), B["e1"].v(), AF.Ln, bias=1.0)
        yield None
        brb = ps()
        mm(bv(brb, np.s_[:, 0:256]), M2, B["sp"].v())
        bbT = ps()
        for p_ in range(2):
            mm(bv(bbT, np.s_[:, p_ * 128:(p_ + 1) * 128]), B["sp"].v(np.s_[:, p_ * 128:(p_ + 1) * 128]), Mc)
        act(B["erb"].v(), bv(brb, np.s_[:, 0:256]), AF.Exp)
        for cc_ in range(2):
            oc_ = 1 - cc_
            ms(B["kdec"].v(np.s_[oc_ * 64:(oc_ + 1) * 64, cc_, :]), 0.0)
            tt(B["kdec"].v(np.s_[cc_ * 64:(cc_ + 1) * 64, cc_, :]), B["ktok"].v(np.s_[cc_ * 64:(cc_ + 1) * 64, :]), B["erb"].v(np.s_[cc_ * 64:(cc_ + 1) * 64, :]), ALU.mult)
        bT3 = V(bbT.t[:, 0:256].rearrange("p (c n) -> p c n", c=2), bv(bbT).units)
        act(B["ebT"].v(), bT3, AF.Exp, scale=-1.0 / 16)
        act(B["enbT"].v(), bT3, AF.Exp, scale=1.0 / 16)
        qk3 = B["qkraw"].v()
        for hh_ in range(2):
            oh_ = 1 - hh_
            ms(B["qe"].v(np.s_[oh_ * 64:(oh_ + 1) * 64, hh_, :, :]), 0.0)
            stt(B["qe"].v(np.s_[hh_ * 64:(hh_ + 1) * 64, hh_, :, :]), B["qkraw"].v(np.s_[hh_ * 64:(hh_ + 1) * 64, 0:2, :]), 0.125,
                B["ebT"].v(np.s_[hh_ * 64:(hh_ + 1) * 64, :, :]), ALU.mult, ALU.mult)
        tt(B["ke"].v(), B["qkraw"].v(np.s_[:, 2:4, :]), B["enbT"].v(), ALU.mult)
        yield None
        batt = ps()
        for h in range(4):
            pr, hh = h // 2, h % 2
            mm(bv(batt, np.s_[:, h * 128:(h + 1) * 128]), B["ke"].v(np.s_[:, pr, :]), B["qe"].v(np.s_[:, hh, pr, :]))
        tt(B["attm"].v(), b3(batt), V(cst.t[:, 2, :].unsqueeze(1).to_broadcast([128, 4, 128]), cst.v().units), ALU.mult)
        yield "need_state"
        if not sample:
            Sc = [S_p[e], S_p[e]]
        else:
            Sc = [S_s[e][0], S_s[e][1]]
        Sb = [B["Sb0"], B["Sb1"]]

        def upd(cc):
            S = Sc[cc]
            bs = ps()
            for pr in range(2):
                mm(bv(bs, np.s_[:, pr * 256:(pr + 1) * 256]), B["kdec"].v(np.s_[:, cc, pr * 128:(pr + 1) * 128]),
                   B["vbf"].v(np.s_[:, pr * 256:(pr + 1) * 256]))
            for pr in range(2):
                for hh in range(2):
                    sv = S.v(np.s_[hh * 64:(hh + 1) * 64, pr, :])
                    stt(sv, sv, B["ebT"].v(np.s_[hh * 64:(hh + 1) * 64, pr, cc * 64 + 63:cc * 64 + 64]),
                        bv(bs, np.s_[hh * 64:(hh + 1) * 64, pr * 256 + hh * 128:pr * 256 + (hh + 1) * 128]), ALU.mult, ALU.add)

        if not sample:
            cp(Sb[0].v(), Sc[0].v(), eng="act")
            upd(0)
            cp(Sb[1].v(), Sc[1].v(), eng="act")
            upd(1)
        else:
            cp(Sb[0].v(), Sc[0].v(), eng="act")
            cp(Sb[1].v(), Sc[1].v(), eng="act")
            upd(0)
            upd(1)
        if want_state:
            if not sample:
                for hh in range(2):
                    dma(dv(glap[e].rearrange("(pr hh) k v -> hh k pr v", hh=2)[hh], "glap"), S_p[e].v(np.s_[hh * 64:(hh + 1) * 64, :, :]))
            else:
                for s in range(2):
                    for hh in range(2):
                        dma(dv(glas[e, s].rearrange("(pr hh) k v -> hh k pr v", hh=2)[hh], "glas"), S_s[e][s].v(np.s_[hh * 64:(hh + 1) * 64, :, :]))
        yield "state_done"
        bo = ps()
        for h in range(4):
            pr, hh = h // 2, h % 2
            for cc in range(2):
                ov_ = bv(bo, np.s_[:, h * 128 + cc * 64:h * 128 + (cc + 1) * 64])
                mm(ov_, B["vbf"].v(np.s_[:, h * 128:(h + 1) * 128]), B["attm"].v(np.s_[:, h, cc * 64:(cc + 1) * 64]), start=True, stop=False)
                mm(ov_, Sb[cc].v(np.s_[:, pr, :]), B["qe"].v(np.s_[:, hh, pr, cc * 64:(cc + 1) * 64]), start=False, stop=True)
        act(B["sq"].v(), bv(bo), AF.Square)
        yield None
        bm = ps()
        mm(bv(bm), ones128, B["sq"].v())
        act(B["rs2"].v(), bv(bm), AF.Ln, bias=EPS)
        act(B["rs2"].v(), B["rs2"].v(), AF.Exp, scale=-0.5)
        stt(B["y1"].v(), b3(bo), pv.w(pv.t[:, 88 + e:89 + e]), V(B["rs2"].t[:, :].rearrange("p (c n) -> p c n", c=4), B["rs2"].v().units), ALU.mult, ALU.mult)
        tt(B["mixT"].v(np.s_[:, 4:8, :]), B["y1"].v(), B["sg"].v(), ALU.mult, eng="dve")
        yield None
        for half in range(2):
            b = ps()
            for c in range(4):
                dc = half * 4 + c
                for mc in range(8):
                    mm(bv(b, np.s_[:, c * 128:(c + 1) * 128]), wB.v(np.s_[:, mc, dc * 128:(dc + 1) * 128]), B["mixT"].v(np.s_[:, mc, :]),
                       start=(mc == 0), stop=(mc == 7))
            add_to_x(tcol, half * 4, 4, b)
            yield None

    def run_pipelined(specs, genf):
        def mk(i, spec):
            return {"g": genf(par=i % 2, **spec), "par": i % 2, "half": False, "sdone": False, "done": False, "wait": False}

        def step(st):
            cur_pool[0] = st["par"]
            r = next(st["g"], "DONE")
            cur_pool[0] = None
            if r == "DONE":
                st["done"] = True; st["sdone"] = True; st["half"] = True
            elif r == "half":
                st["half"] = True
            elif r == "need_state":
                st["wait"] = True
            elif r == "state_done":
                st["sdone"] = True

        old = None
        for i, spec in enumerate(specs):
            new = mk(i, spec)
            while True:
                if new["wait"] and (old is None or old["sdone"]):
                    new["wait"] = False
                progressed = False
                if not new["done"] and not new["wait"] and not (new["half"] and old is None and False):
                    step(new); progressed = True
                if old is not None and not old["done"]:
                    step(old); progressed = True
                if old is not None and old["done"]:
                    old = None
                if new["done"] or (new["half"] and old is None):
                    break
                assert progressed
            old = None if new["done"] else new
        while old is not None and not old["done"]:
            old["wait"] = False
            step(old)

    def attend(o, qlo, nq, kblk, vblk):
        B = OB
        obanks = [banks[0], banks[1], banks[2]]
        sB = banks[3]

        def scores_a(hg):
            pT = B[f"pT{hg % 2}"]
            for hi in range(4):
                h = hg * 4 + hi
                pr, hh = h // 2, h % 2
                sA = banks[4 + hi]
                qv = B["qT"].v(np.s_[:, hh, pr, qlo:qlo + nq])
                if nq == 128:
                    mm(bv(sA), jbf.v(), B["H"].w(B["H"].t[:, h, 0:4, :].rearrange("p a b -> p (a b)")), start=True, stop=False)
                    for kb in range(1, 5):
                        sl = 4 - kb
                        mm(bv(sA, np.s_[:, sl * 128:sl * 128 + nq]), kblk(kb, h), qv, start=False, stop=(kb == 4))
                else:
                    for kb in range(1, 5):
                        sl = 4 - kb
                        tgt = bv(sA, np.s_[:, sl * 128:sl * 128 + nq])
                        mm(tgt, jbf.v(), B["H"].v(np.s_[:, h, sl, qlo:qlo + nq]), start=True, stop=False)
                        mm(tgt, kblk(kb, h), qv, start=False, stop=True)
                act(pT.w(pT.t[:, hi, 0:4, 0:nq]),
                    V(sA.t[:, :].rearrange("p (c n) -> p c n", c=4)[:, :, 0:nq], bv(sA).units), AF.Exp)

        def scores_b(hg):
            pT = B[f"pT{hg % 2}"]
            for hi in range(4):
                h = hg * 4 + hi
                pr, hh = h // 2, h % 2
                qv = B["qT"].v(np.s_[:, hh, pr, qlo:qlo + nq])
                tgt = bv(sB, np.s_[:, hi * 128:hi * 128 + nq])
                mm(tgt, jbf.v(), B["H"].v(np.s_[:, h, 4, qlo:qlo + nq]), start=True, stop=False)
                mm(tgt, kblk(0, h), qv, start=False, stop=True)
            act(pT.w(pT.t[:, :, 4, 0:nq]), V(sB.t[:, :].rearrange("p (c n) -> p c n", c=4)[:, :, 0:nq], bv(sB).units), AF.Exp)

        def pvs(hg):
            pT = B[f"pT{hg % 2}"]
            for hi in range(4):
                h = hg * 4 + hi
                ob = obanks[h // 7]
                col = (h % 7) * 65
                for kb in range(5):
                    mm(bv(ob, np.s_[0:nq, col:col + 65]), pT.w(pT.t[:, hi, 4 - kb, 0:nq]), vblk(kb, h), start=(kb == 0), stop=(kb == 4))

        scores_a(0); scores_b(0)
        for hg in range(1, 4):
            scores_a(hg)
            pvs(hg - 1)
            scores_b(hg)
        pvs(3)
        for bi, ob in enumerate(obanks):
            nh = 7 if bi < 2 else 2
            h0 = bi * 7
            o3 = V(ob.t[0:nq, 0:nh * 65].rearrange("p (h n) -> p h n", h=nh), bv(ob).units)
            rcp(B["rc"].w(B["rc"].t[0:nq, h0:h0 + nh]), V(o3.ap[:, :, 64], o3.units))
            tt(B["atok"].w(B["atok"].t[0:nq, h0 * 64:(h0 + nh) * 64].rearrange("p (h n) -> p h n", h=nh)),
               V(o3.ap[:, :, 0:64], o3.units),
               B["rc"].w(B["rc"].t[0:nq, h0:h0 + nh].unsqueeze(2).to_broadcast([nq, nh, 64])), ALU.mult)
        for half in range(2):
            b = banks[4 + half]
            for c in range(4):
                fc = half * 4 + c
                tr(bv(b, np.s_[:, c * 128:c * 128 + nq]), B["atok"].w(B["atok"].t[0:nq, fc * 128:(fc + 1) * 128]), cst.w(cst.t[0:nq, 0, 0:nq]))
            cp(B["attT"].w(B["attT"].t[:, half * 4:half * 4 + 4, qlo:qlo + nq]),
               V(b.t[:, :].rearrange("p (c n) -> p c n", c=4)[:, :, 0:nq], bv(b).units), eng="act")

    def build_hscr(o):
        B = OB
        for h0 in range(16):
            src = bass.AP(ext_d, o * 16 * 768 + h0 * 768, [[1, 128], [128, 5], [1, 128]])
            dma(B["H"].w(B["H"].t[:, h0, :, :]), dv(src, ("ext", o)), q="pool")
        ms(B["H"].w(B["H"].t[64:128, :, 4, 64:128]), NEG)
        ms(B["H"].w(B["H"].t[0:64, :, 0, 0:64]), NEG)
        dma(dv(hscr.ap()[o], ("hscr", o)), B["H"].w(B["H"].t[:, :, :, :].rearrange("p a b c -> p (a b c)")))

    def odd_phase_begin(o, j):
        B = OB
        dma(B["H"].w(B["H"].t[:, :, :, :].rearrange("p a b c -> p (a b c)")), dv(hscr.ap()[o], ("hscr", o)))
        if j == 0:
            ms(B["kring"].w(B["kring"].t[:, :, 512:1024], [("s", s) for s in range(4, 8)]), 0.0)
            ms(B["vring"].w(B["vring"].t[:, 4:8, :], [("s", s) for s in range(4, 8)]), 0.0)
        else:
            dma(B["kring"].w(B["kring"].t[:, :, 512:1024], [("s", s) for s in range(4, 8)]), dv(kscr.ap()[o], ("kscr", o)))
            dma(B["vring"].w(B["vring"].t[:, 4:8, :], [("s", s) for s in range(4, 8)]), dv(vscr.ap()[o], ("vscr", o)))

    def odd_phase_end(o, j, last):
        B = OB
        if not last:
            dma(dv(kscr.ap()[o], ("kscr", o)), B["kring"].w(B["kring"].t[:, :, 512:1024], [("s", s) for s in range(4, 8)]))
            dma(dv(vscr.ap()[o], ("vscr", o)), B["vring"].w(B["vring"].t[:, 4:8, :], [("s", s) for s in range(4, 8)]))

    def odd_tile(o, l, tcol, t, sample, out_rows):
        B = OB
        hT = lambda c: B["hT"].v(np.s_[:, c, :])
        rmsnorm(tcol, 0, l, hT, B["sqt"], B["rst"])
        hTa = B["hT"]
        slot = t if not sample else None
        for which in range(2):
            for half in range(2):
                b = ps()
                for c in range(4):
                    col0 = which * D + (half * 4 + c) * 128
                    for kc in range(8):
                        mm(bv(b, np.s_[:, c * 128:(c + 1) * 128]), wA.v(np.s_[:, kc, col0:col0 + 128]), hTa.v(np.s_[:, kc, :]),
                           start=(kc == 0), stop=(kc == 7))
                cp(B["raw"].v(), bv(b), eng="act")
                act(B["sq"].v(), bv(b), AF.Square)
                bm = ps()
                mm(bv(bm), ones64, B["sq"].v())
                act(B["rs2"].v(), bv(bm), AF.Ln, bias=EPS)
                act(B["rs2"].v(), B["rs2"].v(), AF.Exp, scale=-0.5)
                r3 = lambda T_: V(T_.t[:, :].rearrange("p (c n) -> p c n", c=4), T_.v().units + [ov_unit] * 0)
                if which == 0:
                    for hh_ in range(2):
                        oh_ = 1 - hh_
                        ms(B["qT"].v(np.s_[oh_ * 64:(oh_ + 1) * 64, hh_, half * 4:half * 4 + 4, :]), 0.0)
                        stt(B["qT"].v(np.s_[hh_ * 64:(hh_ + 1) * 64, hh_, half * 4:half * 4 + 4, :]),
                            B["raw"].w(r3(B["raw"]).ap[hh_ * 64:(hh_ + 1) * 64]), qnc.w(qnc.t[hh_ * 64:(hh_ + 1) * 64, o:o + 1]),
                            B["rs2"].w(r3(B["rs2"]).ap[hh_ * 64:(hh_ + 1) * 64]), ALU.mult, ALU.mult)
                else:
                    stt(B["kn"].v(np.s_[:, half * 4:half * 4 + 4, :]), B["raw"].w(r3(B["raw"]).ap), pv.w(pv.t[:, 92 + o:93 + o]),
                        B["rs2"].w(r3(B["rs2"]).ap), ALU.mult, ALU.mult)
        if not sample:
            cp(B["kring"].w(B["kring"].t[:, :, slot * 128:(slot + 1) * 128], ("s", slot)), B["kn"].v(), eng="act")
        else:
            cp(B["kown"].v(), B["kn"].v(), eng="act")
        vb = [ps(), ps()]
        for half in range(2):
            for kc in range(8):
                mm(bv(vb[half]), hTa.v(np.s_[:, kc, :]), wA.v(np.s_[:, kc, 2 * D + half * 512:2 * D + (half + 1) * 512]), start=(kc == 0), stop=(kc == 7))
        for half in range(2):
            src = V(vb[half].t[:, :].rearrange("p (h n) -> p h n", h=8), bv(vb[half]).units)
            if not sample:
                dst = B["vring"].w(B["vring"].t[:, slot, :].rearrange("p (h n) -> p h n", h=16)[:, half * 8:(half + 1) * 8, 0:64], ("s", slot))
            else:
                dst = B["vown"].w(B["vown"].t[:, :].rearrange("p (h n) -> p h n", h=16)[:, half * 8:(half + 1) * 8, 0:64])
            cp(dst, src, eng="act")
        if not sample:
            ms(B["vring"].w(B["vring"].t[:, slot, :].rearrange("p (h n) -> p h n", h=16)[:, :, 64:65], ("s", slot)), 1.0)
        else:
            ms(B["vown"].w(B["vown"].t[:, :].rearrange("p (h n) -> p h n", h=16)[:, :, 64:65]), 1.0)
        if out_rows is not None or sample:
            for half in range(2):
                cp(B["stage"].v(np.s_[:, half * 512:(half + 1) * 512]), bv(vb[half]), eng="act")
            if not sample:
                dma(dv(vp[o, out_rows:out_rows + 128, :], "vp"), B["stage"].v())
            else:
                for s in range(2):
                    dma(dv(vs[o, s, 448:512, :], "vs"), B["stage"].v(np.s_[s * 64:(s + 1) * 64, :]))
            for half in range(2):
                b = ps()
                for c in range(4):
                    tr(bv(b, np.s_[:, c * 128:(c + 1) * 128]), B["kn"].v(np.s_[:, half * 4 + c, :]), ident)
                cp(B["stage"].v(np.s_[:, half * 512:(half + 1) * 512]), bv(b), eng="act")
            if not sample:
                dma(dv(kp[o, out_rows:out_rows + 128, :], "kp"), B["stage"].v())
            else:
                for s in range(2):
                    dma(dv(ks[o, s, 448:512, :], "ks"), B["stage"].v(np.s_[s * 64:(s + 1) * 64, :]))
        if not sample:
            def kblk(kb, h):
                s_ = (t - 4 + kb) % 8
                return B["kring"].w(B["kring"].t[:, h // 2, s_ * 128:(s_ + 1) * 128], ("s", s_))
            def vblk(kb, h):
                s_ = (t - 4 + kb) % 8
                return B["vring"].w(B["vring"].t[:, s_, h * 65:(h + 1) * 65], ("s", s_))
            attend(o, 0, 128, kblk, vblk)
        else:
            for s in range(2):
                dma(dv(ks[o, s, 0:448, :], "ks"), dv(ck[o, s, 64:512, :], "ck"))
                dma(dv(vs[o, s, 0:448, :], "vs"), dv(cv[o, s, 64:512, :], "cv"))
                shift = 64 * s
                for m in range(5):
                    r_lo = max(0, 128 * m - shift); r_hi = min(512, 128 * m + 128 - shift)
                    kslot = B["kring"].w(B["kring"].t[:, :, m * 128:(m + 1) * 128], ("s", m))
                    vslot = B["vring"].w(B["vring"].t[:, m, :], ("s", m))
                    ms(vslot, 0.0)
                    if r_hi > r_lo:
                        p_lo = r_lo + shift - 128 * m
                        n = r_hi - r_lo
                        ms(B["ckst"].v(), 0.0)
                        dma(B["ckst"].w(B["ckst"].t[p_lo:p_lo + n, :]), dv(ck[o, s, r_lo:r_hi, :], "ck"))
                        for half in range(2):
                            b = ps()
                            for c in range(4):
                                tr(bv(b, np.s_[:, c * 128:(c + 1) * 128]), B["ckst"].v(np.s_[:, (half * 4 + c) * 128:(half * 4 + c + 1) * 128]), ident)
                            cp(B["kring"].w(B["kring"].t[:, half * 4:half * 4 + 4, m * 128:(m + 1) * 128], ("s", m)),
                               V(b.t[:, :].rearrange("p (c n) -> p c n", c=4), bv(b).units), eng="act")
                        dma(B["vring"].w(B["vring"].t[p_lo:p_lo + n, m, :].rearrange("p (h n) -> p h n", h=16)[:, :, 0:64], ("s", m)),
                            dv(cv[o, s, r_lo:r_hi, :].rearrange("r (h n) -> r h n", h=16), "cv"), q="pool")
                        ms(B["vring"].w(B["vring"].t[:, m, :].rearrange("p (h n) -> p h n", h=16)[:, :, 64:65], ("s", m)), 1.0)
                    else:
                        ms(kslot, 0.0)
                cp(B["kring"].w(B["kring"].t[:, :, 4 * 128 + shift:4 * 128 + shift + 64], ("s", 4)), B["kown"].v(np.s_[:, :, shift:shift + 64]), eng="act")
                cp(B["vring"].w(B["vring"].t[shift:shift + 64, 4, :], ("s", 4)), B["vown"].v(np.s_[shift:shift + 64, :]), eng="dve")
                def kblk(kb, h):
                    return B["kring"].w(B["kring"].t[:, h // 2, kb * 128:(kb + 1) * 128], ("s", kb))
                def vblk(kb, h):
                    return B["vring"].w(B["vring"].t[:, kb, h * 65:(h + 1) * 65], ("s", kb))
                attend(o, shift, 64, kblk, vblk)
        for half in range(2):
            b = ps()
            for c in range(4):
                dc = half * 4 + c
                for mc in range(8):
                    mm(bv(b, np.s_[:, c * 128:(c + 1) * 128]), wB.v(np.s_[:, mc, dc * 128:(dc + 1) * 128]), B["attT"].v(np.s_[:, mc, :]),
                       start=(mc == 0), stop=(mc == 7))
            add_to_x(tcol, half * 4, 4, b)

    def ffn_phase(l, ntiles, extra=()):
        B = FB
        ntok = ntiles * 128

        def load_slab(s):
            i = s % 2
            dma(B[f"win{i}"].v(np.s_[:, :, 0:256], "g"), dv(w_ffn_in[l][:, s * 256:(s + 1) * 256].rearrange("(kc p) n -> p kc n", p=128), "wfi"), q="pool")
            dma(B[f"win{i}"].v(np.s_[:, :, 256:512], "u"), dv(w_ffn_in[l][:, DFF + s * 256:DFF + (s + 1) * 256].rearrange("(kc p) n -> p kc n", p=128), "wfi"), q="pool")
            dma(B[f"wout{i}"].v(), dv(w_ffn_out[l][s * 256:(s + 1) * 256, :].rearrange("(fc p) n -> p fc n", p=128), "wfo"), q="pool")

        extra = list(extra)
        load_slab(0)
        load_slab(1)

        def issue_extra(k):
            for _ in range(k):
                if extra:
                    extra.pop(0)()

        def norm_a(t):
            sqt = B[f"sqt{t % 3}"]
            act(sqt.v(), x.v(np.s_[:, :, t * 128:(t + 1) * 128], ("t", t * 128)), AF.Square)
            b = ps()
            for c in range(8):
                mm(bv(b, np.s_[:, 0:128]), onesD, sqt.v(np.s_[:, c, :]), start=(c == 0), stop=(c == 7))
            return b

        def norm_b(t, b):
            rst = B[f"rst{t % 3}"]
            act(rst.v(), bv(b, np.s_[:, 0:128]), AF.Ln, bias=EPS)
            act(rst.v(), rst.v(), AF.Exp, scale=-0.5)
            for c in range(8):
                stt(B["hTall"].v(np.s_[:, c, t * 128:(t + 1) * 128], ("t", t)), x.v(np.s_[:, c, t * 128:(t + 1) * 128], ("t", t * 128)),
                    pv.w(pv.t[:, 32 + l * 8 + c:32 + l * 8 + c + 1]), rst.v(), ALU.mult, ALU.mult)

        nb_ = norm_a(0)
        for t in range(ntiles):
            nxt_ = norm_a(t + 1) if t + 1 < ntiles else None
            norm_b(t, nb_)
            nb_ = nxt_
        chunks = []
        c0 = 0
        while c0 < ntok:
            n = min(512, ntok - c0)
            chunks.append((c0, n))
            c0 += n
        it = 0
        for s in range(NSLAB):
            if 1 <= s and s + 1 < NSLAB:
                load_slab(s + 1)
            issue_extra(2)
            i = s % 2
            win = B[f"win{i}"]; wout = B[f"wout{i}"]
            for (c0, n) in chunks:
                tkeys = [("t", tt_) for tt_ in range(c0 // 128, (c0 + n) // 128)]
                ab = B[f"a{it % 2}"]; sgf = B[f"sgf{it % 2}"]
                it += 1
                for fc in range(2):
                    bg_ = ps(); bu_ = ps()
                    for kc in range(8):
                        mm(bv(bg_, np.s_[:, 0:n]), win.v(np.s_[:, kc, fc * 128:(fc + 1) * 128], "g"), B["hTall"].v(np.s_[:, kc, c0:c0 + n], tkeys), start=(kc == 0), stop=(kc == 7))
                    for kc in range(8):
                        mm(bv(bu_, np.s_[:, 0:n]), win.v(np.s_[:, kc, 256 + fc * 128:256 + (fc + 1) * 128], "u"), B["hTall"].v(np.s_[:, kc, c0:c0 + n], tkeys), start=(kc == 0), stop=(kc == 7))
                    act(sgf.v(np.s_[:, 0:n]), bv(bg_, np.s_[:, 0:n]), AF.Silu)
                    tt(ab.v(np.s_[:, fc, 0:n]), sgf.v(np.s_[:, 0:n]), bv(bu_, np.s_[:, 0:n]), ALU.mult)
                for dc in range(8):
                    by = ps()
                    for fc in range(2):
                        mm(bv(by, np.s_[:, 0:n]), wout.v(np.s_[:, fc, dc * 128:(dc + 1) * 128]), ab.v(np.s_[:, fc, 0:n]), start=(fc == 0), stop=(fc == 1))
                    xv = x.v(np.s_[:, dc, c0:c0 + n], [("t", tt_ * 128) for tt_ in range(c0 // 128, (c0 + n) // 128)])
                    tt(xv, xv, bv(by, np.s_[:, 0:n]), ALU.add)
        issue_extra(len(extra))

    def mixer_weight_thunks(l):
        if l % 2 == 0:
            a = load_w_thunks(wA.v(), w_in_ab[l // 2].rearrange("(kc p) n -> p kc n", p=128), "w_in")
            b = load_w_thunks(wB.v(), w_out_ab[l // 2].rearrange("(kc p) n -> p kc n", p=128), "w_out")
        else:
            a = load_w_thunks(wA.v(np.s_[:, :, 0:3 * D]), w_qkv[l // 2].rearrange("(kc p) n -> p kc n", p=128), "w_qkv")
            b = load_w_thunks(wB.v(), w_o_att[l // 2].rearrange("(kc p) n -> p kc n", p=128), "w_o")
        out = []
        for kc in range(8):
            out.append(a[kc]); out.append(b[kc])
        return out

    def issue_mixer_weights(l):
        for th in mixer_weight_thunks(l):
            th()

    issue_mixer_weights(0)
    phase_barrier()
    for o_ in range(NO):
        build_hscr(o_)
    phase_barrier()
    for t in range(JT):
        load_x_tile(t * 128, xp[t * 128:(t + 1) * 128, :])
    for j in range(NJ):
        last = (j == NJ - 1)
        ntiles = JT + (1 if last else 0)
        if last:
            load_x_tile(JT * 128, xs[:, :])
        for l in range(DEPTH):
            if _CFG.get("STAGE") == "io":
                break
            phase_barrier()
            if l % 2 == 0:
                e = l // 2
                specs = [dict(e=e, l=l, tcol=t * 128, sample=False, want_state=(last and t == JT - 1)) for t in range(JT)]
                if last:
                    specs.append(dict(e=e, l=l, tcol=JT * 128, sample=True, want_state=True))
                run_pipelined(specs, even_gen)
            else:
                o = l // 2
                odd_phase_begin(o, j)
                for t in range(JT):
                    g = j * JT + t
                    orow = (g - (NTSEQ - 4)) * 128 if g >= NTSEQ - 4 else None
                    odd_tile(o, l, t * 128, t, False, orow)
                odd_phase_end(o, j, last)
                if last:
                    odd_tile(o, l, JT * 128, None, True, None)
            if _CFG.get("STAGE") == "mix" or (_CFG.get("STAGE") == "mix0" and l == 0):
                continue
            phase_barrier()
            if l + 1 < DEPTH:
                ths = mixer_weight_thunks(l + 1)
            elif not last:
                ths = mixer_weight_thunks(0)
            else:
                ths = []
            ffn_phase(l, ntiles, ths)
        phase_barrier()
        for t in range(JT):
            store_y_tile(t * 128, yp[(j * JT + t) * 128:(j * JT + t + 1) * 128, :])
            if not last:
                load_x_tile(t * 128, xp[((j + 1) * JT + t) * 128:((j + 1) * JT + t + 1) * 128, :])
        if last:
            store_y_tile(JT * 128, ys[:, :])

    with nc.allow_non_contiguous_dma(reason="small strided parameter/state transfers"):
        stats = P.emit()
    return nc, stats


_BUILD_CACHE = {}


def kernel(x_prompt, x_sample, state_conv, state_gla, cache_k, cache_v,
           norm_mix, norm_ffn, w_in_ab, conv_w, gla_gk_w2, gla_gk_b, gla_onorm, w_out_ab,
           w_qkv, q_norm, k_norm, rel_bias, w_o_att, w_ffn_in, w_ffn_out):
    f = lambda a: np.ascontiguousarray(np.asarray(a, dtype=np.float32))
    x_prompt = f(x_prompt); x_sample = f(x_sample)
    BATCH, SEQ, _ = x_prompt.shape
    DEPTH = _CFG["DEPTH"]
    NE = (DEPTH + 1) // 2; NO = DEPTH // 2
    key = (SEQ, DEPTH)
    if key not in _BUILD_CACHE:
        _BUILD_CACHE[key] = build(SEQ, DEPTH)
    nc, stats = _BUILD_CACHE[key]
    state_conv = f(state_conv); state_gla = f(state_gla); cache_k = f(cache_k); cache_v = f(cache_v)
    shared = {"norm_mix": f(norm_mix), "norm_ffn": f(norm_ffn), "w_in_ab": f(w_in_ab), "conv_w": f(conv_w),
              "gk_w2": f(gla_gk_w2), "gk_b": f(gla_gk_b), "onorm": f(gla_onorm), "w_out_ab": f(w_out_ab),
              "w_qkv": f(w_qkv), "q_norm": f(q_norm), "k_norm": f(k_norm), "rel_bias": f(rel_bias),
              "w_o_att": f(w_o_att), "w_ffn_in": f(w_ffn_in), "w_ffn_out": f(w_ffn_out), "consts": _consts()}
    in_maps = []
    for c in range(8):
        b = c % 4
        m = dict(shared)
        m["xp"] = x_prompt[b]
        m["xs"] = x_sample[2 * b:2 * b + 2].reshape(128, D)
        m["sconv"] = np.ascontiguousarray(state_conv[:NE, 2 * b:2 * b + 2])
        m["sgla"] = np.ascontiguousarray(state_gla[:NE, 2 * b:2 * b + 2])
        if NO > 0:
            m["ck"] = np.ascontiguousarray(cache_k[:NO, 2 * b:2 * b + 2].reshape(NO, 2, 512, D))
            m["cv"] = np.ascontiguousarray(cache_v[:NO, 2 * b:2 * b + 2].reshape(NO, 2, 512, D))
        else:
            m["ck"] = np.zeros((1, 2, 512, D), np.float32); m["cv"] = np.zeros((1, 2, 512, D), np.float32)
        in_maps.append(m)
    ncores = _CFG.get("NCORES", 8)
    res = run_bass_kernel_spmd(nc, in_maps[:ncores], core_ids=list(range(ncores)))
    R = list(res.results)
    while len(R) < 4:
        R.append(R[0])
    y_prompt = np.stack([R[b]["yp"] for b in range(4)])
    y_sample = np.concatenate([R[b]["ys"].reshape(2, 64, D) for b in range(4)])
    conv_p = np.stack([R[b]["convp"] for b in range(4)], axis=1)
    gla_p = np.stack([R[b]["glap"] for b in range(4)], axis=1)
    k_p = np.stack([R[b]["kp"][:NO].reshape(NO, 512, 16, 64) for b in range(4)], axis=1)
    v_p = np.stack([R[b]["vp"][:NO].reshape(NO, 512, 16, 64) for b in range(4)], axis=1)
    conv_s = np.concatenate([R[b]["convs"] for b in range(4)], axis=1)
    gla_s = np.concatenate([R[b]["glas"] for b in range(4)], axis=1)
    k_s = np.concatenate([R[b]["ks"][:NO].reshape(NO, 2, 512, 16, 64) for b in range(4)], axis=1)
    v_s = np.concatenate([R[b]["vs"][:NO].reshape(NO, 2, 512, 16, 64) for b in range(4)], axis=1)
    return (y_prompt, y_sample, conv_p, gla_p, k_p, v_p, conv_s, gla_s, k_s, v_s)
```

```python
import numpy as np
import concourse.bass as bass
import concourse.mybir as mybir
from concourse.bass_utils import run_bass_kernel_spmd

F32 = mybir.dt.float32
BF16 = mybir.dt.bfloat16
I32 = mybir.dt.int32
AF = mybir.ActivationFunctionType
ALU = mybir.AluOpType

import os
DBG_DMA = bool(os.environ.get("DBG_DMA"))
SAME_ENGINE_SYNC = True
RAW_ONLY_SAME_ENGINE = bool(int(os.environ.get("RAW_ONLY", "0")))
EPOCH = 30000
N_DMA_SEMS = 8


class Unit:
    __slots__ = ("name", "w", "rs")

    def __init__(self, name):
        self.name = name
        self.w = None
        self.rs = []


class Rec:
    __slots__ = ("eng", "fn", "deps", "is_dma", "marked", "num", "sem_i", "cnt", "prev_cnt", "desc", "raw")


class V:
    __slots__ = ("ap", "units", "ro")

    def __init__(self, ap, units):
        self.ap = ap
        self.units = list(units)
        self.ro = ()


class Prog:
    ENGS = ("pe", "act", "dve", "pool", "sp")

    def __init__(self, nc):
        self.nc = nc
        self.q = {e: [] for e in self.ENGS}
        self.units = {}
        self.n_dma = 0
        self.n_dma_q = {}
        self.dma_recs = []

    def unit(self, key):
        u = self.units.get(key)
        if u is None:
            u = Unit(key)
            self.units[key] = u
        return u

    def op(self, eng, fn, r=(), w=(), dma=False):
        rec = Rec()
        rec.eng = eng
        rec.fn = fn
        rec.is_dma = dma
        rec.marked = False
        rec.num = 0
        deps = {}
        raw = set()
        for u in r:
            if u.w is not None:
                deps[id(u.w)] = u.w
                raw.add(id(u.w))
        rec.raw = raw
        for u in w:
            if u.w is not None:
                deps[id(u.w)] = u.w
            for x in u.rs:
                deps[id(x)] = x
        deps.pop(id(rec), None)
        rec.deps = list(deps.values())
        for u in r:
            u.rs.append(rec)
        for u in w:
            u.w = rec
            u.rs = []
        if dma:
            kq = self.n_dma_q.get(eng, 0)
            self.n_dma_q[eng] = kq + 1
            self.n_dma += 1
            base = {"sp": 0, "pool": 1, "act": 2}[eng] * N_DMA_SEMS
            rec.sem_i = base + kq % N_DMA_SEMS
            rec.cnt = 16 * (kq // N_DMA_SEMS + 1)
            self.dma_recs.append(rec)
        self.q[eng].append(rec)
        return rec

    def _units(self, views):
        out = []
        for v in views:
            if v is not None:
                out.extend(v.units)
        return out

    def mm(self, out, lhsT, rhs, start=True, stop=True, **kw):
        return self.op("pe", lambda e: e.matmul(out.ap, lhsT.ap, rhs.ap, start=start, stop=stop, **kw),
                       r=self._units([lhsT, rhs]), w=out.units)

    def transpose(self, out, in_, ident):
        return self.op("pe", lambda e: e.transpose(out.ap, in_.ap, ident.ap),
                       r=self._units([in_, ident]), w=out.units)

    def act(self, out, in_, func, bias=None, scale=None, eng="act"):
        kw = {}
        rs = [in_]
        if bias is not None:
            if isinstance(bias, V):
                kw["bias"] = bias.ap
                rs.append(bias)
            else:
                kw["bias"] = bias
        if scale is not None:
            if isinstance(scale, V):
                kw["scale"] = scale.ap
                rs.append(scale)
            else:
                kw["scale"] = scale
        return self.op(eng, lambda e: e.activation(out.ap, in_.ap, func, **kw),
                       r=self._units(rs), w=out.units)

    def tt(self, out, in0, in1, op, eng="dve"):
        return self.op(eng, lambda e: e.tensor_tensor(out.ap, in0.ap, in1.ap, op),
                       r=self._units([in0, in1]), w=out.units)

    def stt(self, out, in0, scalar, in1, op0, op1, eng="dve"):
        rs = [in0, in1]
        sc = scalar
        if isinstance(scalar, V):
            rs.append(scalar)
            sc = scalar.ap
        return self.op(eng, lambda e: e.scalar_tensor_tensor(out.ap, in0.ap, sc, in1.ap, op0, op1),
                       r=self._units(rs), w=out.units)

    def ts(self, out, in0, s1, s2, op0, op1=None, eng="dve"):
        rs = [in0]
        a1, a2 = s1, s2
        if isinstance(s1, V):
            rs.append(s1)
            a1 = s1.ap
        if isinstance(s2, V):
            rs.append(s2)
            a2 = s2.ap
        if op1 is None:
            return self.op(eng, lambda e: e.tensor_scalar(out.ap, in0.ap, a1, None, op0),
                           r=self._units(rs), w=out.units)
        return self.op(eng, lambda e: e.tensor_scalar(out.ap, in0.ap, a1, a2, op0, op1),
                       r=self._units(rs), w=out.units)

    def copy(self, out, in_, eng="dve"):
        if eng == "act":
            return self.op("act", lambda e: e.copy(out.ap, in_.ap), r=in_.units, w=out.units)
        return self.op(eng, lambda e: e.tensor_copy(out.ap, in_.ap), r=in_.units, w=out.units)

    def memset(self, out, val, eng="dve"):
        return self.op(eng, lambda e: e.memset(out.ap, val), r=(), w=out.units)

    def recip(self, out, in_):
        return self.op("dve", lambda e: e.reciprocal(out.ap, in_.ap), r=in_.units, w=out.units)

    def dma(self, out, in_, q="sp", **kw):
        return self.op(q, lambda e: e.dma_start(out=out.ap, in_=in_.ap, **kw),
                       r=in_.units, w=out.units, dma=True)

    def emit(self):
        nc = self.nc
        for e in self.ENGS:
            for rec in self.q[e]:
                keep = []
                for d in rec.deps:
                    if d.is_dma:
                        keep.append(d)
                    elif d.eng != rec.eng:
                        d.marked = True
                        keep.append(d)
                    elif d.eng != "pe" and (rec.is_dma or (SAME_ENGINE_SYNC and (not RAW_ONLY_SAME_ENGINE or id(d) in rec.raw))):
                        d.marked = True
                        keep.append(d)
                rec.deps = keep
        nmark = {}
        for e in self.ENGS:
            n = 0
            for rec in self.q[e]:
                if (not rec.is_dma) and rec.marked:
                    n += 1
                    rec.num = n
            nmark[e] = n
        esems = {e: [nc.alloc_semaphore(f"s_{e}_{k}") for k in range(nmark[e] // EPOCH + 1)]
                 for e in self.ENGS}
        dsems = [nc.alloc_semaphore(f"s_dma_{k}") for k in range(2 * N_DMA_SEMS)]
        stats = {}
        with nc.Block() as block:
            decos = {"pe": block.tensor, "act": block.scalar, "dve": block.vector,
                     "pool": block.gpsimd, "sp": block.sync}
            for e in self.ENGS:
                def body(eo, e=e):
                    seen = {}
                    seen_d = {}
                    nw = 0
                    for rec in self.q[e]:
                        if rec.is_dma and rec.cnt > 16:
                            if seen_d.get(rec.sem_i, 0) < rec.cnt - 16:
                                eo.wait_ge(dsems[rec.sem_i], rec.cnt - 16)
                                seen_d[rec.sem_i] = rec.cnt - 16
                                nw += 1
                        need_e = {}
                        need_d = {}
                        for d in rec.deps:
                            if d.is_dma:
                                if d.cnt > need_d.get(d.sem_i, 0):
                                    need_d[d.sem_i] = d.cnt
                            elif d.num > need_e.get(d.eng, 0):
                                need_e[d.eng] = d.num
                        for si, cnt in need_d.items():
                            if seen_d.get(si, 0) >= cnt:
                                continue
                            eo.wait_ge(dsems[si], cnt)
                            seen_d[si] = cnt
                            nw += 1
                        for de, num in need_e.items():
                            if seen.get(de, 0) >= num:
                                continue
                            ep = (num - 1) // EPOCH
                            eo.wait_ge(esems[de][ep], num - ep * EPOCH)
                            seen[de] = num
                            nw += 1
                        if rec.is_dma and DBG_DMA:
                            print("DMA", nc.get_next_instruction_name(), e, getattr(rec, "desc", None))
                        ins = rec.fn(eo)
                        if rec.is_dma:
                            ins.then_inc(dsems[rec.sem_i], 16)
                        elif rec.marked:
                            ep = (rec.num - 1) // EPOCH
                            ins.then_inc(esems[e][ep], 1)
                    if e == "sp":
                        last = {}
                        for rec in self.dma_recs:
                            last[rec.sem_i] = max(last.get(rec.sem_i, 0), rec.cnt)
                        for i, c in last.items():
                            if seen_d.get(i, 0) < c:
                                eo.wait_ge(dsems[i], c)
                    stats[e] = (len(self.q[e]), nw)
                decos[e](body)
        return stats


class Tile:
    def __init__(self, P, name, shape, dtype, psum=False):
        self.P = P
        self.name = name
        self.shape = shape
        nc = P.nc
        if psum:
            self.t = nc.alloc_psum_tensor(name, shape, dtype)
        else:
            self.t = nc.alloc_sbuf_tensor(name, shape, dtype)

    def v(self, idx=None, keys=("",)):
        ap = self.t[idx] if idx is not None else self.t[:]
        if not isinstance(keys, list):
            keys = (keys,)
        return V(ap, [self.P.unit((self.name, k)) for k in keys])


def dram_v(P, ap, key):
    return V(ap, [P.unit(("dram", key))])

D = 1024
DFF = 2816
NSLAB = DFF // 256
EPS = 1e-6
NEG = -30000.0
JT = 8
SBUF_LO = 16384 + 256
SBUF_HI = 229376 - 128

_CFG = {"DEPTH": 4}


def _consts():
    c = np.zeros((128, 7, 128), np.float32)
    idx = np.arange(128)
    c[:, 0] = np.eye(128)
    c[:, 1] = np.eye(128)[::-1]
    same = (idx[:, None] // 64) == (idx[None, :] // 64)
    c[:, 2] = (same & (idx[:, None] <= idx[None, :])).astype(np.float32)
    c[:, 3] = (same & (idx[:, None] > idx[None, :])).astype(np.float32) * (-1.0 / 16)
    c[:, 4] = 1.0 / 1024
    c[:, 5] = same.astype(np.float32) / 64.0
    c[:, 6] = 1.0 / 128
    return c.reshape(128, 7 * 128)


def build(SEQ, DEPTH):
    NJ = SEQ // (JT * 128)
    NTSEQ = SEQ // 128
    NE = (DEPTH + 1) // 2
    NO = DEPTH // 2
    TOKMAX = (JT + 1) * 128
    nc = bass.Bass("TRN2", target_bir_lowering=False)
    P = Prog(nc)

    def din(name, shape):
        return nc.dram_tensor(name, shape, F32, kind="ExternalInput").ap()

    def dout(name, shape):
        return nc.dram_tensor(name, shape, F32, kind="ExternalOutput").ap()

    xp = din("xp", [SEQ, D]); xs = din("xs", [128, D])
    sconv = din("sconv", [NE, 2, 2, 512]); sgla = din("sgla", [NE, 2, 4, 64, 128])
    ck = din("ck", [max(NO, 1), 2, 512, D]); cv = din("cv", [max(NO, 1), 2, 512, D])
    norm_mix = din("norm_mix", [4, D]); norm_ffn = din("norm_ffn", [4, D])
    w_in_ab = din("w_in_ab", [2, D, 3088]); conv_w = din("conv_w", [2, 3, 512])
    gk_w2 = din("gk_w2", [2, 16, 256]); gk_b = din("gk_b", [2, 256]); onorm = din("onorm", [2, 128])
    w_out_ab = din("w_out_ab", [2, D, D]); w_qkv = din("w_qkv", [2, D, 3 * D])
    q_norm = din("q_norm", [2, 64]); k_norm = din("k_norm", [2, 64]); rel_bias = din("rel_bias", [2, 16, 320])
    w_o_att = din("w_o_att", [2, D, D]); w_ffn_in = din("w_ffn_in", [4, D, 2 * DFF]); w_ffn_out = din("w_ffn_out", [4, DFF, D])
    consts_d = din("consts", [128, 7 * 128])
    yp = dout("yp", [SEQ, D]); ys = dout("ys", [128, D])
    convp = dout("convp", [NE, 2, 512]); glap = dout("glap", [NE, 4, 64, 128])
    kp = dout("kp", [max(NO, 1), 512, D]); vp = dout("vp", [max(NO, 1), 512, D])
    convs = dout("convs", [NE, 2, 2, 512]); glas = dout("glas", [NE, 2, 4, 64, 128])
    ks = dout("ks", [max(NO, 1), 2, 512, D]); vs = dout("vs", [max(NO, 1), 2, 512, D])
    ext_d = nc.dram_tensor("ext_scr", [max(NO, 1), 16, 768], F32)
    kscr = nc.dram_tensor("k_scr", [max(NO, 1), 128, 8, 512], BF16)
    vscr = nc.dram_tensor("v_scr", [max(NO, 1), 128, 4, 1040], BF16)
    hscr = nc.dram_tensor("h_scr", [max(NO, 1), 128, 16 * 5 * 128], BF16)

    def dv(ap, key):
        return V(ap, [P.unit(("dram", key))])

    cur = [SBUF_LO]

    class T2:
        def __init__(self, name, shape, dtype, base=None, ov=None):
            nbytes = int(np.prod(shape[1:])) * (4 if dtype in (F32, I32) else 2)
            nbytes = (nbytes + 63) // 64 * 64
            if base is None:
                off = cur[0]; cur[0] += nbytes
            else:
                off = base[0]; base[0] += nbytes
            assert off + nbytes <= SBUF_HI, (name, off, nbytes)
            self.t = nc.alloc_sbuf_tensor_at(name, shape, dtype, offset=off)
            self.name = name
            self.ov = ov

        def v(self, idx=None, keys=("",)):
            ap = self.t[idx] if idx is not None else self.t[:]
            if not isinstance(keys, list):
                keys = (keys,)
            vv = V(ap, [P.unit((self.name, k)) for k in keys])
            if self.ov is not None:
                vv.ro = [self.ov]
            return vv

        def w(self, ap, keys=("",)):
            if not isinstance(keys, list):
                keys = (keys,)
            vv = V(ap, [P.unit((self.name, k)) for k in keys])
            if self.ov is not None:
                vv.ro = [self.ov]
            return vv

    class Alias:
        def __init__(self, base, ap):
            self.base = base; self.ap0 = ap
        def v(self, idx=None, keys=("",)):
            return self.base.w(self.ap0 if idx is None else self.ap0[idx], keys)

    x = T2("x", [128, 8, TOKMAX], F32)
    cst = T2("cst", [128, 7, 128], F32)
    jbf = T2("jbf", [128, 128], BF16)
    ones1 = T2("ones1", [1, 128], F32)
    pv = T2("pv", [128, 128], F32); pvst = T2("pvst", [128, 128], F32)
    gw2b = T2("gw2b", [32, 2, 256], BF16)
    qnc = T2("qnc", [128, 2], F32)
    uhalo = T2("uhalo", [128, 2, 4, 2], F32)
    S_p = [T2(f"S_p{e}", [128, 2, 128], F32) for e in range(NE)]
    S_s = [[T2(f"S_s{e}_{s}", [128, 2, 128], F32) for s in range(2)] for e in range(NE)]
    dummy = T2("dummy", [128, 8], F32)
    wA = T2("wA", [128, 8, 3088], BF16)
    wB = T2("wB", [128, 8, D], BF16)
    R1 = cur[0]
    ov_unit = P.unit(("ov", "R1"))

    _b_io = [R1]
    xstages = [T2(f"xstage{i_}", [128, D], F32, base=_b_io, ov=ov_unit) for i_ in range(3)]
    ystages = [T2(f"ystage{i_}", [128, D], F32, base=_b_io, ov=ov_unit) for i_ in range(2)]
    xst_i = [0]
    ident = cst.v(np.s_[:, 0, :]); Jf = cst.v(np.s_[:, 1, :]); Mc = cst.v(np.s_[:, 2, :]); M2 = cst.v(np.s_[:, 3, :])
    onesD = cst.v(np.s_[:, 4, :]); ones64 = cst.v(np.s_[:, 5, :]); ones128 = cst.v(np.s_[:, 6, :])

    def phase_barrier():
        P.op("dve", lambda e: e.memset(dummy.t[:, 0:1], 0.0), r=(), w=[ov_unit, P.unit(("dummy", ""))])

    banks = [Tile(P, f"pb{i}", [128, 512], F32, psum=True) for i in range(8)]
    bank_i = [0]

    cur_pool = [None]
    pool_i = [0, 0]

    def ps():
        if cur_pool[0] is not None:
            p_ = cur_pool[0]
            b = banks[p_ * 4 + pool_i[p_] % 4]
            pool_i[p_] += 1
            return b
        b = banks[bank_i[0] % 8]
        bank_i[0] += 1
        return b

    def bv(b, idx=None):
        return V(b.t[idx] if idx is not None else b.t[:], [P.unit((b.name, ""))])

    def units_r(views):
        out = []
        for v_ in views:
            if v_ is None or not isinstance(v_, V):
                continue
            out.extend(v_.units)
            out.extend(getattr(v_, "ro", ()))
        return out

    def units_w(v_):
        return list(v_.units)

    def units_wr(v_):
        return list(getattr(v_, "ro", ()))

    def OP(eng, fn, ins, out, dma=False):
        return P.op(eng, fn, r=units_r(ins) + units_wr(out), w=units_w(out), dma=dma)

    def mm(out, lhsT, rhs, start=True, stop=True):
        return OP("pe", lambda e: e.matmul(out.ap, lhsT.ap, rhs.ap, start=start, stop=stop), [lhsT, rhs], out)

    def tr(out, in_, idv):
        return OP("pe", lambda e: e.transpose(out.ap, in_.ap, idv.ap), [in_, idv], out)

    def act(out, in_, func, bias=None, scale=None):
        kw = {}
        if bias is not None:
            kw["bias"] = bias.ap if isinstance(bias, V) else bias
        if scale is not None:
            kw["scale"] = scale.ap if isinstance(scale, V) else scale
        return OP("act", lambda e: e.activation(out.ap, in_.ap, func, **kw), [in_, bias, scale], out)

    def tt(out, a, b, op, eng="dve"):
        return OP(eng, lambda e: e.tensor_tensor(out.ap, a.ap, b.ap, op), [a, b], out)

    def stt(out, a, sc, b, op0, op1, eng="dve"):
        s_ = sc.ap if isinstance(sc, V) else sc
        return OP(eng, lambda e: e.scalar_tensor_tensor(out.ap, a.ap, s_, b.ap, op0, op1), [a, sc, b], out)

    def cp(out, in_, eng="dve"):
        if eng == "act":
            return OP("act", lambda e: e.copy(out.ap, in_.ap), [in_], out)
        return OP(eng, lambda e: e.tensor_copy(out.ap, in_.ap), [in_], out)

    def ms(out, val, eng="dve"):
        return OP(eng, lambda e: e.memset(out.ap, val), [], out)

    def rcp(out, in_):
        return OP("dve", lambda e: e.reciprocal(out.ap, in_.ap), [in_], out)

    def dma(out, in_, q="sp"):
        return OP(q, lambda e: e.dma_start(out=out.ap, in_=in_.ap), [in_], out, dma=True)

    dma(cst.v(), dv(consts_d.rearrange("p (a b) -> p a b", a=7), "consts"))
    cp(jbf.v(), Jf)
    ms(ones1.v(), 1.0)
    ms(pvst.v(), 0.0)
    dma(pvst.v(np.s_[0:32, :]), dv(norm_mix.rearrange("l (c p) -> (l c) p", p=128), "nm"))
    dma(pvst.v(np.s_[32:64, :]), dv(norm_ffn.rearrange("l (c p) -> (l c) p", p=128), "nf"))
    dma(pvst.v(np.s_[64:88, :]), dv(conv_w.rearrange("e i (c p) -> (e i c) p", p=128), "cw"))
    dma(pvst.v(np.s_[88:90, :]), dv(onorm, "onc"))
    for hh in range(2):
        dma(pvst.v(np.s_[90:92, hh * 64:(hh + 1) * 64]), dv(q_norm, "qn"))
        dma(pvst.v(np.s_[92:94, hh * 64:(hh + 1) * 64]), dv(k_norm, "kn"))
    _bt = ps()
    tr(bv(_bt, np.s_[:, 0:128]), pvst.v(), ident)
    cp(pv.v(), bv(_bt, np.s_[:, 0:128]))
    OP("dve", lambda e: e.tensor_scalar(qnc.t[:], pv.t[:, 90:92], 0.125, None, ALU.mult), [pv.v()], qnc.v())
    ms(gw2b.v(), 0.0)
    dma(gw2b.v(np.s_[0:16, :, :]), dv(gk_w2.rearrange("e r n -> r e n"), "gw2"), q="pool")
    dma(gw2b.v(np.s_[16:17, :, :]), dv(gk_b.rearrange("(o e) n -> o e n", o=1), "gb"), q="pool")
    ms(uhalo.v(), 0.0)
    for e_ in range(NE):
        ms(S_p[e_].v(), 0.0)
        for s in range(2):
            for hh in range(2):
                dma(S_s[e_][s].w(S_s[e_][s].t[hh * 64:(hh + 1) * 64, :, :]),
                    dv(sgla[e_, s].rearrange("(pr hh) k v -> hh k pr v", hh=2)[hh], "sgla"))
    for o in range(NO):
        dma(dv(ext_d.ap()[o, :, 0:64], ("ext", o)), dv(rel_bias[o, :, 0:64], "rb"))
        dma(dv(ext_d.ap()[o, :, 64:384], ("ext", o)), dv(rel_bias[o, :, 0:320], "rb"))
        dma(pvst.v(np.s_[0:16, 0:1]), dv(rel_bias[o, :, 319:320], "rb"))
        cp(pvst.v(np.s_[0:16, 1:128]), V(pvst.t[0:16, 0:1].to_broadcast([16, 127]), pvst.v().units))
        for q_ in range(3):
            dma(dv(ext_d.ap()[o, :, 384 + q_ * 128:512 + q_ * 128], ("ext", o)), pvst.v(np.s_[0:16, :]))

    def load_x_tile(tcol, src_rows):
        xstage = xstages[xst_i[0] % 3]; xst_i[0] += 1
        dma(xstage.v(), dv(src_rows, "xin"))
        for half in range(2):
            b = ps()
            for c in range(4):
                tr(bv(b, np.s_[:, c * 128:(c + 1) * 128]), xstage.v(np.s_[:, (half * 4 + c) * 128:(half * 4 + c + 1) * 128]), ident)
            cp(x.v(np.s_[:, half * 4:half * 4 + 4, tcol:tcol + 128], ("t", tcol)),
               V(b.t[:, :].rearrange("p (c n) -> p c n", c=4), bv(b).units), eng="act")

    def load_x_issue(k, src_rows):
        dma(xstages[k % 3].v(), dv(src_rows, "xin"))

    def load_x_finish(k, tcol):
        xstage = xstages[k % 3]
        for half in range(2):
            b = ps()
            for c in range(4):
                tr(bv(b, np.s_[:, c * 128:(c + 1) * 128]), xstage.v(np.s_[:, (half * 4 + c) * 128:(half * 4 + c + 1) * 128]), ident)
            cp(x.v(np.s_[:, half * 4:half * 4 + 4, tcol:tcol + 128], ("t", tcol)),
               V(b.t[:, :].rearrange("p (c n) -> p c n", c=4), bv(b).units), eng="act")

    def store_y_tile(tcol, dst_rows, own_stage=None):
        if own_stage is not None:
            xstage = ystages[own_stage % 2]
        else:
            xstage = xstages[xst_i[0] % 3]; xst_i[0] += 1
        for half in range(2):
            b = ps()
            for c in range(4):
                tr(bv(b, np.s_[:, c * 128:(c + 1) * 128]), x.v(np.s_[:, half * 4 + c, tcol:tcol + 128], ("t", tcol)), ident)
            cp(xstage.v(np.s_[:, half * 512:(half + 1) * 512]), bv(b), eng="act")
        dma(dv(dst_rows, "yout"), xstage.v())

    def rmsnorm(tcol, gcol_t, l, hT, sqt, rst):
        xt = x.v(np.s_[:, :, tcol:tcol + 128], ("t", tcol))
        act(sqt.v(), xt, AF.Square)
        b = ps()
        for c in range(8):
            mm(bv(b, np.s_[:, 0:128]), onesD, sqt.v(np.s_[:, c, :]), start=(c == 0), stop=(c == 7))
        act(rst.v(), bv(b, np.s_[:, 0:128]), AF.Ln, bias=EPS)
        act(rst.v(), rst.v(), AF.Exp, scale=-0.5)
        for c in range(8):
            stt(hT(c), x.v(np.s_[:, c, tcol:tcol + 128], ("t", tcol)), pv.w(pv.t[:, gcol_t + l * 8 + c:gcol_t + l * 8 + c + 1]), rst.v(), ALU.mult, ALU.mult)

    def add_to_x(tcol, c0, nch, b, n=128):
        xv = x.v(np.s_[:, c0:c0 + nch, tcol:tcol + n], ("t", tcol))
        tt(xv, xv, V(b.t[:, 0:nch * n].rearrange("p (c n) -> p c n", c=nch), bv(b).units), ALU.add)

    def load_w_thunks(dst, src_ap, key):
        nk = src_ap.shape[1]
        return [(lambda kc=kc: dma(_sub(dst, kc), dv(src_ap[:, kc, :], key), q="pool")) for kc in range(nk)]

    def load_w(dst, src_ap, key):
        for th in load_w_thunks(dst, src_ap, key):
            th()

    def _sub(dst, kc):
        vv = V(dst.ap[:, kc, :], dst.units)
        vv.ro = dst.ro
        return vv

    _ebase = [R1]

    def even_bufs(par):
        base = _ebase
        B = {}
        def mk(name, shape, dt):
            B[name] = T2(f"e{par}_" + name, shape, dt, base=base, ov=ov_unit)
        mk("hT", [128, 8, 128], BF16); mk("sqt", [128, 8, 128], F32); mk("rst", [128, 128], F32)
        mk("cg", [128, 4, 128], F32); mk("ub", [128, 4, 130], F32); mk("ubA", [128, 4, 66], F32); mk("ubB", [128, 4, 66], F32)
        mk("yc", [128, 4, 128], F32); mk("tmp", [128, 4, 128], F32)
        mk("mixT", [128, 8, 128], BF16); mk("sg", [128, 4, 128], F32); mk("gkT", [32, 128], BF16)
        mk("ktok", [128, 256], F32); mk("vbf", [128, 512], BF16)
        mk("e1", [128, 256], F32); mk("sp", [128, 256], F32); mk("erb", [128, 256], F32); mk("kdec", [128, 2, 256], BF16)
        mk("ebT", [128, 2, 128], F32); mk("enbT", [128, 2, 128], F32)
        mk("qe", [128, 2, 2, 128], BF16); mk("ke", [128, 2, 128], BF16)
        mk("attm", [128, 4, 128], BF16); mk("Sb0", [128, 2, 128], BF16); mk("Sb1", [128, 2, 128], BF16)
        mk("sq", [128, 512], F32); mk("rs2", [128, 512], F32); mk("y1", [128, 4, 128], F32)
        mk("cstg", [2, 512], F32); mk("qkraw", [128, 4, 128], F32)
        return B

    def odd_bufs():
        base = [R1]
        B = {}
        def mk(name, shape, dt):
            B[name] = T2("o_" + name, shape, dt, base=base, ov=ov_unit)
        mk("hT", [128, 8, 128], BF16); mk("rst", [128, 128], F32)
        mk("qT", [128, 2, 8, 128], BF16); mk("kn", [128, 8, 128], F32); mk("kown", [128, 8, 128], BF16)
        mk("raw", [128, 512], F32); mk("sq", [128, 512], F32); mk("rs2", [128, 512], F32)
        mk("kring", [128, 8, 8 * 128], BF16); mk("vring", [128, 8, 16 * 65], BF16); mk("vown", [128, 16 * 65], BF16)
        mk("H", [128, 16, 5, 128], BF16); mk("pT0", [128, 4, 5, 128], BF16); mk("pT1", [128, 4, 5, 128], BF16)
        mk("atok", [128, D], F32); mk("rc", [128, 16], F32); mk("attT", [128, 8, 128], BF16)
        mk("stage", [128, D], F32)
        B["ckst"] = B["stage"]
        B["sqt"] = Alias(B["atok"], B["atok"].t[:, :].rearrange("p (c n) -> p c n", c=8))
        return B

    def ffn_bufs():
        base = [R1]
        B = {}
        def mk(name, shape, dt):
            B[name] = T2("f_" + name, shape, dt, base=base, ov=ov_unit)
        mk("hTall", [128, 8, TOKMAX], BF16)
        for i in range(3):
            mk(f"sqt{i}", [128, 8, 128], F32); mk(f"rst{i}", [128, 128], F32)
        for i in range(2):
            mk(f"win{i}", [128, 8, 512], BF16); mk(f"wout{i}", [128, 2, D], BF16)
            mk(f"a{i}", [128, 2, 512], BF16); mk(f"sgf{i}", [128, 512], F32)
        return B

    EBS = [even_bufs(0), even_bufs(1)]; OB = odd_bufs(); FB = ffn_bufs()

    def even_gen(e, l, tcol, sample, want_state, par):
        B = EBS[par]
        hT = lambda c: B["hT"].v(np.s_[:, c, :])
        rmsnorm(tcol, 0, l, hT, B["sqt"], B["rst"])
        hTa = B["hT"]
        yield None

        def fm4(col0, nch=4):
            b = ps()
            for c in range(nch):
                for kc in range(8):
                    mm(bv(b, np.s_[:, c * 128:(c + 1) * 128]), wA.v(np.s_[:, kc, col0 + c * 128:col0 + (c + 1) * 128]), hTa.v(np.s_[:, kc, :]),
                       start=(kc == 0), stop=(kc == 7))
            return b

        def b3(b, nch=4):
            return V(b.t[:, 0:nch * 128].rearrange("p (c n) -> p c n", c=nch), bv(b).units)

        def conv_out(ub, col, dst, key):
            _b = ps()
            for c_ in range(4):
                tr(bv(_b, np.s_[0:2, c_ * 128:(c_ + 1) * 128]), ub.v(np.s_[:, c_, col:col + 2]), ident)
            cp(B["cstg"].v(), bv(_b, np.s_[0:2, :]), eng="act")
            dma(dv(dst, key), B["cstg"].v())

        bcg = fm4(0)
        cp(B["cg"].v(), b3(bcg), eng="act")
        yield None
        bhc = fm4(1024)
        if not sample:
            cp(B["ub"].v(np.s_[:, :, 0:2]), uhalo.v(np.s_[:, e, :, :]), eng="dve")
            tt(B["ub"].v(np.s_[:, :, 2:130]), b3(bhc), B["cg"].v(), ALU.mult)
            segs = [(B["ub"], 128, 0)]
        else:
            for s, ub in enumerate((B["ubA"], B["ubB"])):
                dma(B["cstg"].v(), dv(sconv[e, s], "sconv"))
                _b = ps()
                for c_ in range(4):
                    tr(bv(_b, np.s_[:, c_ * 2:c_ * 2 + 2]), B["cstg"].v(np.s_[:, c_ * 128:(c_ + 1) * 128]), cst.w(cst.t[0:2, 0, 0:2]))
                cp(ub.v(np.s_[:, :, 0:2]), V(_b.t[:, 0:8].rearrange("p (c r) -> p c r", c=4), bv(_b).units), eng="act")
                tt(ub.v(np.s_[:, :, 2:66]), V(bhc.t[:, :].rearrange("p (c n) -> p c n", c=4)[:, :, s * 64:(s + 1) * 64], bv(bhc).units),
                   B["cg"].v(np.s_[:, :, s * 64:(s + 1) * 64]), ALU.mult)
            segs = [(B["ubA"], 64, 0), (B["ubB"], 64, 64)]
        for ub, n, off in segs:
            yv = B["yc"].v(np.s_[:, :, off:off + n])
            tv = B["tmp"].v(np.s_[:, :, off:off + n])
            wv = lambda i: V(pv.t[:, 64 + (e * 3 + i) * 4:64 + (e * 3 + i) * 4 + 4].unsqueeze(2).to_broadcast([128, 4, n]), pv.v().units)
            tt(yv, ub.v(np.s_[:, :, 2:2 + n]), wv(2), ALU.mult, eng="dve")
            tt(tv, ub.v(np.s_[:, :, 1:1 + n]), wv(1), ALU.mult, eng="dve")
            tt(yv, yv, tv, ALU.add, eng="dve")
            tt(tv, ub.v(np.s_[:, :, 0:n]), wv(0), ALU.mult, eng="dve")
            tt(yv, yv, tv, ALU.add, eng="dve")
        if not sample:
            cp(uhalo.v(np.s_[:, e, :, :]), B["ub"].v(np.s_[:, :, 128:130]), eng="dve")
            if want_state:
                conv_out(B["ub"], 128, convp[e], "convp")
        else:
            for s, ub in enumerate((B["ubA"], B["ubB"])):
                conv_out(ub, 64, convs[e, s], "convs")
        yield None
        bbg = fm4(512)
        tt(B["mixT"].v(np.s_[:, 0:4, :]), b3(bbg), B["yc"].v(), ALU.mult)
        yield None
        bgk = ps()
        for kc in range(8):
            mm(bv(bgk, np.s_[0:16, 0:128]), wA.v(np.s_[:, kc, 3072:3088]), hTa.v(np.s_[:, kc, :]), start=(kc == 0), stop=(kc == 7))
        ms(B["gkT"].v(), 1.0)
        cp(B["gkT"].v(np.s_[0:16, :]), bv(bgk, np.s_[0:16, 0:128]), eng="act")
        bk = ps(); bvv = ps()
        for kc in range(8):
            mm(bv(bk, np.s_[:, 0:256]), hTa.v(np.s_[:, kc, :]), wA.v(np.s_[:, kc, 1792:2048]), start=(kc == 0), stop=(kc == 7))
        for kc in range(8):
            mm(bv(bvv), hTa.v(np.s_[:, kc, :]), wA.v(np.s_[:, kc, 2048:2560]), start=(kc == 0), stop=(kc == 7))
        cp(B["ktok"].v(), bv(bk, np.s_[:, 0:256]), eng="act")
        cp(B["vbf"].v(), bv(bvv), eng="act")
        yield None
        bqk = fm4(1536)
        cp(B["qkraw"].v(), b3(bqk), eng="act")
        yield None
        bg = fm4(2560)
        act(B["sg"].v(), b3(bg), AF.Silu)
        yield "half"

        bz = ps()
        mm(bv(bz, np.s_[:, 0:256]), B["gkT"].v(), gw2b.v(np.s_[:, e, :]))
        act(B["e1"].v(), bv(bz, np.s_[:, 0:256]), AF.Exp, scale=-1.0)
        act(B["sp"].v(), B["e1"].v(), AF.Ln, bias=1.0)
        yield None
        brb = ps()
        mm(bv(brb, np.s_[:, 0:256]), M2, B["sp"].v())
        bbT = ps()
        for p_ in range(2):
            mm(bv(bbT, np.s_[:, p_ * 128:(p_ + 1) * 128]), B["sp"].v(np.s_[:, p_ * 128:(p_ + 1) * 128]), Mc)
        act(B["erb"].v(), bv(brb, np.s_[:, 0:256]), AF.Exp)
        for cc_ in range(2):
            oc_ = 1 - cc_
            ms(B["kdec"].v(np.s_[oc_ * 64:(oc_ + 1) * 64, cc_, :]), 0.0)
            tt(B["kdec"].v(np.s_[cc_ * 64:(cc_ + 1) * 64, cc_, :]), B["ktok"].v(np.s_[cc_ * 64:(cc_ + 1) * 64, :]), B["erb"].v(np.s_[cc_ * 64:(cc_ + 1) * 64, :]), ALU.mult)
        bT3 = V(bbT.t[:, 0:256].rearrange("p (c n) -> p c n", c=2), bv(bbT).units)
        act(B["ebT"].v(), bT3, AF.Exp, scale=-1.0 / 16)
        act(B["enbT"].v(), bT3, AF.Exp, scale=1.0 / 16)
        qk3 = B["qkraw"].v()
        for hh_ in range(2):
            oh_ = 1 - hh_
            ms(B["qe"].v(np.s_[oh_ * 64:(oh_ + 1) * 64, hh_, :, :]), 0.0)
            stt(B["qe"].v(np.s_[hh_ * 64:(hh_ + 1) * 64, hh_, :, :]), B["qkraw"].v(np.s_[hh_ * 64:(hh_ + 1) * 64, 0:2, :]), 0.125,
                B["ebT"].v(np.s_[hh_ * 64:(hh_ + 1) * 64, :, :]), ALU.mult, ALU.mult)
        tt(B["ke"].v(), B["qkraw"].v(np.s_[:, 2:4, :]), B["enbT"].v(), ALU.mult)
        yield None
        batt = ps()
        for h in range(4):
            pr, hh = h // 2, h % 2
            mm(bv(batt, np.s_[:, h * 128:(h + 1) * 128]), B["ke"].v(np.s_[:, pr, :]), B["qe"].v(np.s_[:, hh, pr, :]))
        tt(B["attm"].v(), b3(batt), V(cst.t[:, 2, :].unsqueeze(1).to_broadcast([128, 4, 128]), cst.v().units), ALU.mult)
        yield "need_state"
        if not sample:
            Sc = [S_p[e], S_p[e]]
        else:
            Sc = [S_s[e][0], S_s[e][1]]
        Sb = [B["Sb0"], B["Sb1"]]

        def upd(cc):
            S = Sc[cc]
            bs = ps()
            for pr in range(2):
                mm(bv(bs, np.s_[:, pr * 256:(pr + 1) * 256]), B["kdec"].v(np.s_[:, cc, pr * 128:(pr + 1) * 128]),
                   B["vbf"].v(np.s_[:, pr * 256:(pr + 1) * 256]))
            for pr in range(2):
                for hh in range(2):
                    sv = S.v(np.s_[hh * 64:(hh + 1) * 64, pr, :])
                    stt(sv, sv, B["ebT"].v(np.s_[hh * 64:(hh + 1) * 64, pr, cc * 64 + 63:cc * 64 + 64]),
                        bv(bs, np.s_[hh * 64:(hh + 1) * 64, pr * 256 + hh * 128:pr * 256 + (hh + 1) * 128]), ALU.mult, ALU.add)

        if not sample:
            cp(Sb[0].v(), Sc[0].v(), eng="act")
            upd(0)
            cp(Sb[1].v(), Sc[1].v(), eng="act")
            upd(1)
        else:
            cp(Sb[0].v(), Sc[0].v(), eng="act")
            cp(Sb[1].v(), Sc[1].v(), eng="act")
            upd(0)
            upd(1)
        if want_state:
            if not sample:
                for hh in range(2):
                    dma(dv(glap[e].rearrange("(pr hh) k v -> hh k pr v", hh=2)[hh], "glap"), S_p[e].v(np.s_[hh * 64:(hh + 1) * 64, :, :]))
            else:
                for s in range(2):
                    for hh in range(2):
                        dma(dv(glas[e, s].rearrange("(pr hh) k v -> hh k pr v", hh=2)[hh], "glas"), S_s[e][s].v(np.s_[hh * 64:(hh + 1) * 64, :, :]))
        yield "state_done"
        bo = ps()
        for h in range(4):
            pr, hh = h // 2, h % 2
            for cc in range(2):
                ov_ = bv(bo, np.s_[:, h * 128 + cc * 64:h * 128 + (cc + 1) * 64])
                mm(ov_, B["vbf"].v(np.s_[:, h * 128:(h + 1) * 128]), B["attm"].v(np.s_[:, h, cc * 64:(cc + 1) * 64]), start=True, stop=False)
                mm(ov_, Sb[cc].v(np.s_[:, pr, :]), B["qe"].v(np.s_[:, hh, pr, cc * 64:(cc + 1) * 64]), start=False, stop=True)
        act(B["sq"].v(), bv(bo), AF.Square)
        yield None
        bm = ps()
        mm(bv(bm), ones128, B["sq"].v())
        act(B["rs2"].v(), bv(bm), AF.Ln, bias=EPS)
        act(B["rs2"].v(), B["rs2"].v(), AF.Exp, scale=-0.5)
        stt(B["y1"].v(), b3(bo), pv.w(pv.t[:, 88 + e:89 + e]), V(B["rs2"].t[:, :].rearrange("p (c n) -> p c n", c=4), B["rs2"].v().units), ALU.mult, ALU.mult)
        tt(B["mixT"].v(np.s_[:, 4:8, :]), B["y1"].v(), B["sg"].v(), ALU.mult, eng="dve")
        yield None
        for half in range(2):
            b = ps()
            for c in range(4):
                dc = half * 4 + c
                for mc in range(8):
                    mm(bv(b, np.s_[:, c * 128:(c + 1) * 128]), wB.v(np.s_[:, mc, dc * 128:(dc + 1) * 128]), B["mixT"].v(np.s_[:, mc, :]),
                       start=(mc == 0), stop=(mc == 7))
            add_to_x(tcol, half * 4, 4, b)
            yield None

    def run_pipelined(specs, genf):
        def mk(i, spec):
            return {"g": genf(par=i % 2, **spec), "par": i % 2, "half": False, "sdone": False, "done": False, "wait": False}

        def step(st):
            cur_pool[0] = st["par"]
            r = next(st["g"], "DONE")
            cur_pool[0] = None
            if r == "DONE":
                st["done"] = True; st["sdone"] = True; st["half"] = True
            elif r == "half":
                st["half"] = True
            elif r == "need_state":
                st["wait"] = True
            elif r == "state_done":
                st["sdone"] = True

        old = None
        for i, spec in enumerate(specs):
            new = mk(i, spec)
            while True:
                if new["wait"] and (old is None or old["sdone"]):
                    new["wait"] = False
                progressed = False
                if not new["done"] and not new["wait"] and not (new["half"] and old is None and False):
                    step(new); progressed = True
                if old is not None and not old["done"]:
                    step(old); progressed = True
                if old is not None and old["done"]:
                    old = None
                if new["done"] or (new["half"] and old is None):
                    break
                assert progressed
            old = None if new["done"] else new
        while old is not None and not old["done"]:
            old["wait"] = False
            step(old)

    def attend(o, qlo, nq, kblk, vblk):
        B = OB
        obanks = [banks[0], banks[1], banks[2]]
        sB = banks[3]

        def scores_a(hg):
            pT = B[f"pT{hg % 2}"]
            for hi in range(4):
                h = hg * 4 + hi
                pr, hh = h // 2, h % 2
                sA = banks[4 + hi]
                qv = B["qT"].v(np.s_[:, hh, pr, qlo:qlo + nq])
                if nq == 128:
                    mm(bv(sA), jbf.v(), B["H"].w(B["H"].t[:, h, 0:4, :].rearrange("p a b -> p (a b)")), start=True, stop=False)
                    for kb in range(1, 5):
                        sl = 4 - kb
                        mm(bv(sA, np.s_[:, sl * 128:sl * 128 + nq]), kblk(kb, h), qv, start=False, stop=(kb == 4))
                else:
                    for kb in range(1, 5):
                        sl = 4 - kb
                        tgt = bv(sA, np.s_[:, sl * 128:sl * 128 + nq])
                        mm(tgt, jbf.v(), B["H"].v(np.s_[:, h, sl, qlo:qlo + nq]), start=True, stop=False)
                        mm(tgt, kblk(kb, h), qv, start=False, stop=True)
                act(pT.w(pT.t[:, hi, 0:4, 0:nq]),
                    V(sA.t[:, :].rearrange("p (c n) -> p c n", c=4)[:, :, 0:nq], bv(sA).units), AF.Exp)

        def scores_b(hg):
            pT = B[f"pT{hg % 2}"]
            for hi in range(4):
                h = hg * 4 + hi
                pr, hh = h // 2, h % 2
                qv = B["qT"].v(np.s_[:, hh, pr, qlo:qlo + nq])
                tgt = bv(sB, np.s_[:, hi * 128:hi * 128 + nq])
                mm(tgt, jbf.v(), B["H"].v(np.s_[:, h, 4, qlo:qlo + nq]), start=True, stop=False)
                mm(tgt, kblk(0, h), qv, start=False, stop=True)
            act(pT.w(pT.t[:, :, 4, 0:nq]), V(sB.t[:, :].rearrange("p (c n) -> p c n", c=4)[:, :, 0:nq], bv(sB).units), AF.Exp)

        def pvs(hg):
            pT = B[f"pT{hg % 2}"]
            for hi in range(4):
                h = hg * 4 + hi
                ob = obanks[h // 7]
                col = (h % 7) * 65
                for kb in range(5):
                    mm(bv(ob, np.s_[0:nq, col:col + 65]), pT.w(pT.t[:, hi, 4 - kb, 0:nq]), vblk(kb, h), start=(kb == 0), stop=(kb == 4))

        scores_a(0); scores_b(0)
        for hg in range(1, 4):
            scores_a(hg)
            pvs(hg - 1)
            scores_b(hg)
        pvs(3)
        for bi, ob in enumerate(obanks):
            nh = 7 if bi < 2 else 2
            h0 = bi * 7
            o3 = V(ob.t[0:nq, 0:nh * 65].rearrange("p (h n) -> p h n", h=nh), bv(ob).units)
            rcp(B["rc"].w(B["rc"].t[0:nq, h0:h0 + nh]), V(o3.ap[:, :, 64], o3.units))
            tt(B["atok"].w(B["atok"].t[0:nq, h0 * 64:(h0 + nh) * 64].rearrange("p (h n) -> p h n", h=nh)),
               V(o3.ap[:, :, 0:64], o3.units),
               B["rc"].w(B["rc"].t[0:nq, h0:h0 + nh].unsqueeze(2).to_broadcast([nq, nh, 64])), ALU.mult)
        for half in range(2):
            b = banks[4 + half]
            for c in range(4):
                fc = half * 4 + c
                tr(bv(b, np.s_[:, c * 128:c * 128 + nq]), B["atok"].w(B["atok"].t[0:nq, fc * 128:(fc + 1) * 128]), cst.w(cst.t[0:nq, 0, 0:nq]))
            cp(B["attT"].w(B["attT"].t[:, half * 4:half * 4 + 4, qlo:qlo + nq]),
               V(b.t[:, :].rearrange("p (c n) -> p c n", c=4)[:, :, 0:nq], bv(b).units), eng="act")

    def build_hscr(o):
        B = OB
        for h0 in range(16):
            src = bass.AP(ext_d, o * 16 * 768 + h0 * 768, [[1, 128], [128, 5], [1, 128]])
            dma(B["H"].w(B["H"].t[:, h0, :, :]), dv(src, ("ext", o)), q="pool")
        ms(B["H"].w(B["H"].t[64:128, :, 4, 64:128]), NEG)
        ms(B["H"].w(B["H"].t[0:64, :, 0, 0:64]), NEG)
        dma(dv(hscr.ap()[o], ("hscr", o)), B["H"].w(B["H"].t[:, :, :, :].rearrange("p a b c -> p (a b c)")))

    def odd_phase_begin(o, j):
        B = OB
        dma(B["H"].w(B["H"].t[:, :, :, :].rearrange("p a b c -> p (a b c)")), dv(hscr.ap()[o], ("hscr", o)))
        if j == 0:
            ms(B["kring"].w(B["kring"].t[:, :, 512:1024], [("s", s) for s in range(4, 8)]), 0.0)
            ms(B["vring"].w(B["vring"].t[:, 4:8, :], [("s", s) for s in range(4, 8)]), 0.0)
        else:
            dma(B["kring"].w(B["kring"].t[:, :, 512:1024], [("s", s) for s in range(4, 8)]), dv(kscr.ap()[o], ("kscr", o)))
            dma(B["vring"].w(B["vring"].t[:, 4:8, :], [("s", s) for s in range(4, 8)]), dv(vscr.ap()[o], ("vscr", o)))

    def odd_phase_end(o, j, last):
        B = OB
        if not last:
            dma(dv(kscr.ap()[o], ("kscr", o)), B["kring"].w(B["kring"].t[:, :, 512:1024], [("s", s) for s in range(4, 8)]))
            dma(dv(vscr.ap()[o], ("vscr", o)), B["vring"].w(B["vring"].t[:, 4:8, :], [("s", s) for s in range(4, 8)]))

    def odd_tile(o, l, tcol, t, sample, out_rows):
        B = OB
        hT = lambda c: B["hT"].v(np.s_[:, c, :])
        rmsnorm(tcol, 0, l, hT, B["sqt"], B["rst"])
        hTa = B["hT"]
        slot = t if not sample else None
        for which in range(2):
            for half in range(2):
                b = ps()
                for c in range(4):
                    col0 = which * D + (half * 4 + c) * 128
                    for kc in range(8):
                        mm(bv(b, np.s_[:, c * 128:(c + 1) * 128]), wA.v(np.s_[:, kc, col0:col0 + 128]), hTa.v(np.s_[:, kc, :]),
                           start=(kc == 0), stop=(kc == 7))
                cp(B["raw"].v(), bv(b), eng="act")
                act(B["sq"].v(), bv(b), AF.Square)
                bm = ps()
                mm(bv(bm), ones64, B["sq"].v())
                act(B["rs2"].v(), bv(bm), AF.Ln, bias=EPS)
                act(B["rs2"].v(), B["rs2"].v(), AF.Exp, scale=-0.5)
                r3 = lambda T_: V(T_.t[:, :].rearrange("p (c n) -> p c n", c=4), T_.v().units + [ov_unit] * 0)
                if which == 0:
                    for hh_ in range(2):
                        oh_ = 1 - hh_
                        ms(B["qT"].v(np.s_[oh_ * 64:(oh_ + 1) * 64, hh_, half * 4:half * 4 + 4, :]), 0.0)
                        stt(B["qT"].v(np.s_[hh_ * 64:(hh_ + 1) * 64, hh_, half * 4:half * 4 + 4, :]),
                            B["raw"].w(r3(B["raw"]).ap[hh_ * 64:(hh_ + 1) * 64]), qnc.w(qnc.t[hh_ * 64:(hh_ + 1) * 64, o:o + 1]),
                            B["rs2"].w(r3(B["rs2"]).ap[hh_ * 64:(hh_ + 1) * 64]), ALU.mult, ALU.mult)
                else:
                    stt(B["kn"].v(np.s_[:, half * 4:half * 4 + 4, :]), B["raw"].w(r3(B["raw"]).ap), pv.w(pv.t[:, 92 + o:93 + o]),
                        B["rs2"].w(r3(B["rs2"]).ap), ALU.mult, ALU.mult)
        if not sample:
            cp(B["kring"].w(B["kring"].t[:, :, slot * 128:(slot + 1) * 128], ("s", slot)), B["kn"].v(), eng="act")
        else:
            cp(B["kown"].v(), B["kn"].v(), eng="act")
        vb = [ps(), ps()]
        for half in range(2):
            for kc in range(8):
                mm(bv(vb[half]), hTa.v(np.s_[:, kc, :]), wA.v(np.s_[:, kc, 2 * D + half * 512:2 * D + (half + 1) * 512]), start=(kc == 0), stop=(kc == 7))
        for half in range(2):
            src = V(vb[half].t[:, :].rearrange("p (h n) -> p h n", h=8), bv(vb[half]).units)
            if not sample:
                dst = B["vring"].w(B["vring"].t[:, slot, :].rearrange("p (h n) -> p h n", h=16)[:, half * 8:(half + 1) * 8, 0:64], ("s", slot))
            else:
                dst = B["vown"].w(B["vown"].t[:, :].rearrange("p (h n) -> p h n", h=16)[:, half * 8:(half + 1) * 8, 0:64])
            cp(dst, src, eng="act")
        if not sample:
            ms(B["vring"].w(B["vring"].t[:, slot, :].rearrange("p (h n) -> p h n", h=16)[:, :, 64:65], ("s", slot)), 1.0)
        else:
            ms(B["vown"].w(B["vown"].t[:, :].rearrange("p (h n) -> p h n", h=16)[:, :, 64:65]), 1.0)
        if out_rows is not None or sample:
            for half in range(2):
                cp(B["stage"].v(np.s_[:, half * 512:(half + 1) * 512]), bv(vb[half]), eng="act")
            if not sample:
                dma(dv(vp[o, out_rows:out_rows + 128, :], "vp"), B["stage"].v())
            else:
                for s in range(2):
                    dma(dv(vs[o, s, 448:512, :], "vs"), B["stage"].v(np.s_[s * 64:(s + 1) * 64, :]))
            for half in range(2):
                b = ps()
                for c in range(4):
                    tr(bv(b, np.s_[:, c * 128:(c + 1) * 128]), B["kn"].v(np.s_[:, half * 4 + c, :]), ident)
                cp(B["stage"].v(np.s_[:, half * 512:(half + 1) * 512]), bv(b), eng="act")
            if not sample:
                dma(dv(kp[o, out_rows:out_rows + 128, :], "kp"), B["stage"].v())
            else:
                for s in range(2):
                    dma(dv(ks[o, s, 448:512, :], "ks"), B["stage"].v(np.s_[s * 64:(s + 1) * 64, :]))
        if not sample:
            def kblk(kb, h):
                s_ = (t - 4 + kb) % 8
                return B["kring"].w(B["kring"].t[:, h // 2, s_ * 128:(s_ + 1) * 128], ("s", s_))
            def vblk(kb, h):
                s_ = (t - 4 + kb) % 8
                return B["vring"].w(B["vring"].t[:, s_, h * 65:(h + 1) * 65], ("s", s_))
            attend(o, 0, 128, kblk, vblk)
        else:
            for s in range(2):
                dma(dv(ks[o, s, 0:448, :], "ks"), dv(ck[o, s, 64:512, :], "ck"))
                dma(dv(vs[o, s, 0:448, :], "vs"), dv(cv[o, s, 64:512, :], "cv"))
                shift = 64 * s
                for m in range(5):
                    r_lo = max(0, 128 * m - shift); r_hi = min(512, 128 * m + 128 - shift)
                    kslot = B["kring"].w(B["kring"].t[:, :, m * 128:(m + 1) * 128], ("s", m))
                    vslot = B["vring"].w(B["vring"].t[:, m, :], ("s", m))
                    ms(vslot, 0.0)
                    if r_hi > r_lo:
                        p_lo = r_lo + shift - 128 * m
                        n = r_hi - r_lo
                        ms(B["ckst"].v(), 0.0)
                        dma(B["ckst"].w(B["ckst"].t[p_lo:p_lo + n, :]), dv(ck[o, s, r_lo:r_hi, :], "ck"))
                        for half in range(2):
                            b = ps()
                            for c in range(4):
                                tr(bv(b, np.s_[:, c * 128:(c + 1) * 128]), B["ckst"].v(np.s_[:, (half * 4 + c) * 128:(half * 4 + c + 1) * 128]), ident)
                            cp(B["kring"].w(B["kring"].t[:, half * 4:half * 4 + 4, m * 128:(m + 1) * 128], ("s", m)),
                               V(b.t[:, :].rearrange("p (c n) -> p c n", c=4), bv(b).units), eng="act")
                        dma(B["vring"].w(B["vring"].t[p_lo:p_lo + n, m, :].rearrange("p (h n) -> p h n", h=16)[:, :, 0:64], ("s", m)),
                            dv(cv[o, s, r_lo:r_hi, :].rearrange("r (h n) -> r h n", h=16), "cv"), q="pool")
                        ms(B["vring"].w(B["vring"].t[:, m, :].rearrange("p (h n) -> p h n", h=16)[:, :, 64:65], ("s", m)), 1.0)
                    else:
                        ms(kslot, 0.0)
                cp(B["kring"].w(B["kring"].t[:, :, 4 * 128 + shift:4 * 128 + shift + 64], ("s", 4)), B["kown"].v(np.s_[:, :, shift:shift + 64]), eng="act")
                cp(B["vring"].w(B["vring"].t[shift:shift + 64, 4, :], ("s", 4)), B["vown"].v(np.s_[shift:shift + 64, :]), eng="dve")
                def kblk(kb, h):
                    return B["kring"].w(B["kring"].t[:, h // 2, kb * 128:(kb + 1) * 128], ("s", kb))
                def vblk(kb, h):
                    return B["vring"].w(B["vring"].t[:, kb, h * 65:(h + 1) * 65], ("s", kb))
                attend(o, shift, 64, kblk, vblk)
        for half in range(2):
            b = ps()
            for c in range(4):
                dc = half * 4 + c
                for mc in range(8):
                    mm(bv(b, np.s_[:, c * 128:(c + 1) * 128]), wB.v(np.s_[:, mc, dc * 128:(dc + 1) * 128]), B["attT"].v(np.s_[:, mc, :]),
                       start=(mc == 0), stop=(mc == 7))
            add_to_x(tcol, half * 4, 4, b)

    def ffn_phase(l, ntiles, extra=()):
        B = FB
        ntok = ntiles * 128

        def load_slab(s):
            i = s % 2
            dma(B[f"win{i}"].v(np.s_[:, :, 0:256], "g"), dv(w_ffn_in[l][:, s * 256:(s + 1) * 256].rearrange("(kc p) n -> p kc n", p=128), "wfi"), q="pool")
            dma(B[f"win{i}"].v(np.s_[:, :, 256:512], "u"), dv(w_ffn_in[l][:, DFF + s * 256:DFF + (s + 1) * 256].rearrange("(kc p) n -> p kc n", p=128), "wfi"), q="pool")
            dma(B[f"wout{i}"].v(), dv(w_ffn_out[l][s * 256:(s + 1) * 256, :].rearrange("(fc p) n -> p fc n", p=128), "wfo"), q="pool")

        extra = list(extra)
        load_slab(0)
        load_slab(1)

        def issue_extra(k):
            for _ in range(k):
                if extra:
                    extra.pop(0)()

        def norm_a(t):
            sqt = B[f"sqt{t % 3}"]
            act(sqt.v(), x.v(np.s_[:, :, t * 128:(t + 1) * 128], ("t", t * 128)), AF.Square)
            b = ps()
            for c in range(8):
                mm(bv(b, np.s_[:, 0:128]), onesD, sqt.v(np.s_[:, c, :]), start=(c == 0), stop=(c == 7))
            return b

        def norm_b(t, b):
            rst = B[f"rst{t % 3}"]
            act(rst.v(), bv(b, np.s_[:, 0:128]), AF.Ln, bias=EPS)
            act(rst.v(), rst.v(), AF.Exp, scale=-0.5)
            for c in range(8):
                stt(B["hTall"].v(np.s_[:, c, t * 128:(t + 1) * 128], ("t", t)), x.v(np.s_[:, c, t * 128:(t + 1) * 128], ("t", t * 128)),
                    pv.w(pv.t[:, 32 + l * 8 + c:32 + l * 8 + c + 1]), rst.v(), ALU.mult, ALU.mult)

        nb_ = norm_a(0)
        for t in range(ntiles):
            nxt_ = norm_a(t + 1) if t + 1 < ntiles else None
            norm_b(t, nb_)
            nb_ = nxt_
        chunks = []
        c0 = 0
        while c0 < ntok:
            n = min(512, ntok - c0)
            chunks.append((c0, n))
            c0 += n
        it = 0
        for s in range(NSLAB):
            if 1 <= s and s + 1 < NSLAB:
                load_slab(s + 1)
            issue_extra(2)
            i = s % 2
            win = B[f"win{i}"]; wout = B[f"wout{i}"]
            for (c0, n) in chunks:
                tkeys = [("t", tt_) for tt_ in range(c0 // 128, (c0 + n) // 128)]
                ab = B[f"a{it % 2}"]; sgf = B[f"sgf{it % 2}"]
                it += 1
                for fc in range(2):
                    bg_ = ps(); bu_ = ps()
                    for kc in range(8):
                        mm(bv(bg_, np.s_[:, 0:n]), win.v(np.s_[:, kc, fc * 128:(fc + 1) * 128], "g"), B["hTall"].v(np.s_[:, kc, c0:c0 + n], tkeys), start=(kc == 0), stop=(kc == 7))
                    for kc in range(8):
                        mm(bv(bu_, np.s_[:, 0:n]), win.v(np.s_[:, kc, 256 + fc * 128:256 + (fc + 1) * 128], "u"), B["hTall"].v(np.s_[:, kc, c0:c0 + n], tkeys), start=(kc == 0), stop=(kc == 7))
                    act(sgf.v(np.s_[:, 0:n]), bv(bg_, np.s_[:, 0:n]), AF.Silu)
                    tt(ab.v(np.s_[:, fc, 0:n]), sgf.v(np.s_[:, 0:n]), bv(bu_, np.s_[:, 0:n]), ALU.mult)
                for dc in range(8):
                    by = ps()
                    for fc in range(2):
                        mm(bv(by, np.s_[:, 0:n]), wout.v(np.s_[:, fc, dc * 128:(dc + 1) * 128]), ab.v(np.s_[:, fc, 0:n]), start=(fc == 0), stop=(fc == 1))
                    xv = x.v(np.s_[:, dc, c0:c0 + n], [("t", tt_ * 128) for tt_ in range(c0 // 128, (c0 + n) // 128)])
                    tt(xv, xv, bv(by, np.s_[:, 0:n]), ALU.add)
        issue_extra(len(extra))

    def mixer_weight_thunks(l):
        if l % 2 == 0:
            a = load_w_thunks(wA.v(), w_in_ab[l // 2].rearrange("(kc p) n -> p kc n", p=128), "w_in")
            b = load_w_thunks(wB.v(), w_out_ab[l // 2].rearrange("(kc p) n -> p kc n", p=128), "w_out")
        else:
            a = load_w_thunks(wA.v(np.s_[:, :, 0:3 * D]), w_qkv[l // 2].rearrange("(kc p) n -> p kc n", p=128), "w_qkv")
            b = load_w_thunks(wB.v(), w_o_att[l // 2].rearrange("(kc p) n -> p kc n", p=128), "w_o")
        out = []
        for kc in range(8):
            out.append(a[kc]); out.append(b[kc])
        return out

    def issue_mixer_weights(l):
        for th in mixer_weight_thunks(l):
            th()

    issue_mixer_weights(0)
    phase_barrier()
    for o_ in range(NO):
        build_hscr(o_)
    phase_barrier()
    for t in range(JT):
        load_x_tile(t * 128, xp[t * 128:(t + 1) * 128, :])
    for j in range(NJ):
        last = (j == NJ - 1)
        ntiles = JT + (1 if last else 0)
        if last:
            load_x_tile(JT * 128, xs[:, :])
        for l in range(DEPTH):
            if _CFG.get("STAGE") == "io":
                break
            phase_barrier()
            if l % 2 == 0:
                e = l // 2
                specs = [dict(e=e, l=l, tcol=t * 128, sample=False, want_state=(last and t == JT - 1)) for t in range(JT)]
                if last:
                    specs.append(dict(e=e, l=l, tcol=JT * 128, sample=True, want_state=True))
                run_pipelined(specs, even_gen)
            else:
                o = l // 2
                odd_phase_begin(o, j)
                for t in range(JT):
                    g = j * JT + t
                    orow = (g - (NTSEQ - 4)) * 128 if g >= NTSEQ - 4 else None
                    odd_tile(o, l, t * 128, t, False, orow)
                odd_phase_end(o, j, last)
                if last:
                    odd_tile(o, l, JT * 128, None, True, None)
            if _CFG.get("STAGE") == "mix" or (_CFG.get("STAGE") == "mix0" and l == 0):
                continue
            phase_barrier()
            if l + 1 < DEPTH:
                ths = mixer_weight_thunks(l + 1)
            elif not last:
                ths = mixer_weight_thunks(0)
            else:
                ths = []
            ffn_phase(l, ntiles, ths)
        phase_barrier()
        nxt_rows = lambda t: xp[((j + 1) * JT + t) * 128:((j + 1) * JT + t + 1) * 128, :]
        if not last:
            for t in range(3):
                load_x_issue(t, nxt_rows(t))
        for t in range(JT):
            store_y_tile(t * 128, yp[(j * JT + t) * 128:(j * JT + t + 1) * 128, :], own_stage=t)
            if not last:
                load_x_finish(t, t * 128)
                if t + 3 < JT:
                    load_x_issue(t + 3, nxt_rows(t + 3))
        if last:
            store_y_tile(JT * 128, ys[:, :])

    with nc.allow_non_contiguous_dma(reason="small strided parameter/state transfers"):
        stats = P.emit()
    return nc, stats


_BUILD_CACHE = {}


def kernel(x_prompt, x_sample, state_conv, state_gla, cache_k, cache_v,
           norm_mix, norm_ffn, w_in_ab, conv_w, gla_gk_w2, gla_gk_b, gla_onorm, w_out_ab,
           w_qkv, q_norm, k_norm, rel_bias, w_o_att, w_ffn_in, w_ffn_out):
    f = lambda a: np.ascontiguousarray(np.asarray(a, dtype=np.float32))
    x_prompt = f(x_prompt); x_sample = f(x_sample)
    BATCH, SEQ, _ = x_prompt.shape
    DEPTH = _CFG["DEPTH"]
    NE = (DEPTH + 1) // 2; NO = DEPTH // 2
    key = (SEQ, DEPTH)
    if key not in _BUILD_CACHE:
        _BUILD_CACHE[key] = build(SEQ, DEPTH)
    nc, stats = _BUILD_CACHE[key]
    state_conv = f(state_conv); state_gla = f(state_gla); cache_k = f(cache_k); cache_v = f(cache_v)
    shared = {"norm_mix": f(norm_mix), "norm_ffn": f(norm_ffn), "w_in_ab": f(w_in_ab), "conv_w": f(conv_w),
              "gk_w2": f(gla_gk_w2), "gk_b": f(gla_gk_b), "onorm": f(gla_onorm), "w_out_ab": f(w_out_ab),
              "w_qkv": f(w_qkv), "q_norm": f(q_norm), "k_norm": f(k_norm), "rel_bias": f(rel_bias),
              "w_o_att": f(w_o_att), "w_ffn_in": f(w_ffn_in), "w_ffn_out": f(w_ffn_out), "consts": _consts()}
    in_maps = []
    for c in range(8):
        b = c % 4
        m = dict(shared)
        m["xp"] = x_prompt[b]
        m["xs"] = x_sample[2 * b:2 * b + 2].reshape(128, D)
        m["sconv"] = np.ascontiguousarray(state_conv[:NE, 2 * b:2 * b + 2])
        m["sgla"] = np.ascontiguousarray(state_gla[:NE, 2 * b:2 * b + 2])
        if NO > 0:
            m["ck"] = np.ascontiguousarray(cache_k[:NO, 2 * b:2 * b + 2].reshape(NO, 2, 512, D))
            m["cv"] = np.ascontiguousarray(cache_v[:NO, 2 * b:2 * b + 2].reshape(NO, 2, 512, D))
        else:
            m["ck"] = np.zeros((1, 2, 512, D), np.float32); m["cv"] = np.zeros((1, 2, 512, D), np.float32)
        in_maps.append(m)
    ncores = _CFG.get("NCORES", 8)
    res = run_bass_kernel_spmd(nc, in_maps[:ncores], core_ids=list(range(ncores)))
    R = list(res.results)
    while len(R) < 4:
        R.append(R[0])
    y_prompt = np.stack([R[b]["yp"] for b in range(4)])
    y_sample = np.concatenate([R[b]["ys"].reshape(2, 64, D) for b in range(4)])
    conv_p = np.stack([R[b]["convp"] for b in range(4)], axis=1)
    gla_p = np.stack([R[b]["glap"] for b in range(4)], axis=1)
    k_p = np.stack([R[b]["kp"][:NO].reshape(NO, 512, 16, 64) for b in range(4)], axis=1)
    v_p = np.stack([R[b]["vp"][:NO].reshape(NO, 512, 16, 64) for b in range(4)], axis=1)
    conv_s = np.concatenate([R[b]["convs"] for b in range(4)], axis=1)
    gla_s = np.concatenate([R[b]["glas"] for b in range(4)], axis=1)
    k_s = np.concatenate([R[b]["ks"][:NO].reshape(NO, 2, 512, 16, 64) for b in range(4)], axis=1)
    v_s = np.concatenate([R[b]["vs"][:NO].reshape(NO, 2, 512, 16, 64) for b in range(4)], axis=1)
    return (y_prompt, y_sample, conv_p, gla_p, k_p, v_p, conv_s, gla_s, k_s, v_s)
```

```python
import numpy as np
import concourse.bass as bass
import concourse.mybir as mybir
from concourse.bass_utils import run_bass_kernel_spmd

F32 = mybir.dt.float32
BF16 = mybir.dt.bfloat16
I32 = mybir.dt.int32
AF = mybir.ActivationFunctionType
ALU = mybir.AluOpType

import os
DBG_DMA = bool(os.environ.get("DBG_DMA"))
SAME_ENGINE_SYNC = True
RAW_ONLY_SAME_ENGINE = bool(int(os.environ.get("RAW_ONLY", "0")))
EPOCH = 30000
N_DMA_SEMS = 8


class Unit:
    __slots__ = ("name", "w", "rs")

    def __init__(self, name):
        self.name = name
        self.w = None
        self.rs = []


class Rec:
    __slots__ = ("eng", "fn", "deps", "is_dma", "marked", "num", "sem_i", "cnt", "prev_cnt", "desc", "raw")


class V:
    __slots__ = ("ap", "units", "ro")

    def __init__(self, ap, units):
        self.ap = ap
        self.units = list(units)
        self.ro = ()


class Prog:
    ENGS = ("pe", "act", "dve", "pool", "sp")

    def __init__(self, nc):
        self.nc = nc
        self.q = {e: [] for e in self.ENGS}
        self.units = {}
        self.n_dma = 0
        self.n_dma_q = {}
        self.dma_recs = []

    def unit(self, key):
        u = self.units.get(key)
        if u is None:
            u = Unit(key)
            self.units[key] = u
        return u

    def op(self, eng, fn, r=(), w=(), dma=False):
        rec = Rec()
        rec.eng = eng
        rec.fn = fn
        rec.is_dma = dma
        rec.marked = False
        rec.num = 0
        deps = {}
        raw = set()
        for u in r:
            if u.w is not None:
                deps[id(u.w)] = u.w
                raw.add(id(u.w))
        rec.raw = raw
        for u in w:
            if u.w is not None:
                deps[id(u.w)] = u.w
            for x in u.rs:
                deps[id(x)] = x
        deps.pop(id(rec), None)
        rec.deps = list(deps.values())
        for u in r:
            u.rs.append(rec)
        for u in w:
            u.w = rec
            u.rs = []
        if dma:
            kq = self.n_dma_q.get(eng, 0)
            self.n_dma_q[eng] = kq + 1
            self.n_dma += 1
            base = {"sp": 0, "pool": 1, "act": 2}[eng] * N_DMA_SEMS
            rec.sem_i = base + kq % N_DMA_SEMS
            rec.cnt = 16 * (kq // N_DMA_SEMS + 1)
            self.dma_recs.append(rec)
        self.q[eng].append(rec)
        return rec

    def _units(self, views):
        out = []
        for v in views:
            if v is not None:
                out.extend(v.units)
        return out

    def mm(self, out, lhsT, rhs, start=True, stop=True, **kw):
        return self.op("pe", lambda e: e.matmul(out.ap, lhsT.ap, rhs.ap, start=start, stop=stop, **kw),
                       r=self._units([lhsT, rhs]), w=out.units)

    def transpose(self, out, in_, ident):
        return self.op("pe", lambda e: e.transpose(out.ap, in_.ap, ident.ap),
                       r=self._units([in_, ident]), w=out.units)

    def act(self, out, in_, func, bias=None, scale=None, eng="act"):
        kw = {}
        rs = [in_]
        if bias is not None:
            if isinstance(bias, V):
                kw["bias"] = bias.ap
                rs.append(bias)
            else:
                kw["bias"] = bias
        if scale is not None:
            if isinstance(scale, V):
                kw["scale"] = scale.ap
                rs.append(scale)
            else:
                kw["scale"] = scale
        return self.op(eng, lambda e: e.activation(out.ap, in_.ap, func, **kw),
                       r=self._units(rs), w=out.units)

    def tt(self, out, in0, in1, op, eng="dve"):
        return self.op(eng, lambda e: e.tensor_tensor(out.ap, in0.ap, in1.ap, op),
                       r=self._units([in0, in1]), w=out.units)

    def stt(self, out, in0, scalar, in1, op0, op1, eng="dve"):
        rs = [in0, in1]
        sc = scalar
        if isinstance(scalar, V):
            rs.append(scalar)
            sc = scalar.ap
        return self.op(eng, lambda e: e.scalar_tensor_tensor(out.ap, in0.ap, sc, in1.ap, op0, op1),
                       r=self._units(rs), w=out.units)

    def ts(self, out, in0, s1, s2, op0, op1=None, eng="dve"):
        rs = [in0]
        a1, a2 = s1, s2
        if isinstance(s1, V):
            rs.append(s1)
            a1 = s1.ap
        if isinstance(s2, V):
            rs.append(s2)
            a2 = s2.ap
        if op1 is None:
            return self.op(eng, lambda e: e.tensor_scalar(out.ap, in0.ap, a1, None, op0),
                           r=self._units(rs), w=out.units)
        return self.op(eng, lambda e: e.tensor_scalar(out.ap, in0.ap, a1, a2, op0, op1),
                       r=self._units(rs), w=out.units)

    def copy(self, out, in_, eng="dve"):
        if eng == "act":
            return self.op("act", lambda e: e.copy(out.ap, in_.ap), r=in_.units, w=out.units)
        return self.op(eng, lambda e: e.tensor_copy(out.ap, in_.ap), r=in_.units, w=out.units)

    def memset(self, out, val, eng="dve"):
        return self.op(eng, lambda e: e.memset(out.ap, val), r=(), w=out.units)

    def recip(self, out, in_):
        return self.op("dve", lambda e: e.reciprocal(out.ap, in_.ap), r=in_.units, w=out.units)

    def dma(self, out, in_, q="sp", **kw):
        return self.op(q, lambda e: e.dma_start(out=out.ap, in_=in_.ap, **kw),
                       r=in_.units, w=out.units, dma=True)

    def emit(self):
        nc = self.nc
        for e in self.ENGS:
            for rec in self.q[e]:
                keep = []
                for d in rec.deps:
                    if d.is_dma:
                        keep.append(d)
                    elif d.eng != rec.eng:
                        d.marked = True
                        keep.append(d)
                    elif d.eng != "pe" and (rec.is_dma or (SAME_ENGINE_SYNC and (not RAW_ONLY_SAME_ENGINE or id(d) in rec.raw))):
                        d.marked = True
                        keep.append(d)
                rec.deps = keep
        nmark = {}
        for e in self.ENGS:
            n = 0
            for rec in self.q[e]:
                if (not rec.is_dma) and rec.marked:
                    n += 1
                    rec.num = n
            nmark[e] = n
        esems = {e: [nc.alloc_semaphore(f"s_{e}_{k}") for k in range(nmark[e] // EPOCH + 1)]
                 for e in self.ENGS}
        dsems = [nc.alloc_semaphore(f"s_dma_{k}") for k in range(2 * N_DMA_SEMS)]
        stats = {}
        with nc.Block() as block:
            decos = {"pe": block.tensor, "act": block.scalar, "dve": block.vector,
                     "pool": block.gpsimd, "sp": block.sync}
            for e in self.ENGS:
                def body(eo, e=e):
                    seen = {}
                    seen_d = {}
                    nw = 0
                    for rec in self.q[e]:
                        if rec.is_dma and rec.cnt > 16:
                            if seen_d.get(rec.sem_i, 0) < rec.cnt - 16:
                                eo.wait_ge(dsems[rec.sem_i], rec.cnt - 16)
                                seen_d[rec.sem_i] = rec.cnt - 16
                                nw += 1
                        need_e = {}
                        need_d = {}
                        for d in rec.deps:
                            if d.is_dma:
                                if d.cnt > need_d.get(d.sem_i, 0):
                                    need_d[d.sem_i] = d.cnt
                            elif d.num > need_e.get(d.eng, 0):
                                need_e[d.eng] = d.num
                        for si, cnt in need_d.items():
                            if seen_d.get(si, 0) >= cnt:
                                continue
                            eo.wait_ge(dsems[si], cnt)
                            seen_d[si] = cnt
                            nw += 1
                        for de, num in need_e.items():
                            if seen.get(de, 0) >= num:
                                continue
                            ep = (num - 1) // EPOCH
                            eo.wait_ge(esems[de][ep], num - ep * EPOCH)
                            seen[de] = num
                            nw += 1
                        if rec.is_dma and DBG_DMA:
                            print("DMA", nc.get_next_instruction_name(), e, getattr(rec, "desc", None))
                        ins = rec.fn(eo)
                        if rec.is_dma:
                            ins.then_inc(dsems[rec.sem_i], 16)
                        elif rec.marked:
                            ep = (rec.num - 1) // EPOCH
                            ins.then_inc(esems[e][ep], 1)
                    if e == "sp":
                        last = {}
                        for rec in self.dma_recs:
                            last[rec.sem_i] = max(last.get(rec.sem_i, 0), rec.cnt)
                        for i, c in last.items():
                            if seen_d.get(i, 0) < c:
                                eo.wait_ge(dsems[i], c)
                    stats[e] = (len(self.q[e]), nw)
                decos[e](body)
        return stats


class Tile:
    def __init__(self, P, name, shape, dtype, psum=False):
        self.P = P
        self.name = name
        self.shape = shape
        nc = P.nc
        if psum:
            self.t = nc.alloc_psum_tensor(name, shape, dtype)
        else:
            self.t = nc.alloc_sbuf_tensor(name, shape, dtype)

    def v(self, idx=None, keys=("",)):
        ap = self.t[idx] if idx is not None else self.t[:]
        if not isinstance(keys, list):
            keys = (keys,)
        return V(ap, [self.P.unit((self.name, k)) for k in keys])


def dram_v(P, ap, key):
    return V(ap, [P.unit(("dram", key))])

D = 1024
DFF = 2816
NSLAB = DFF // 256
EPS = 1e-6
NEG = -30000.0
JT = 8
SBUF_LO = 16384 + 256
SBUF_HI = 229376 - 128

_CFG = {"DEPTH": 4}


def _consts():
    c = np.zeros((128, 7, 128), np.float32)
    idx = np.arange(128)
    c[:, 0] = np.eye(128)
    c[:, 1] = np.eye(128)[::-1]
    same = (idx[:, None] // 64) == (idx[None, :] // 64)
    c[:, 2] = (same & (idx[:, None] <= idx[None, :])).astype(np.float32)
    c[:, 3] = (same & (idx[:, None] > idx[None, :])).astype(np.float32) * (-1.0 / 16)
    c[:, 4] = 1.0 / 1024
    c[:, 5] = same.astype(np.float32) / 64.0
    c[:, 6] = 1.0 / 128
    return c.reshape(128, 7 * 128)


def build(SEQ, DEPTH):
    NJ = SEQ // (JT * 128)
    NTSEQ = SEQ // 128
    NE = (DEPTH + 1) // 2
    NO = DEPTH // 2
    TOKMAX = (JT + 1) * 128
    nc = bass.Bass("TRN2", target_bir_lowering=False)
    P = Prog(nc)

    def din(name, shape):
        return nc.dram_tensor(name, shape, F32, kind="ExternalInput").ap()

    def dout(name, shape):
        return nc.dram_tensor(name, shape, F32, kind="ExternalOutput").ap()

    xp = din("xp", [SEQ, D]); xs = din("xs", [128, D])
    sconv = din("sconv", [NE, 2, 2, 512]); sgla = din("sgla", [NE, 2, 4, 64, 128])
    ck = din("ck", [max(NO, 1), 2, 512, D]); cv = din("cv", [max(NO, 1), 2, 512, D])
    norm_mix = din("norm_mix", [4, D]); norm_ffn = din("norm_ffn", [4, D])
    w_in_ab = din("w_in_ab", [2, D, 3088]); conv_w = din("conv_w", [2, 3, 512])
    gk_w2 = din("gk_w2", [2, 16, 256]); gk_b = din("gk_b", [2, 256]); onorm = din("onorm", [2, 128])
    w_out_ab = din("w_out_ab", [2, D, D]); w_qkv = din("w_qkv", [2, D, 3 * D])
    q_norm = din("q_norm", [2, 64]); k_norm = din("k_norm", [2, 64]); rel_bias = din("rel_bias", [2, 16, 320])
    w_o_att = din("w_o_att", [2, D, D]); w_ffn_in = din("w_ffn_in", [4, D, 2 * DFF]); w_ffn_out = din("w_ffn_out", [4, DFF, D])
    consts_d = din("consts", [128, 7 * 128])
    yp = dout("yp", [SEQ, D]); ys = dout("ys", [128, D])
    convp = dout("convp", [NE, 2, 512]); glap = dout("glap", [NE, 4, 64, 128])
    kp = dout("kp", [max(NO, 1), 512, D]); vp = dout("vp", [max(NO, 1), 512, D])
    convs = dout("convs", [NE, 2, 2, 512]); glas = dout("glas", [NE, 2, 4, 64, 128])
    ks = dout("ks", [max(NO, 1), 2, 512, D]); vs = dout("vs", [max(NO, 1), 2, 512, D])
    ext_d = nc.dram_tensor("ext_scr", [max(NO, 1), 16, 768], F32)
    kscr = nc.dram_tensor("k_scr", [max(NO, 1), 128, 8, 512], BF16)
    vscr = nc.dram_tensor("v_scr", [max(NO, 1), 128, 4, 1040], BF16)
    hscr = nc.dram_tensor("h_scr", [max(NO, 1), 128, 16 * 5 * 128], BF16)

    def dv(ap, key):
        return V(ap, [P.unit(("dram", key))])

    cur = [SBUF_LO]

    class T2:
        def __init__(self, name, shape, dtype, base=None, ov=None):
            nbytes = int(np.prod(shape[1:])) * (4 if dtype in (F32, I32) else 2)
            nbytes = (nbytes + 63) // 64 * 64
            if base is None:
                off = cur[0]; cur[0] += nbytes
            else:
                off = base[0]; base[0] += nbytes
            assert off + nbytes <= SBUF_HI, (name, off, nbytes)
            self.t = nc.alloc_sbuf_tensor_at(name, shape, dtype, offset=off)
            self.name = name
            self.ov = ov

        def v(self, idx=None, keys=("",)):
            ap = self.t[idx] if idx is not None else self.t[:]
            if not isinstance(keys, list):
                keys = (keys,)
            vv = V(ap, [P.unit((self.name, k)) for k in keys])
            if self.ov is not None:
                vv.ro = [self.ov]
            return vv

        def w(self, ap, keys=("",)):
            if not isinstance(keys, list):
                keys = (keys,)
            vv = V(ap, [P.unit((self.name, k)) for k in keys])
            if self.ov is not None:
                vv.ro = [self.ov]
            return vv

    class Alias:
        def __init__(self, base, ap):
            self.base = base; self.ap0 = ap
        def v(self, idx=None, keys=("",)):
            return self.base.w(self.ap0 if idx is None else self.ap0[idx], keys)

    x = T2("x", [128, 8, TOKMAX], F32)
    cst = T2("cst", [128, 7, 128], F32)
    jbf = T2("jbf", [128, 128], BF16)
    ones1 = T2("ones1", [1, 128], F32)
    pv = T2("pv", [128, 128], F32); pvst = T2("pvst", [128, 128], F32)
    gw2b = T2("gw2b", [32, 2, 256], BF16)
    qnc = T2("qnc", [128, 2], F32)
    uhalo = T2("uhalo", [128, 2, 4, 2], F32)
    S_p = [T2(f"S_p{e}", [128, 2, 128], F32) for e in range(NE)]
    S_s = [[T2(f"S_s{e}_{s}", [128, 2, 128], F32) for s in range(2)] for e in range(NE)]
    dummy = T2("dummy", [128, 8], F32)
    wA = T2("wA", [128, 8, 3088], BF16)
    wB = T2("wB", [128, 8, D], BF16)
    R1 = cur[0]
    ov_unit = P.unit(("ov", "R1"))

    _b_io = [R1]
    xstages = [T2(f"xstage{i_}", [128, D], F32, base=_b_io, ov=ov_unit) for i_ in range(3)]
    ystages = [T2(f"ystage{i_}", [128, D], F32, base=_b_io, ov=ov_unit) for i_ in range(2)]
    xst_i = [0]
    ident = cst.v(np.s_[:, 0, :]); Jf = cst.v(np.s_[:, 1, :]); Mc = cst.v(np.s_[:, 2, :]); M2 = cst.v(np.s_[:, 3, :])
    onesD = cst.v(np.s_[:, 4, :]); ones64 = cst.v(np.s_[:, 5, :]); ones128 = cst.v(np.s_[:, 6, :])

    def phase_barrier():
        P.op("dve", lambda e: e.memset(dummy.t[:, 0:1], 0.0), r=(), w=[ov_unit, P.unit(("dummy", ""))])

    banks = [Tile(P, f"pb{i}", [128, 512], F32, psum=True) for i in range(8)]
    bank_i = [0]

    cur_pool = [None]
    pool_i = [0, 0]

    def ps():
        if cur_pool[0] is not None:
            p_ = cur_pool[0]
            b = banks[p_ * 4 + pool_i[p_] % 4]
            pool_i[p_] += 1
            return b
        b = banks[bank_i[0] % 8]
        bank_i[0] += 1
        return b

    def bv(b, idx=None):
        return V(b.t[idx] if idx is not None else b.t[:], [P.unit((b.name, ""))])

    def units_r(views):
        out = []
        for v_ in views:
            if v_ is None or not isinstance(v_, V):
                continue
            out.extend(v_.units)
            out.extend(getattr(v_, "ro", ()))
        return out

    def units_w(v_):
        return list(v_.units)

    def units_wr(v_):
        return list(getattr(v_, "ro", ()))

    def OP(eng, fn, ins, out, dma=False):
        return P.op(eng, fn, r=units_r(ins) + units_wr(out), w=units_w(out), dma=dma)

    def mm(out, lhsT, rhs, start=True, stop=True):
        return OP("pe", lambda e: e.matmul(out.ap, lhsT.ap, rhs.ap, start=start, stop=stop), [lhsT, rhs], out)

    def tr(out, in_, idv):
        return OP("pe", lambda e: e.transpose(out.ap, in_.ap, idv.ap), [in_, idv], out)

    def act(out, in_, func, bias=None, scale=None):
        kw = {}
        if bias is not None:
            kw["bias"] = bias.ap if isinstance(bias, V) else bias
        if scale is not None:
            kw["scale"] = scale.ap if isinstance(scale, V) else scale
        return OP("act", lambda e: e.activation(out.ap, in_.ap, func, **kw), [in_, bias, scale], out)

    def tt(out, a, b, op, eng="dve"):
        return OP(eng, lambda e: e.tensor_tensor(out.ap, a.ap, b.ap, op), [a, b], out)

    def stt(out, a, sc, b, op0, op1, eng="dve"):
        s_ = sc.ap if isinstance(sc, V) else sc
        return OP(eng, lambda e: e.scalar_tensor_tensor(out.ap, a.ap, s_, b.ap, op0, op1), [a, sc, b], out)

    def cp(out, in_, eng="dve"):
        if eng == "act":
            return OP("act", lambda e: e.copy(out.ap, in_.ap), [in_], out)
        return OP(eng, lambda e: e.tensor_copy(out.ap, in_.ap), [in_], out)

    def ms(out, val, eng="dve"):
        return OP(eng, lambda e: e.memset(out.ap, val), [], out)

    def rcp(out, in_):
        return OP("dve", lambda e: e.reciprocal(out.ap, in_.ap), [in_], out)

    def dma(out, in_, q="sp"):
        return OP(q, lambda e: e.dma_start(out=out.ap, in_=in_.ap), [in_], out, dma=True)

    dma(cst.v(), dv(consts_d.rearrange("p (a b) -> p a b", a=7), "consts"))
    cp(jbf.v(), Jf)
    ms(ones1.v(), 1.0)
    ms(pvst.v(), 0.0)
    dma(pvst.v(np.s_[0:32, :]), dv(norm_mix.rearrange("l (c p) -> (l c) p", p=128), "nm"))
    dma(pvst.v(np.s_[32:64, :]), dv(norm_ffn.rearrange("l (c p) -> (l c) p", p=128), "nf"))
    dma(pvst.v(np.s_[64:88, :]), dv(conv_w.rearrange("e i (c p) -> (e i c) p", p=128), "cw"))
    dma(pvst.v(np.s_[88:90, :]), dv(onorm, "onc"))
    for hh in range(2):
        dma(pvst.v(np.s_[90:92, hh * 64:(hh + 1) * 64]), dv(q_norm, "qn"))
        dma(pvst.v(np.s_[92:94, hh * 64:(hh + 1) * 64]), dv(k_norm, "kn"))
    _bt = ps()
    tr(bv(_bt, np.s_[:, 0:128]), pvst.v(), ident)
    cp(pv.v(), bv(_bt, np.s_[:, 0:128]))
    OP("dve", lambda e: e.tensor_scalar(qnc.t[:], pv.t[:, 90:92], 0.125, None, ALU.mult), [pv.v()], qnc.v())
    ms(gw2b.v(), 0.0)
    dma(gw2b.v(np.s_[0:16, :, :]), dv(gk_w2.rearrange("e r n -> r e n"), "gw2"), q="pool")
    dma(gw2b.v(np.s_[16:17, :, :]), dv(gk_b.rearrange("(o e) n -> o e n", o=1), "gb"), q="pool")
    ms(uhalo.v(), 0.0)
    for e_ in range(NE):
        ms(S_p[e_].v(), 0.0)
        for s in range(2):
            for hh in range(2):
                dma(S_s[e_][s].w(S_s[e_][s].t[hh * 64:(hh + 1) * 64, :, :]),
                    dv(sgla[e_, s].rearrange("(pr hh) k v -> hh k pr v", hh=2)[hh], "sgla"))
    for o in range(NO):
        dma(dv(ext_d.ap()[o, :, 0:64], ("ext", o)), dv(rel_bias[o, :, 0:64], "rb"))
        dma(dv(ext_d.ap()[o, :, 64:384], ("ext", o)), dv(rel_bias[o, :, 0:320], "rb"))
        dma(pvst.v(np.s_[0:16, 0:1]), dv(rel_bias[o, :, 319:320], "rb"))
        cp(pvst.v(np.s_[0:16, 1:128]), V(pvst.t[0:16, 0:1].to_broadcast([16, 127]), pvst.v().units))
        for q_ in range(3):
            dma(dv(ext_d.ap()[o, :, 384 + q_ * 128:512 + q_ * 128], ("ext", o)), pvst.v(np.s_[0:16, :]))

    def load_x_tile(tcol, src_rows):
        xstage = xstages[xst_i[0] % 3]; xst_i[0] += 1
        dma(xstage.v(), dv(src_rows, "xin"))
        for half in range(2):
            b = ps()
            for c in range(4):
                tr(bv(b, np.s_[:, c * 128:(c + 1) * 128]), xstage.v(np.s_[:, (half * 4 + c) * 128:(half * 4 + c + 1) * 128]), ident)
            cp(x.v(np.s_[:, half * 4:half * 4 + 4, tcol:tcol + 128], ("t", tcol)),
               V(b.t[:, :].rearrange("p (c n) -> p c n", c=4), bv(b).units), eng="act")

    def load_x_issue(k, src_rows):
        dma(xstages[k % 3].v(), dv(src_rows, "xin"))

    def load_x_finish(k, tcol):
        xstage = xstages[k % 3]
        for half in range(2):
            b = ps()
            for c in range(4):
                tr(bv(b, np.s_[:, c * 128:(c + 1) * 128]), xstage.v(np.s_[:, (half * 4 + c) * 128:(half * 4 + c + 1) * 128]), ident)
            cp(x.v(np.s_[:, half * 4:half * 4 + 4, tcol:tcol + 128], ("t", tcol)),
               V(b.t[:, :].rearrange("p (c n) -> p c n", c=4), bv(b).units), eng="act")

    def store_y_tile(tcol, dst_rows, own_stage=None):
        if own_stage is not None:
            xstage = ystages[own_stage % 2]
        else:
            xstage = xstages[xst_i[0] % 3]; xst_i[0] += 1
        for half in range(2):
            b = ps()
            for c in range(4):
                tr(bv(b, np.s_[:, c * 128:(c + 1) * 128]), x.v(np.s_[:, half * 4 + c, tcol:tcol + 128], ("t", tcol)), ident)
            cp(xstage.v(np.s_[:, half * 512:(half + 1) * 512]), bv(b), eng="act")
        dma(dv(dst_rows, "yout"), xstage.v())

    def rmsnorm(tcol, gcol_t, l, hT, sqt, rst):
        xt = x.v(np.s_[:, :, tcol:tcol + 128], ("t", tcol))
        act(sqt.v(), xt, AF.Square)
        b = ps()
        for c in range(8):
            mm(bv(b, np.s_[:, 0:128]), onesD, sqt.v(np.s_[:, c, :]), start=(c == 0), stop=(c == 7))
        act(rst.v(), bv(b, np.s_[:, 0:128]), AF.Ln, bias=EPS)
        act(rst.v(), rst.v(), AF.Exp, scale=-0.5)
        for c in range(8):
            stt(hT(c), x.v(np.s_[:, c, tcol:tcol + 128], ("t", tcol)), pv.w(pv.t[:, gcol_t + l * 8 + c:gcol_t + l * 8 + c + 1]), rst.v(), ALU.mult, ALU.mult)

    def add_to_x(tcol, c0, nch, b, n=128):
        xv = x.v(np.s_[:, c0:c0 + nch, tcol:tcol + n], ("t", tcol))
        tt(xv, xv, V(b.t[:, 0:nch * n].rearrange("p (c n) -> p c n", c=nch), bv(b).units), ALU.add)

    def load_w_thunks(dst, src_ap, key):
        nk = src_ap.shape[1]
        return [(lambda kc=kc: dma(_sub(dst, kc), dv(src_ap[:, kc, :], key), q="pool")) for kc in range(nk)]

    def load_w(dst, src_ap, key):
        for th in load_w_thunks(dst, src_ap, key):
            th()

    def _sub(dst, kc):
        vv = V(dst.ap[:, kc, :], dst.units)
        vv.ro = dst.ro
        return vv

    _ebase = [R1]

    def even_bufs(par):
        base = _ebase
        B = {}
        def mk(name, shape, dt):
            B[name] = T2(f"e{par}_" + name, shape, dt, base=base, ov=ov_unit)
        mk("hT", [128, 8, 128], BF16); mk("sqt", [128, 8, 128], F32); mk("rst", [128, 128], F32)
        mk("cg", [128, 4, 128], F32); mk("ub", [128, 4, 130], F32); mk("ubA", [128, 4, 66], F32); mk("ubB", [128, 4, 66], F32)
        mk("yc", [128, 4, 128], F32); mk("tmp", [128, 4, 128], F32)
        mk("mixT", [128, 8, 128], BF16); mk("sg", [128, 4, 128], F32); mk("gkT", [32, 128], BF16)
        mk("ktok", [128, 256], F32); mk("vbf", [128, 512], BF16)
        mk("e1", [128, 256], F32); mk("sp", [128, 256], F32); mk("erb", [128, 256], F32); mk("kdec", [128, 2, 256], BF16)
        mk("ebT", [128, 2, 128], F32); mk("enbT", [128, 2, 128], F32)
        mk("qe", [128, 2, 2, 128], BF16); mk("ke", [128, 2, 128], BF16)
        mk("attm", [128, 4, 128], BF16); mk("Sb0", [128, 2, 128], BF16); mk("Sb1", [128, 2, 128], BF16)
        mk("sq", [128, 512], F32); mk("rs2", [128, 512], F32); mk("y1", [128, 4, 128], F32)
        mk("cstg", [2, 512], F32); mk("qkraw", [128, 4, 128], F32)
        return B

    def odd_bufs():
        base = [R1]
        B = {}
        def mk(name, shape, dt):
            B[name] = T2("o_" + name, shape, dt, base=base, ov=ov_unit)
        mk("hT", [128, 8, 128], BF16); mk("rst", [128, 128], F32)
        mk("qT", [128, 2, 8, 128], BF16); mk("kn", [128, 8, 128], F32); mk("kown", [128, 8, 128], BF16)
        mk("raw", [128, 512], F32); mk("sq", [128, 512], F32); mk("rs2", [128, 512], F32)
        mk("kring", [128, 8, 8 * 128], BF16); mk("vring", [128, 8, 16 * 65], BF16); mk("vown", [128, 16 * 65], BF16)
        mk("H", [128, 16, 5, 128], BF16); mk("pT0", [128, 4, 5, 128], BF16); mk("pT1", [128, 4, 5, 128], BF16)
        mk("atok", [128, D], F32); mk("rc", [128, 16], F32); mk("attT", [128, 8, 128], BF16)
        mk("stage", [128, D], F32)
        B["ckst"] = B["stage"]
        B["sqt"] = Alias(B["atok"], B["atok"].t[:, :].rearrange("p (c n) -> p c n", c=8))
        return B

    def ffn_bufs():
        base = [R1]
        B = {}
        def mk(name, shape, dt):
            B[name] = T2("f_" + name, shape, dt, base=base, ov=ov_unit)
        mk("hTall", [128, 8, TOKMAX], BF16)
        for i in range(3):
            mk(f"sqt{i}", [128, 8, 128], F32); mk(f"rst{i}", [128, 128], F32)
        for i in range(2):
            mk(f"win{i}", [128, 8, 512], BF16); mk(f"wout{i}", [128, 2, D], BF16)
            mk(f"a{i}", [128, 2, 512], BF16); mk(f"sgf{i}", [128, 512], F32)
        return B

    EBS = [even_bufs(0), even_bufs(1)]; OB = odd_bufs(); FB = ffn_bufs()

    def even_gen(e, l, tcol, sample, want_state, par):
        B = EBS[par]
        hT = lambda c: B["hT"].v(np.s_[:, c, :])
        rmsnorm(tcol, 0, l, hT, B["sqt"], B["rst"])
        hTa = B["hT"]
        yield None

        def fm4(col0, nch=4):
            b = ps()
            for c in range(nch):
                for kc in range(8):
                    mm(bv(b, np.s_[:, c * 128:(c + 1) * 128]), wA.v(np.s_[:, kc, col0 + c * 128:col0 + (c + 1) * 128]), hTa.v(np.s_[:, kc, :]),
                       start=(kc == 0), stop=(kc == 7))
            return b

        def b3(b, nch=4):
            return V(b.t[:, 0:nch * 128].rearrange("p (c n) -> p c n", c=nch), bv(b).units)

        def conv_out(ub, col, dst, key):
            _b = ps()
            for c_ in range(4):
                tr(bv(_b, np.s_[0:2, c_ * 128:(c_ + 1) * 128]), ub.v(np.s_[:, c_, col:col + 2]), ident)
            cp(B["cstg"].v(), bv(_b, np.s_[0:2, :]), eng="act")
            dma(dv(dst, key), B["cstg"].v())

        bcg = fm4(0)
        cp(B["cg"].v(), b3(bcg), eng="act")
        yield None
        bhc = fm4(1024)
        if not sample:
            cp(B["ub"].v(np.s_[:, :, 0:2]), uhalo.v(np.s_[:, e, :, :]), eng="dve")
            tt(B["ub"].v(np.s_[:, :, 2:130]), b3(bhc), B["cg"].v(), ALU.mult)
            segs = [(B["ub"], 128, 0)]
        else:
            for s, ub in enumerate((B["ubA"], B["ubB"])):
                dma(B["cstg"].v(), dv(sconv[e, s], "sconv"))
                _b = ps()
                for c_ in range(4):
                    tr(bv(_b, np.s_[:, c_ * 2:c_ * 2 + 2]), B["cstg"].v(np.s_[:, c_ * 128:(c_ + 1) * 128]), cst.w(cst.t[0:2, 0, 0:2]))
                cp(ub.v(np.s_[:, :, 0:2]), V(_b.t[:, 0:8].rearrange("p (c r) -> p c r", c=4), bv(_b).units), eng="act")
                tt(ub.v(np.s_[:, :, 2:66]), V(bhc.t[:, :].rearrange("p (c n) -> p c n", c=4)[:, :, s * 64:(s + 1) * 64], bv(bhc).units),
                   B["cg"].v(np.s_[:, :, s * 64:(s + 1) * 64]), ALU.mult)
            segs = [(B["ubA"], 64, 0), (B["ubB"], 64, 64)]
        for ub, n, off in segs:
            yv = B["yc"].v(np.s_[:, :, off:off + n])
            tv = B["tmp"].v(np.s_[:, :, off:off + n])
            wv = lambda i: V(pv.t[:, 64 + (e * 3 + i) * 4:64 + (e * 3 + i) * 4 + 4].unsqueeze(2).to_broadcast([128, 4, n]), pv.v().units)
            tt(yv, ub.v(np.s_[:, :, 2:2 + n]), wv(2), ALU.mult, eng="dve")
            tt(tv, ub.v(np.s_[:, :, 1:1 + n]), wv(1), ALU.mult, eng="dve")
            tt(yv, yv, tv, ALU.add, eng="dve")
            tt(tv, ub.v(np.s_[:, :, 0:n]), wv(0), ALU.mult, eng="dve")
            tt(yv, yv, tv, ALU.add, eng="dve")
        if not sample:
            cp(uhalo.v(np.s_[:, e, :, :]), B["ub"].v(np.s_[:, :, 128:130]), eng="dve")
            if want_state:
                conv_out(B["ub"], 128, convp[e], "convp")
        else:
            for s, ub in enumerate((B["ubA"], B["ubB"])):
                conv_out(ub, 64, convs[e, s], "convs")
        yield None
        bbg = fm4(512)
        tt(B["mixT"].v(np.s_[:, 0:4, :]), b3(bbg), B["yc"].v(), ALU.mult)
        yield None
        bgk = ps()
        for kc in range(8):
            mm(bv(bgk, np.s_[0:16, 0:128]), wA.v(np.s_[:, kc, 3072:3088]), hTa.v(np.s_[:, kc, :]), start=(kc == 0), stop=(kc == 7))
        ms(B["gkT"].v(), 1.0)
        cp(B["gkT"].v(np.s_[0:16, :]), bv(bgk, np.s_[0:16, 0:128]), eng="act")
        bk = ps(); bvv = ps()
        for kc in range(8):
            mm(bv(bk, np.s_[:, 0:256]), hTa.v(np.s_[:, kc, :]), wA.v(np.s_[:, kc, 1792:2048]), start=(kc == 0), stop=(kc == 7))
        for kc in range(8):
            mm(bv(bvv), hTa.v(np.s_[:, kc, :]), wA.v(np.s_[:, kc, 2048:2560]), start=(kc == 0), stop=(kc == 7))
        cp(B["ktok"].v(), bv(bk, np.s_[:, 0:256]), eng="act")
        cp(B["vbf"].v(), bv(bvv), eng="act")
        yield None
        bqk = fm4(1536)
        cp(B["qkraw"].v(), b3(bqk), eng="act")
        yield None
        bg = fm4(2560)
        act(B["sg"].v(), b3(bg), AF.Silu)
        yield "half"

        bz = ps()
        mm(bv(bz, np.s_[:, 0:256]), B["gkT"].v(), gw2b.v(np.s_[:, e, :]))
        act(B["e1"].v(), bv(bz, np.s_[:, 0:256]), AF.Exp, scale=-1.0)
        act(B["sp"].v(), B["e1"].v(), AF.Ln, bias=1.0)
        yield None
        brb = ps()
        mm(bv(brb, np.s_[:, 0:256]), M2, B["sp"].v())
        bbT = ps()
        for p_ in range(2):
            mm(bv(bbT, np.s_[:, p_ * 128:(p_ + 1) * 128]), B["sp"].v(np.s_[:, p_ * 128:(p_ + 1) * 128]), Mc)
        act(B["erb"].v(), bv(brb, np.s_[:, 0:256]), AF.Exp)
        for cc_ in range(2):
            oc_ = 1 - cc_
            ms(B["kdec"].v(np.s_[oc_ * 64:(oc_ + 1) * 64, cc_, :]), 0.0)
            tt(B["kdec"].v(np.s_[cc_ * 64:(cc_ + 1) * 64, cc_, :]), B["ktok"].v(np.s_[cc_ * 64:(cc_ + 1) * 64, :]), B["erb"].v(np.s_[cc_ * 64:(cc_ + 1) * 64, :]), ALU.mult)
        bT3 = V(bbT.t[:, 0:256].rearrange("p (c n) -> p c n", c=2), bv(bbT).units)
        act(B["ebT"].v(), bT3, AF.Exp, scale=-1.0 / 16)
        act(B["enbT"].v(), bT3, AF.Exp, scale=1.0 / 16)
        qk3 = B["qkraw"].v()
        for hh_ in range(2):
            oh_ = 1 - hh_
            ms(B["qe"].v(np.s_[oh_ * 64:(oh_ + 1) * 64, hh_, :, :]), 0.0)
            stt(B["qe"].v(np.s_[hh_ * 64:(hh_ + 1) * 64, hh_, :, :]), B["qkraw"].v(np.s_[hh_ * 64:(hh_ + 1) * 64, 0:2, :]), 0.125,
                B["ebT"].v(np.s_[hh_ * 64:(hh_ + 1) * 64, :, :]), ALU.mult, ALU.mult)
        tt(B["ke"].v(), B["qkraw"].v(np.s_[:, 2:4, :]), B["enbT"].v(), ALU.mult)
        yield None
        batt = ps()
        for h in range(4):
            pr, hh = h // 2, h % 2
            mm(bv(batt, np.s_[:, h * 128:(h + 1) * 128]), B["ke"].v(np.s_[:, pr, :]), B["qe"].v(np.s_[:, hh, pr, :]))
        tt(B["attm"].v(), b3(batt), V(cst.t[:, 2, :].unsqueeze(1).to_broadcast([128, 4, 128]), cst.v().units), ALU.mult)
        yield "need_state"
        if not sample:
            Sc = [S_p[e], S_p[e]]
        else:
            Sc = [S_s[e][0], S_s[e][1]]
        Sb = [B["Sb0"], B["Sb1"]]

        def upd(cc):
            S = Sc[cc]
            bs = ps()
            for pr in range(2):
                mm(bv(bs, np.s_[:, pr * 256:(pr + 1) * 256]), B["kdec"].v(np.s_[:, cc, pr * 128:(pr + 1) * 128]),
                   B["vbf"].v(np.s_[:, pr * 256:(pr + 1) * 256]))
            for pr in range(2):
                for hh in range(2):
                    sv = S.v(np.s_[hh * 64:(hh + 1) * 64, pr, :])
                    stt(sv, sv, B["ebT"].v(np.s_[hh * 64:(hh + 1) * 64, pr, cc * 64 + 63:cc * 64 + 64]),
                        bv(bs, np.s_[hh * 64:(hh + 1) * 64, pr * 256 + hh * 128:pr * 256 + (hh + 1) * 128]), ALU.mult, ALU.add)

        if not sample:
            cp(Sb[0].v(), Sc[0].v(), eng="act")
            upd(0)
            cp(Sb[1].v(), Sc[1].v(), eng="act")
            upd(1)
        else:
            cp(Sb[0].v(), Sc[0].v(), eng="act")
            cp(Sb[1].v(), Sc[1].v(), eng="act")
            upd(0)
            upd(1)
        if want_state:
            if not sample:
                for hh in range(2):
                    dma(dv(glap[e].rearrange("(pr hh) k v -> hh k pr v", hh=2)[hh], "glap"), S_p[e].v(np.s_[hh * 64:(hh + 1) * 64, :, :]))
            else:
                for s in range(2):
                    for hh in range(2):
                        dma(dv(glas[e, s].rearrange("(pr hh) k v -> hh k pr v", hh=2)[hh], "glas"), S_s[e][s].v(np.s_[hh * 64:(hh + 1) * 64, :, :]))
        yield "state_done"
        bo = ps()
        for h in range(4):
            pr, hh = h // 2, h % 2
            for cc in range(2):
                ov_ = bv(bo, np.s_[:, h * 128 + cc * 64:h * 128 + (cc + 1) * 64])
                mm(ov_, B["vbf"].v(np.s_[:, h * 128:(h + 1) * 128]), B["attm"].v(np.s_[:, h, cc * 64:(cc + 1) * 64]), start=True, stop=False)
                mm(ov_, Sb[cc].v(np.s_[:, pr, :]), B["qe"].v(np.s_[:, hh, pr, cc * 64:(cc + 1) * 64]), start=False, stop=True)
        act(B["sq"].v(), bv(bo), AF.Square)
        yield None
        bm = ps()
        mm(bv(bm), ones128, B["sq"].v())
        act(B["rs2"].v(), bv(bm), AF.Ln, bias=EPS)
        act(B["rs2"].v(), B["rs2"].v(), AF.Exp, scale=-0.5)
        stt(B["y1"].v(), b3(bo), pv.w(pv.t[:, 88 + e:89 + e]), V(B["rs2"].t[:, :].rearrange("p (c n) -> p c n", c=4), B["rs2"].v().units), ALU.mult, ALU.mult)
        tt(B["mixT"].v(np.s_[:, 4:8, :]), B["y1"].v(), B["sg"].v(), ALU.mult, eng="dve")
        yield None
        for half in range(2):
            b = ps()
            for c in range(4):
                dc = half * 4 + c
                for mc in range(8):
                    mm(bv(b, np.s_[:, c * 128:(c + 1) * 128]), wB.v(np.s_[:, mc, dc * 128:(dc + 1) * 128]), B["mixT"].v(np.s_[:, mc, :]),
                       start=(mc == 0), stop=(mc == 7))
            add_to_x(tcol, half * 4, 4, b)
            yield None

    def run_pipelined(specs, genf):
        def mk(i, spec):
            return {"g": genf(par=i % 2, **spec), "par": i % 2, "half": False, "sdone": False, "done": False, "wait": False}

        def step(st):
            cur_pool[0] = st["par"]
            r = next(st["g"], "DONE")
            cur_pool[0] = None
            if r == "DONE":
                st["done"] = True; st["sdone"] = True; st["half"] = True
            elif r == "half":
                st["half"] = True
            elif r == "need_state":
                st["wait"] = True
            elif r == "state_done":
                st["sdone"] = True

        old = None
        for i, spec in enumerate(specs):
            new = mk(i, spec)
            while True:
                if new["wait"] and (old is None or old["sdone"]):
                    new["wait"] = False
                progressed = False
                if not new["done"] and not new["wait"] and not (new["half"] and old is None and False):
                    step(new); progressed = True
                if old is not None and not old["done"]:
                    step(old); progressed = True
                if old is not None and old["done"]:
                    old = None
                if new["done"] or (new["half"] and old is None):
                    break
                assert progressed
            old = None if new["done"] else new
        while old is not None and not old["done"]:
            old["wait"] = False
            step(old)

    def attend(o, qlo, nq, kblk, vblk):
        B = OB
        obanks = [banks[0], banks[1], banks[2]]
        sB = banks[3]

        def scores_a(hg):
            pT = B[f"pT{hg % 2}"]
            for hi in range(4):
                h = hg * 4 + hi
                pr, hh = h // 2, h % 2
                sA = banks[4 + hi]
                qv = B["qT"].v(np.s_[:, hh, pr, qlo:qlo + nq])
                if nq == 128:
                    mm(bv(sA), jbf.v(), B["H"].w(B["H"].t[:, h, 0:4, :].rearrange("p a b -> p (a b)")), start=True, stop=False)
                    for kb in range(1, 5):
                        sl = 4 - kb
                        mm(bv(sA, np.s_[:, sl * 128:sl * 128 + nq]), kblk(kb, h), qv, start=False, stop=(kb == 4))
                else:
                    for kb in range(1, 5):
                        sl = 4 - kb
                        tgt = bv(sA, np.s_[:, sl * 128:sl * 128 + nq])
                        mm(tgt, jbf.v(), B["H"].v(np.s_[:, h, sl, qlo:qlo + nq]), start=True, stop=False)
                        mm(tgt, kblk(kb, h), qv, start=False, stop=True)
                act(pT.w(pT.t[:, hi, 0:4, 0:nq]),
                    V(sA.t[:, :].rearrange("p (c n) -> p c n", c=4)[:, :, 0:nq], bv(sA).units), AF.Exp)

        def scores_b(hg):
            pT = B[f"pT{hg % 2}"]
            for hi in range(4):
                h = hg * 4 + hi
                pr, hh = h // 2, h % 2
                qv = B["qT"].v(np.s_[:, hh, pr, qlo:qlo + nq])
                tgt = bv(sB, np.s_[:, hi * 128:hi * 128 + nq])
                mm(tgt, jbf.v(), B["H"].v(np.s_[:, h, 4, qlo:qlo + nq]), start=True, stop=False)
                mm(tgt, kblk(0, h), qv, start=False, stop=True)
            act(pT.w(pT.t[:, :, 4, 0:nq]), V(sB.t[:, :].rearrange("p (c n) -> p c n", c=4)[:, :, 0:nq], bv(sB).units), AF.Exp)

        def pvs(hg):
            pT = B[f"pT{hg % 2}"]
            for hi in range(4):
                h = hg * 4 + hi
                ob = obanks[h // 7]
                col = (h % 7) * 65
                for kb in range(5):
                    mm(bv(ob, np.s_[0:nq, col:col + 65]), pT.w(pT.t[:, hi, 4 - kb, 0:nq]), vblk(kb, h), start=(kb == 0), stop=(kb == 4))

        scores_a(0); scores_b(0)
        for hg in range(1, 4):
            scores_a(hg)
            pvs(hg - 1)
            scores_b(hg)
        pvs(3)
        for bi, ob in enumerate(obanks):
            nh = 7 if bi < 2 else 2
            h0 = bi * 7
            o3 = V(ob.t[0:nq, 0:nh * 65].rearrange("p (h n) -> p h n", h=nh), bv(ob).units)
            rcp(B["rc"].w(B["rc"].t[0:nq, h0:h0 + nh]), V(o3.ap[:, :, 64], o3.units))
            tt(B["atok"].w(B["atok"].t[0:nq, h0 * 64:(h0 + nh) * 64].rearrange("p (h n) -> p h n", h=nh)),
               V(o3.ap[:, :, 0:64], o3.units),
               B["rc"].w(B["rc"].t[0:nq, h0:h0 + nh].unsqueeze(2).to_broadcast([nq, nh, 64])), ALU.mult)
        for half in range(2):
            b = banks[4 + half]
            for c in range(4):
                fc = half * 4 + c
                tr(bv(b, np.s_[:, c * 128:c * 128 + nq]), B["atok"].w(B["atok"].t[0:nq, fc * 128:(fc + 1) * 128]), cst.w(cst.t[0:nq, 0, 0:nq]))
            cp(B["attT"].w(B["attT"].t[:, half * 4:half * 4 + 4, qlo:qlo + nq]),
               V(b.t[:, :].rearrange("p (c n) -> p c n", c=4)[:, :, 0:nq], bv(b).units), eng="act")

    def build_hscr(o):
        B = OB
        for h0 in range(16):
            src = bass.AP(ext_d, o * 16 * 768 + h0 * 768, [[1, 128], [128, 5], [1, 128]])
            dma(B["H"].w(B["H"].t[:, h0, :, :]), dv(src, ("ext", o)), q="pool")
        ms(B["H"].w(B["H"].t[64:128, :, 4, 64:128]), NEG)
        ms(B["H"].w(B["H"].t[0:64, :, 0, 0:64]), NEG)
        dma(dv(hscr.ap()[o], ("hscr", o)), B["H"].w(B["H"].t[:, :, :, :].rearrange("p a b c -> p (a b c)")))

    def odd_phase_begin(o, j):
        B = OB
        dma(B["H"].w(B["H"].t[:, :, :, :].rearrange("p a b c -> p (a b c)")), dv(hscr.ap()[o], ("hscr", o)))
        if j == 0:
            ms(B["kring"].w(B["kring"].t[:, :, 512:1024], [("s", s) for s in range(4, 8)]), 0.0)
            ms(B["vring"].w(B["vring"].t[:, 4:8, :], [("s", s) for s in range(4, 8)]), 0.0)
        else:
            dma(B["kring"].w(B["kring"].t[:, :, 512:1024], [("s", s) for s in range(4, 8)]), dv(kscr.ap()[o], ("kscr", o)))
            dma(B["vring"].w(B["vring"].t[:, 4:8, :], [("s", s) for s in range(4, 8)]), dv(vscr.ap()[o], ("vscr", o)))

    def odd_phase_end(o, j, last):
        B = OB
        if not last:
            dma(dv(kscr.ap()[o], ("kscr", o)), B["kring"].w(B["kring"].t[:, :, 512:1024], [("s", s) for s in range(4, 8)]))
            dma(dv(vscr.ap()[o], ("vscr", o)), B["vring"].w(B["vring"].t[:, 4:8, :], [("s", s) for s in range(4, 8)]))

    def odd_tile(o, l, tcol, t, sample, out_rows):
        B = OB
        hT = lambda c: B["hT"].v(np.s_[:, c, :])
        rmsnorm(tcol, 0, l, hT, B["sqt"], B["rst"])
        hTa = B["hT"]
        slot = t if not sample else None
        for which in range(2):
            for half in range(2):
                b = ps()
                for c in range(4):
                    col0 = which * D + (half * 4 + c) * 128
                    for kc in range(8):
                        mm(bv(b, np.s_[:, c * 128:(c + 1) * 128]), wA.v(np.s_[:, kc, col0:col0 + 128]), hTa.v(np.s_[:, kc, :]),
                           start=(kc == 0), stop=(kc == 7))
                cp(B["raw"].v(), bv(b), eng="act")
                act(B["sq"].v(), bv(b), AF.Square)
                bm = ps()
                mm(bv(bm), ones64, B["sq"].v())
                act(B["rs2"].v(), bv(bm), AF.Ln, bias=EPS)
                act(B["rs2"].v(), B["rs2"].v(), AF.Exp, scale=-0.5)
                r3 = lambda T_: V(T_.t[:, :].rearrange("p (c n) -> p c n", c=4), T_.v().units + [ov_unit] * 0)
                if which == 0:
                    for hh_ in range(2):
                        oh_ = 1 - hh_
                        ms(B["qT"].v(np.s_[oh_ * 64:(oh_ + 1) * 64, hh_, half * 4:half * 4 + 4, :]), 0.0)
                        stt(B["qT"].v(np.s_[hh_ * 64:(hh_ + 1) * 64, hh_, half * 4:half * 4 + 4, :]),
                            B["raw"].w(r3(B["raw"]).ap[hh_ * 64:(hh_ + 1) * 64]), qnc.w(qnc.t[hh_ * 64:(hh_ + 1) * 64, o:o + 1]),
                            B["rs2"].w(r3(B["rs2"]).ap[hh_ * 64:(hh_ + 1) * 64]), ALU.mult, ALU.mult)
                else:
                    stt(B["kn"].v(np.s_[:, half * 4:half * 4 + 4, :]), B["raw"].w(r3(B["raw"]).ap), pv.w(pv.t[:, 92 + o:93 + o]),
                        B["rs2"].w(r3(B["rs2"]).ap), ALU.mult, ALU.mult)
        if not sample:
            cp(B["kring"].w(B["kring"].t[:, :, slot * 128:(slot + 1) * 128], ("s", slot)), B["kn"].v(), eng="act")
        else:
            cp(B["kown"].v(), B["kn"].v(), eng="act")
        vb = [ps(), ps()]
        for half in range(2):
            for kc in range(8):
                mm(bv(vb[half]), hTa.v(np.s_[:, kc, :]), wA.v(np.s_[:, kc, 2 * D + half * 512:2 * D + (half + 1) * 512]), start=(kc == 0), stop=(kc == 7))
        for half in range(2):
            src = V(vb[half].t[:, :].rearrange("p (h n) -> p h n", h=8), bv(vb[half]).units)
            if not sample:
                dst = B["vring"].w(B["vring"].t[:, slot, :].rearrange("p (h n) -> p h n", h=16)[:, half * 8:(half + 1) * 8, 0:64], ("s", slot))
            else:
                dst = B["vown"].w(B["vown"].t[:, :].rearrange("p (h n) -> p h n", h=16)[:, half * 8:(half + 1) * 8, 0:64])
            cp(dst, src, eng="act")
        if not sample:
            ms(B["vring"].w(B["vring"].t[:, slot, :].rearrange("p (h n) -> p h n", h=16)[:, :, 64:65], ("s", slot)), 1.0)
        else:
            ms(B["vown"].w(B["vown"].t[:, :].rearrange("p (h n) -> p h n", h=16)[:, :, 64:65]), 1.0)
        if out_rows is not None or sample:
            for half in range(2):
                cp(B["stage"].v(np.s_[:, half * 512:(half + 1) * 512]), bv(vb[half]), eng="act")
            if not sample:
                dma(dv(vp[o, out_rows:out_rows + 128, :], "vp"), B["stage"].v())
            else:
                for s in range(2):
                    dma(dv(vs[o, s, 448:512, :], "vs"), B["stage"].v(np.s_[s * 64:(s + 1) * 64, :]))
            for half in range(2):
                b = ps()
                for c in range(4):
                    tr(bv(b, np.s_[:, c * 128:(c + 1) * 128]), B["kn"].v(np.s_[:, half * 4 + c, :]), ident)
                cp(B["stage"].v(np.s_[:, half * 512:(half + 1) * 512]), bv(b), eng="act")
            if not sample:
                dma(dv(kp[o, out_rows:out_rows + 128, :], "kp"), B["stage"].v())
            else:
                for s in range(2):
                    dma(dv(ks[o, s, 448:512, :], "ks"), B["stage"].v(np.s_[s * 64:(s + 1) * 64, :]))
        if not sample:
            def kblk(kb, h):
                s_ = (t - 4 + kb) % 8
                return B["kring"].w(B["kring"].t[:, h // 2, s_ * 128:(s_ + 1) * 128], ("s", s_))
            def vblk(kb, h):
                s_ = (t - 4 + kb) % 8
                return B["vring"].w(B["vring"].t[:, s_, h * 65:(h + 1) * 65], ("s", s_))
            attend(o, 0, 128, kblk, vblk)
        else:
            for s in range(2):
                dma(dv(ks[o, s, 0:448, :], "ks"), dv(ck[o, s, 64:512, :], "ck"))
                dma(dv(vs[o, s, 0:448, :], "vs"), dv(cv[o, s, 64:512, :], "cv"))
                shift = 64 * s
                for m in range(5):
                    r_lo = max(0, 128 * m - shift); r_hi = min(512, 128 * m + 128 - shift)
                    kslot = B["kring"].w(B["kring"].t[:, :, m * 128:(m + 1) * 128], ("s", m))
                    vslot = B["vring"].w(B["vring"].t[:, m, :], ("s", m))
                    ms(vslot, 0.0)
                    if r_hi > r_lo:
                        p_lo = r_lo + shift - 128 * m
                        n = r_hi - r_lo
                        ms(B["ckst"].v(), 0.0)
                        dma(B["ckst"].w(B["ckst"].t[p_lo:p_lo + n, :]), dv(ck[o, s, r_lo:r_hi, :], "ck"))
                        for half in range(2):
                            b = ps()
                            for c in range(4):
                                tr(bv(b, np.s_[:, c * 128:(c + 1) * 128]), B["ckst"].v(np.s_[:, (half * 4 + c) * 128:(half * 4 + c + 1) * 128]), ident)
                            cp(B["kring"].w(B["kring"].t[:, half * 4:half * 4 + 4, m * 128:(m + 1) * 128], ("s", m)),
                               V(b.t[:, :].rearrange("p (c n) -> p c n", c=4), bv(b).units), eng="act")
                        dma(B["vring"].w(B["vring"].t[p_lo:p_lo + n, m, :].rearrange("p (h n) -> p h n", h=16)[:, :, 0:64], ("s", m)),
                            dv(cv[o, s, r_lo:r_hi, :].rearrange("r (h n) -> r h n", h=16), "cv"), q="pool")
                        ms(B["vring"].w(B["vring"].t[:, m, :].rearrange("p (h n) -> p h n", h=16)[:, :, 64:65], ("s", m)), 1.0)
                    else:
                        ms(kslot, 0.0)
                cp(B["kring"].w(B["kring"].t[:, :, 4 * 128 + shift:4 * 128 + shift + 64], ("s", 4)), B["kown"].v(np.s_[:, :, shift:shift + 64]), eng="act")
                cp(B["vring"].w(B["vring"].t[shift:shift + 64, 4, :], ("s", 4)), B["vown"].v(np.s_[shift:shift + 64, :]), eng="dve")
                def kblk(kb, h):
                    return B["kring"].w(B["kring"].t[:, h // 2, kb * 128:(kb + 1) * 128], ("s", kb))
                def vblk(kb, h):
                    return B["vring"].w(B["vring"].t[:, kb, h * 65:(h + 1) * 65], ("s", kb))
                attend(o, shift, 64, kblk, vblk)
        for half in range(2):
            b = ps()
            for c in range(4):
                dc = half * 4 + c
                for mc in range(8):
                    mm(bv(b, np.s_[:, c * 128:(c + 1) * 128]), wB.v(np.s_[:, mc, dc * 128:(dc + 1) * 128]), B["attT"].v(np.s_[:, mc, :]),
                       start=(mc == 0), stop=(mc == 7))
            add_to_x(tcol, half * 4, 4, b)

    def ffn_phase(l, ntiles, extra=()):
        B = FB
        ntok = ntiles * 128

        def load_slab(s):
            i = s % 2
            dma(B[f"win{i}"].v(np.s_[:, :, 0:256], "g"), dv(w_ffn_in[l][:, s * 256:(s + 1) * 256].rearrange("(kc p) n -> p kc n", p=128), "wfi"), q="pool")
            dma(B[f"win{i}"].v(np.s_[:, :, 256:512], "u"), dv(w_ffn_in[l][:, DFF + s * 256:DFF + (s + 1) * 256].rearrange("(kc p) n -> p kc n", p=128), "wfi"), q="pool")
            dma(B[f"wout{i}"].v(), dv(w_ffn_out[l][s * 256:(s + 1) * 256, :].rearrange("(fc p) n -> p fc n", p=128), "wfo"), q="pool")

        extra = list(extra)
        load_slab(0)
        load_slab(1)

        def issue_extra(k):
            for _ in range(k):
                if extra:
                    extra.pop(0)()

        def norm_a(t):
            sqt = B[f"sqt{t % 3}"]
            act(sqt.v(), x.v(np.s_[:, :, t * 128:(t + 1) * 128], ("t", t * 128)), AF.Square)
            b = ps()
            for c in range(8):
                mm(bv(b, np.s_[:, 0:128]), onesD, sqt.v(np.s_[:, c, :]), start=(c == 0), stop=(c == 7))
            return b

        def norm_b(t, b):
            rst = B[f"rst{t % 3}"]
            act(rst.v(), bv(b, np.s_[:, 0:128]), AF.Ln, bias=EPS)
            act(rst.v(), rst.v(), AF.Exp, scale=-0.5)
            for c in range(8):
                stt(B["hTall"].v(np.s_[:, c, t * 128:(t + 1) * 128], ("t", t)), x.v(np.s_[:, c, t * 128:(t + 1) * 128], ("t", t * 128)),
                    pv.w(pv.t[:, 32 + l * 8 + c:32 + l * 8 + c + 1]), rst.v(), ALU.mult, ALU.mult)

        nb_ = norm_a(0)
        for t in range(ntiles):
            nxt_ = norm_a(t + 1) if t + 1 < ntiles else None
            norm_b(t, nb_)
            nb_ = nxt_
        chunks = []
        c0 = 0
        while c0 < ntok:
            n = min(512, ntok - c0)
            chunks.append((c0, n))
            c0 += n
        it = 0
        for s in range(NSLAB):
            if 1 <= s and s + 1 < NSLAB:
                load_slab(s + 1)
            issue_extra(2)
            i = s % 2
            win = B[f"win{i}"]; wout = B[f"wout{i}"]
            for (c0, n) in chunks:
                tkeys = [("t", tt_) for tt_ in range(c0 // 128, (c0 + n) // 128)]
                ab = B[f"a{it % 2}"]; sgf = B[f"sgf{it % 2}"]
                it += 1
                for fc in range(2):
                    bg_ = ps(); bu_ = ps()
                    for kc in range(8):
                        mm(bv(bg_, np.s_[:, 0:n]), win.v(np.s_[:, kc, fc * 128:(fc + 1) * 128], "g"), B["hTall"].v(np.s_[:, kc, c0:c0 + n], tkeys), start=(kc == 0), stop=(kc == 7))
                    for kc in range(8):
                        mm(bv(bu_, np.s_[:, 0:n]), win.v(np.s_[:, kc, 256 + fc * 128:256 + (fc + 1) * 128], "u"), B["hTall"].v(np.s_[:, kc, c0:c0 + n], tkeys), start=(kc == 0), stop=(kc == 7))
                    act(sgf.v(np.s_[:, 0:n]), bv(bg_, np.s_[:, 0:n]), AF.Silu)
                    tt(ab.v(np.s_[:, fc, 0:n]), sgf.v(np.s_[:, 0:n]), bv(bu_, np.s_[:, 0:n]), ALU.mult)
                for dc in range(8):
                    by = ps()
                    for fc in range(2):
                        mm(bv(by, np.s_[:, 0:n]), wout.v(np.s_[:, fc, dc * 128:(dc + 1) * 128]), ab.v(np.s_[:, fc, 0:n]), start=(fc == 0), stop=(fc == 1))
                    xv = x.v(np.s_[:, dc, c0:c0 + n], [("t", tt_ * 128) for tt_ in range(c0 // 128, (c0 + n) // 128)])
                    tt(xv, xv, bv(by, np.s_[:, 0:n]), ALU.add)
        issue_extra(len(extra))

    def mixer_weight_thunks(l):
        if l % 2 == 0:
            a = load_w_thunks(wA.v(), w_in_ab[l // 2].rearrange("(kc p) n -> p kc n", p=128), "w_in")
            b = load_w_thunks(wB.v(), w_out_ab[l // 2].rearrange("(kc p) n -> p kc n", p=128), "w_out")
        else:
            a = load_w_thunks(wA.v(np.s_[:, :, 0:3 * D]), w_qkv[l // 2].rearrange("(kc p) n -> p kc n", p=128), "w_qkv")
            b = load_w_thunks(wB.v(), w_o_att[l // 2].rearrange("(kc p) n -> p kc n", p=128), "w_o")
        out = []
        for kc in range(8):
            out.append(a[kc]); out.append(b[kc])
        return out

    def issue_mixer_weights(l):
        for th in mixer_weight_thunks(l):
            th()

    issue_mixer_weights(0)
    phase_barrier()
    for o_ in range(NO):
        build_hscr(o_)
    phase_barrier()
    for t in range(JT):
        load_x_tile(t * 128, xp[t * 128:(t + 1) * 128, :])
    for j in range(NJ):
        last = (j == NJ - 1)
        ntiles = JT + (1 if last else 0)
        if last:
            load_x_tile(JT * 128, xs[:, :])
        for l in range(DEPTH):
            if _CFG.get("STAGE") == "io":
                break
            phase_barrier()
            if l % 2 == 0:
                e = l // 2
                specs = [dict(e=e, l=l, tcol=t * 128, sample=False, want_state=(last and t == JT - 1)) for t in range(JT)]
                if last:
                    specs.append(dict(e=e, l=l, tcol=JT * 128, sample=True, want_state=True))
                run_pipelined(specs, even_gen)
            else:
                o = l // 2
                odd_phase_begin(o, j)
                for t in range(JT):
                    g = j * JT + t
                    orow = (g - (NTSEQ - 4)) * 128 if g >= NTSEQ - 4 else None
                    odd_tile(o, l, t * 128, t, False, orow)
                odd_phase_end(o, j, last)
                if last:
                    odd_tile(o, l, JT * 128, None, True, None)
            if _CFG.get("STAGE") == "mix" or (_CFG.get("STAGE") == "mix0" and l == 0):
                continue
            phase_barrier()
            if l + 1 < DEPTH:
                ths = mixer_weight_thunks(l + 1)
            elif not last:
                ths = mixer_weight_thunks(0)
            else:
                ths = []
            ffn_phase(l, ntiles, ths)
        phase_barrier()
        nxt_rows = lambda t: xp[((j + 1) * JT + t) * 128:((j + 1) * JT + t + 1) * 128, :]
        if not last:
            for t in range(3):
                load_x_issue(t, nxt_rows(t))
        for t in range(JT):
            store_y_tile(t * 128, yp[(j * JT + t) * 128:(j * JT + t + 1) * 128, :], own_stage=t)
            if not last:
                load_x_finish(t, t * 128)
                if t + 3 < JT:
                    load_x_issue(t + 3, nxt_rows(t + 3))
        if last:
            store_y_tile(JT * 128, ys[:, :])

    with nc.allow_non_contiguous_dma(reason="small strided parameter/state transfers"):
        stats = P.emit()
    return nc, stats


_BUILD_CACHE = {}


def kernel(x_prompt, x_sample, state_conv, state_gla, cache_k, cache_v,
           norm_mix, norm_ffn, w_in_ab, conv_w, gla_gk_w2, gla_gk_b, gla_onorm, w_out_ab,
           w_qkv, q_norm, k_norm, rel_bias, w_o_att, w_ffn_in, w_ffn_out):
    f = lambda a: np.ascontiguousarray(np.asarray(a, dtype=np.float32))
    x_prompt = f(x_prompt); x_sample = f(x_sample)
    BATCH, SEQ, _ = x_prompt.shape
    DEPTH = _CFG["DEPTH"]
    NE = (DEPTH + 1) // 2; NO = DEPTH // 2
    key = (SEQ, DEPTH)
    if key not in _BUILD_CACHE:
        _BUILD_CACHE[key] = build(SEQ, DEPTH)
    nc, stats = _BUILD_CACHE[key]
    state_conv = f(state_conv); state_gla = f(state_gla); cache_k = f(cache_k); cache_v = f(cache_v)
    shared = {"norm_mix": f(norm_mix), "norm_ffn": f(norm_ffn), "w_in_ab": f(w_in_ab), "conv_w": f(conv_w),
              "gk_w2": f(gla_gk_w2), "gk_b": f(gla_gk_b), "onorm": f(gla_onorm), "w_out_ab": f(w_out_ab),
              "w_qkv": f(w_qkv), "q_norm": f(q_norm), "k_norm": f(k_norm), "rel_bias": f(rel_bias),
              "w_o_att": f(w_o_att), "w_ffn_in": f(w_ffn_in), "w_ffn_out": f(w_ffn_out), "consts": _consts()}
    in_maps = []
    for c in range(8):
        b = c % 4
        m = dict(shared)
        m["xp"] = x_prompt[b]
        m["xs"] = x_sample[2 * b:2 * b + 2].reshape(128, D)
        m["sconv"] = np.ascontiguousarray(state_conv[:NE, 2 * b:2 * b + 2])
        m["sgla"] = np.ascontiguousarray(state_gla[:NE, 2 * b:2 * b + 2])
        if NO > 0:
            m["ck"] = np.ascontiguousarray(cache_k[:NO, 2 * b:2 * b + 2].reshape(NO, 2, 512, D))
            m["cv"] = np.ascontiguousarray(cache_v[:NO, 2 * b:2 * b + 2].reshape(NO, 2, 512, D))
        else:
            m["ck"] = np.zeros((1, 2, 512, D), np.float32); m["cv"] = np.zeros((1, 2, 512, D), np.float32)
        in_maps.append(m)
    ncores = _CFG.get("NCORES", 4)
    res = run_bass_kernel_spmd(nc, in_maps[:ncores], core_ids=list(range(ncores)))
    R = list(res.results)
    while len(R) < 4:
        R.append(R[0])
    y_prompt = np.stack([R[b]["yp"] for b in range(4)])
    y_sample = np.concatenate([R[b]["ys"].reshape(2, 64, D) for b in range(4)])
    conv_p = np.stack([R[b]["convp"] for b in range(4)], axis=1)
    gla_p = np.stack([R[b]["glap"] for b in range(4)], axis=1)
    k_p = np.stack([R[b]["kp"][:NO].reshape(NO, 512, 16, 64) for b in range(4)], axis=1)
    v_p = np.stack([R[b]["vp"][:NO].reshape(NO, 512, 16, 64) for b in range(4)], axis=1)
    conv_s = np.concatenate([R[b]["convs"] for b in range(4)], axis=1)
    gla_s = np.concatenate([R[b]["glas"] for b in range(4)], axis=1)
    k_s = np.concatenate([R[b]["ks"][:NO].reshape(NO, 2, 512, 16, 64) for b in range(4)], axis=1)
    v_s = np.concatenate([R[b]["vs"][:NO].reshape(NO, 2, 512, 16, 64) for b in range(4)], axis=1)
    return (y_prompt, y_sample, conv_p, gla_p, k_p, v_p, conv_s, gla_s, k_s, v_s)
```
